# Optimizing a Trainium2 kernel written in Bass

```python
import jax, jax.numpy as jnp
from jax import lax
import numpy as np

D_MODEL = 2048
BATCH = 8
SEQ = 4096
DEPTH = 1

HEAD_DIM = 128
MIX_WIDTH = D_MODEL
N_HEADS_A = 8
N_KV_A = 2
N_HEADS_B = 8
WIDTH_A = N_HEADS_A * HEAD_DIM
WIDTH_B = N_HEADS_B * HEAD_DIM
GRID_W = 64
AXIAL_THETA = 10000.0
ROPE_THETA = 500000.0
PARTIAL_ROT = HEAD_DIM // 4
DILATED_PAIRS = ((128, 1), (512, 4), (2048, 16))
Q_BLOCK = 128
D_FF = -(-8 * D_MODEL // (3 * 256)) * 256
EPS = 1e-6
PROJ_SIZES = (WIDTH_A, N_KV_A * HEAD_DIM, N_KV_A * HEAD_DIM, WIDTH_B, WIDTH_B, WIDTH_B)
PROJ_OUT = sum(PROJ_SIZES)

kernel_name = "hymba_axial_gqa_dilated_swiglu_encoder"


def rmsnorm(x, g):
    xf = x.astype(jnp.float32)
    y = xf * lax.rsqrt(jnp.mean(xf * xf, axis=-1, keepdims=True) + EPS)
    return (y * g.astype(jnp.float32)).astype(x.dtype)


def rope_angles(pos, dim, theta):
    inv = theta ** (-(jnp.arange(0, dim, 2, dtype=jnp.float32) / dim))
    return pos[:, None] * inv[None, :]


def apply_rope(x, ang):
    xf = x.astype(jnp.float32)
    cos = jnp.cos(ang)[None, :, None, :]
    sin = jnp.sin(ang)[None, :, None, :]
    x1, x2 = jnp.split(xf, 2, axis=-1)
    out = jnp.concatenate([x1 * cos - x2 * sin, x2 * cos + x1 * sin], axis=-1)
    return out.astype(x.dtype)


def axial_rope(x, seq_len):
    rows = seq_len // GRID_W
    row_pos = jnp.repeat(jnp.arange(rows, dtype=jnp.float32), GRID_W)
    col_pos = jnp.tile(jnp.arange(GRID_W, dtype=jnp.float32), rows)
    half = HEAD_DIM // 2
    xr = apply_rope(x[..., :half], rope_angles(row_pos, half, AXIAL_THETA))
    xc = apply_rope(x[..., half:], rope_angles(col_pos, half, AXIAL_THETA))
    return jnp.concatenate([xr, xc], axis=-1)


def partial_rope(x, seq_len):
    pos = jnp.arange(seq_len, dtype=jnp.float32)
    xr = apply_rope(x[..., :PARTIAL_ROT], rope_angles(pos, PARTIAL_ROT, ROPE_THETA))
    return jnp.concatenate([xr, x[..., PARTIAL_ROT:]], axis=-1)


def mixer_a(q, k, v, g_q, g_k):
    b, s, _, hd = q.shape
    grp = N_HEADS_A // N_KV_A
    q = axial_rope(rmsnorm(q, g_q), s) * (hd ** -0.5)
    k = axial_rope(rmsnorm(k, g_k), s)
    nb = s // Q_BLOCK
    qb = jnp.moveaxis(q.reshape(b, nb, Q_BLOCK, N_KV_A, grp, hd), 1, 0)

    def block(qblk):
        sc = jnp.einsum('bqkgd,bskd->bkgqs', qblk, k, preferred_element_type=jnp.float32)
        p = jax.nn.softmax(sc, axis=-1).astype(v.dtype)
        return jnp.einsum('bkgqs,bskd->bqkgd', p, v)

    o = lax.map(block, qb)
    return jnp.moveaxis(o, 0, 1).reshape(b, s, N_HEADS_A * hd)


def banded_attention(q, k, v, w):
    n, l, h, hd = q.shape
    nb = -(-l // w)
    lp = nb * w
    qp = jnp.pad(q, ((0, 0), (0, lp - l), (0, 0), (0, 0)))
    pad_k = ((0, 0), (w, lp - l + w), (0, 0), (0, 0))
    kb = jnp.pad(k, pad_k).reshape(n, nb + 2, w, h, hd)
    vb = jnp.pad(v, pad_k).reshape(n, nb + 2, w, h, hd)
    kwin = jnp.concatenate([kb[:, :-2], kb[:, 1:-1], kb[:, 2:]], axis=2)
    vwin = jnp.concatenate([vb[:, :-2], vb[:, 1:-1], vb[:, 2:]], axis=2)
    qb = qp.reshape(n, nb, w, h, hd)
    sc = jnp.einsum('nbqhd,nbkhd->nbhqk', qb, kwin, preferred_element_type=jnp.float32)
    start = jnp.arange(nb)[:, None, None] * w
    qpos = start + jnp.arange(w)[None, :, None]
    kpos = start - w + jnp.arange(3 * w)[None, None, :]
    valid = (jnp.abs(qpos - kpos) <= w) & (kpos >= 0) & (kpos < l)
    sc = jnp.where(valid[None, :, None], sc, -jnp.inf)
    lse = jax.nn.logsumexp(sc, axis=-1)
    p = jnp.exp(sc - lse[..., None]).astype(v.dtype)
    o = jnp.einsum('nbhqk,nbkhd->nbqhd', p, vwin).reshape(n, lp, h, hd)[:, :l]
    lse = jnp.transpose(lse, (0, 1, 3, 2)).reshape(n, lp, h)[:, :l]
    return o, lse


def dilated_branch(q, k, v, window, dilation):
    b, s, h, hd = q.shape
    l = s // dilation
    n_side = (window // 2) // dilation

    def to_res(t):
        return jnp.transpose(t.reshape(b, l, dilation, h, hd), (0, 2, 1, 3, 4)).reshape(b * dilation, l, h, hd)

    o, lse = banded_attention(to_res(q), to_res(k), to_res(v), n_side)
    o = jnp.transpose(o.reshape(b, dilation, l, h, hd), (0, 2, 1, 3, 4)).reshape(b, s, h, hd)
    lse = jnp.transpose(lse.reshape(b, dilation, l, h), (0, 2, 1, 3)).reshape(b, s, h)
    return o, lse


def mixer_b(q, k, v):
    b, s, h, hd = q.shape
    q = partial_rope(q, s) * (hd ** -0.5)
    k = partial_rope(k, s)
    outs, lses = [], []
    for window, dilation in DILATED_PAIRS:
        o, lse = dilated_branch(q, k, v, window, dilation)
        outs.append(o)
        lses.append(lse)
    wts = jax.nn.softmax(jnp.stack(lses, axis=0), axis=0)
    o = jnp.sum(wts[..., None].astype(v.dtype) * jnp.stack(outs, axis=0), axis=0)
    return o.reshape(b, s, h * hd)


def setup_inputs(seed: int = 0) -> dict:
    key = jax.random.key(seed)
    ks = jax.random.split(key, 12)
    f32 = jnp.float32

    def gain(k, n):
        return 1.0 + 0.02 * jax.random.normal(k, (DEPTH, n), f32)

    x = jax.random.normal(ks[0], (BATCH, SEQ, D_MODEL), f32)
    g_mix = gain(ks[1], D_MODEL)
    w_in = jax.random.normal(ks[2], (DEPTH, D_MODEL, PROJ_OUT), f32) * D_MODEL ** -0.5
    g_q_a = gain(ks[3], HEAD_DIM)
    g_k_a = gain(ks[4], HEAD_DIM)
    g_out_a = gain(ks[5], WIDTH_A)
    g_out_b = gain(ks[6], WIDTH_B)
    w_out = jax.random.normal(ks[7], (DEPTH, MIX_WIDTH, D_MODEL), f32) * (2.0 * MIX_WIDTH) ** -0.5
    g_ffn = gain(ks[8], D_MODEL)
    w_gate_up = jax.random.normal(ks[9], (DEPTH, D_MODEL, 2 * D_FF), f32) * D_MODEL ** -0.5
    w_down = jax.random.normal(ks[10], (DEPTH, D_FF, D_MODEL), f32) * (2.0 * D_FF) ** -0.5
    g_final = 1.0 + 0.02 * jax.random.normal(ks[11], (D_MODEL,), f32)
    return {"x": x, "g_mix": g_mix, "w_in": w_in, "g_q_a": g_q_a, "g_k_a": g_k_a,
            "g_out_a": g_out_a, "g_out_b": g_out_b, "w_out": w_out, "g_ffn": g_ffn,
            "w_gate_up": w_gate_up, "w_down": w_down, "g_final": g_final}


def reference(x, g_mix, w_in, g_q_a, g_k_a, g_out_a, g_out_b, w_out, g_ffn, w_gate_up, w_down, g_final):
    b, s, _ = x.shape
    offs = np.cumsum(PROJ_SIZES)[:-1].tolist()
    for layer in range(DEPTH):
        h = rmsnorm(x, g_mix[layer])
        proj = jnp.einsum('bsd,de->bse', h, w_in[layer])
        qa, ka, va, qb, kb, vb = jnp.split(proj, offs, axis=-1)
        qa = qa.reshape(b, s, N_HEADS_A, HEAD_DIM)
        ka = ka.reshape(b, s, N_KV_A, HEAD_DIM)
        va = va.reshape(b, s, N_KV_A, HEAD_DIM)
        qb = qb.reshape(b, s, N_HEADS_B, HEAD_DIM)
        kb = kb.reshape(b, s, N_HEADS_B, HEAD_DIM)
        vb = vb.reshape(b, s, N_HEADS_B, HEAD_DIM)
        out_a = rmsnorm(mixer_a(qa, ka, va, g_q_a[layer], g_k_a[layer]), g_out_a[layer])
        out_b = rmsnorm(mixer_b(qb, kb, vb), g_out_b[layer])
        mixed = jnp.concatenate([out_a, out_b], axis=-1)
        x = x + jnp.einsum('bse,ed->bsd', mixed, w_out[layer])
        h2 = rmsnorm(x, g_ffn[layer])
        gate, up = jnp.split(jnp.einsum('bsd,df->bsf', h2, w_gate_up[layer]), 2, axis=-1)
        x = x + jnp.einsum('bsf,fd->bsd', jax.nn.silu(gate) * up, w_down[layer])
    return rmsnorm(x, g_final)
```

```python
import numpy as np
from contextlib import ExitStack
import ml_dtypes
import concourse.bass as bass
import concourse.mybir as mybir
from concourse.bass_utils import run_bass_kernel_spmd

F32 = mybir.dt.float32
BF16 = mybir.dt.bfloat16
ALU = mybir.AluOpType
AF = mybir.ActivationFunctionType
AX = mybir.AxisListType

D = 2048
HD = 128
NHA = 8
NKV = 2
NHB = 8
DFF = 5632
PROJ = 4608
EPS = 1e-6
GRID_W = 64
NFC = DFF // 128
SCALE = HD ** -0.5
NREL = 20


def I(name, *a, **k):
    return lambda e: getattr(e, name)(*a, **k)


class Buf:
    __slots__ = ("name", "w", "r", "ld", "st", "excl")

    def __init__(self, name, excl=False):
        self.name = name
        self.excl = excl
        self.w = None
        self.r = {}
        self.ld = None
        self.st = None


class Prog:
    ENG = ("sp", "act", "dve", "pool", "pe")

    def __init__(self, nc, es):
        self.nc = nc
        self.es = es
        self.q = {e: [] for e in self.ENG}
        self.sem = {e: es.enter_context(nc.semaphore("prog_" + e)) for e in self.ENG}
        self.cnt = {e: 0 for e in self.ENG}
        self.waited = {e: {} for e in self.ENG}
        self.dcnt = {}
        self.nsem = 0
        self.stores = {}
        self.alldma = {}

    def new_sem(self, name):
        s = self.es.enter_context(self.nc.semaphore(name + "_%d" % self.nsem))
        self.nsem += 1
        self.dcnt[id(s)] = 0
        return s

    def wait(self, eng, toks):
        w = self.waited[eng]
        for t in toks:
            if t is None:
                continue
            sem, val = t
            if w.get(id(sem), 0) >= val:
                continue
            w[id(sem)] = val
            self.q[eng].append(lambda e, sem=sem, val=val: e.wait_ge(sem, val))

    def _deps(self, reads, writes, extra):
        toks = list(extra)
        for b in reads:
            toks.append(b.w)
            if b.excl:
                toks.extend(b.r.values())
        for b in writes:
            toks.append(b.w)
            toks.extend(b.r.values())
        return toks

    def _commit(self, tok, reads, writes):
        for b in reads:
            b.r[id(tok[0])] = tok
        for b in writes:
            b.w = tok
            b.r = {}

    def op(self, eng, fn, reads=(), writes=(), extra=()):
        return self.group(eng, [fn], reads, writes, extra)

    def group(self, eng, fns, reads=(), writes=(), extra=()):
        self.wait(eng, self._deps(reads, writes, extra))
        self.cnt[eng] += 1
        sem = self.sem[eng]
        tok = (sem, self.cnt[eng])
        for fn in fns[:-1]:
            self.q[eng].append(lambda e, fn=fn: fn(e))
        self.q[eng].append(lambda e, fn=fns[-1], sem=sem: fn(e).then_inc(sem, 1))
        self._commit(tok, reads, writes)
        return tok

    def dma(self, eng, out, in_, reads=(), writes=(), extra=(), sem=None):
        deps = self._deps(reads, writes, extra)
        if writes and writes[0].ld is not None:
            deps = [t for t in deps if t is None or t[0] is not writes[0].ld]
        self.wait(eng, deps)
        if sem is None:
            if writes:
                b = writes[0]
                if b.ld is None:
                    b.ld = self.new_sem("ld_" + b.name)
                sem = b.ld
            else:
                b = reads[0]
                if b.st is None:
                    b.st = self.new_sem("st_" + b.name)
                sem = b.st
        self.dcnt[id(sem)] += 16
        tok = (sem, self.dcnt[id(sem)])
        self.q[eng].append(lambda e, out=out, in_=in_, sem=sem: e.dma_start(out=out, in_=in_).then_inc(sem, 16))
        self._commit(tok, reads, writes)
        if reads and not writes:
            self.stores[id(sem)] = tok
        self.alldma[id(sem)] = tok
        return tok

    def barrier(self):
        toks = [(self.sem[e], self.cnt[e]) for e in self.ENG if self.cnt[e] > 0] + list(self.alldma.values())
        for e in self.ENG:
            self.wait(e, toks)

    def fence_stores(self, engs=("sp", "pool", "act")):
        toks = list(self.stores.values())
        for e in engs:
            self.wait(e, toks)

    def run(self):
        nc = self.nc
        with nc.Block() as block:
            @block.sync
            def _(e):
                for f in self.q["sp"]:
                    f(e)

            @block.scalar
            def _(e):
                for f in self.q["act"]:
                    f(e)

            @block.vector
            def _(e):
                for f in self.q["dve"]:
                    f(e)

            @block.gpsimd
            def _(e):
                for f in self.q["pool"]:
                    f(e)

            @block.tensor
            def _(e):
                for f in self.q["pe"]:
                    f(e)


def _const_tables(S):
    t = np.arange(S)
    half = HD // 2
    inv_a = (10000.0 ** (-(np.arange(0, half, 2, dtype=np.float32) / half))).astype(np.float32)
    row = (t // GRID_W).astype(np.float32)
    col = (t % GRID_W).astype(np.float32)
    ang_r = row[:, None] * inv_a[None, :]
    ang_c = col[:, None] * inv_a[None, :]
    ca = np.zeros((S, 2, 2, 32), np.float32)
    sa = np.zeros((S, 2, 2, 32), np.float32)
    for a, ang in enumerate((ang_r, ang_c)):
        c = np.cos(ang.astype(np.float32)).astype(np.float32)
        s = np.sin(ang.astype(np.float32)).astype(np.float32)
        ca[:, a, 0] = c
        ca[:, a, 1] = c
        sa[:, a, 0] = -s
        sa[:, a, 1] = s
    pr = HD // 4
    inv_b = (500000.0 ** (-(np.arange(0, pr, 2, dtype=np.float32) / pr))).astype(np.float32)
    ang_b = t.astype(np.float32)[:, None] * inv_b[None, :]
    cb = np.cos(ang_b).astype(np.float32)
    sb_ = np.sin(ang_b).astype(np.float32)
    cbt = np.concatenate([cb, cb], axis=1)
    sbt = np.concatenate([-sb_, sb_], axis=1)
    k = np.arange(128)[:, None]
    q = np.arange(512)[None, :]
    mask = np.zeros((NREL, 128, 512), np.float32)
    for r in range(NREL):
        d = 128 * (r - 8) + k - q
        ad = np.abs(d)
        mask[r] = (ad <= 64).astype(np.float32) + ((d % 4 == 0) & (ad <= 256)) + ((d % 16 == 0) & (ad <= 1024))
    return (ca.reshape(S, 128), sa.reshape(S, 128), cbt.astype(np.float32), sbt.astype(np.float32),
            mask.astype(ml_dtypes.bfloat16))


def build(S=4096, debug=False, phases=(1, 2, 3, 4)):
    NT = S // 128
    NQB = S // 512
    nc = bass.Bass("TRN2", target_bir_lowering=False)

    def din(name, shape, dt=F32):
        return nc.dram_tensor(name, list(shape), dt, kind="ExternalInput").ap()

    def dscr(name, shape, dt):
        if debug:
            return nc.dram_tensor(name, list(shape), dt, kind="ExternalOutput").ap()
        return nc.dram_tensor(name, list(shape), dt).ap()

    x = din("x", [S, D])
    w_in = din("w_in", [D, PROJ])
    w_out = din("w_out", [D, D])
    w_gu = din("w_gate_up", [D, 2 * DFF])
    w_dn = din("w_down", [DFF, D])
    g_mix = din("g_mix", [128, 16])
    g_ffn = din("g_ffn", [128, 16])
    g_fin = din("g_final", [128, D])
    g_qk = din("g_qk", [128, 4, 128])
    g_out = din("g_out", [128, 16])
    ropa_c = din("ropa_c", [S, 128])
    ropa_s = din("ropa_s", [S, 128])
    ropb_c = din("ropb_c", [S, 32])
    ropb_s = din("ropb_s", [S, 32])
    maskb = din("maskb", [NREL, 128, 512], BF16)
    out = nc.dram_tensor("out", [S, D], F32, kind="ExternalOutput").ap()

    qaT = dscr("qaT", [NHA, 128, S], BF16)
    kaT = dscr("kaT", [NKV, 128, S], BF16)
    va = dscr("va", [S, NKV * 128], BF16)
    qbT = dscr("qbT", [NHB, 128, S], BF16)
    kbT = dscr("kbT", [NHB, 128, S], BF16)
    vb = dscr("vb", [S, NHB * 128], BF16)
    mixT = dscr("mixT", [16, 128, S], BF16)
    x1d = dscr("x1d", [S, D], F32)
    wout_s = dscr("wout_s", [4, 128, 16, 512], BF16)
    wgu_s = dscr("wgu_s", [22, 128, 2, 16, 256], BF16)
    wd_s = dscr("wd_s", [4, 3, 128, 16, 512], BF16)

    with ExitStack() as es:
        P = Prog(nc, es)

        def sbt(stack, name, shape, dt):
            return stack.enter_context(nc.sbuf_tensor(name, list(shape), dt))

        def pst(stack, name, shape, dt):
            return stack.enter_context(nc.psum_tensor(name, list(shape), dt))

        ident = sbt(es, "ident", [128, 128], BF16)
        ones_bf = sbt(es, "ones_bf", [128, 128], BF16)
        ones_f = sbt(es, "ones_f", [128, 1], F32)
        gout_sb = sbt(es, "gout_sb", [128, 16], F32)
        ssq = sbt(es, "ssq", [128, NT, 16], F32)
        B_ident, B_ones, B_onesf, B_gout, B_ssq = Buf("ident"), Buf("ones"), Buf("onesf"), Buf("gout"), Buf("ssq")
        P.op("pool", I("memset", ident[:], 1.0), writes=[B_ident])
        P.op("pool", I("affine_select", out=ident[:], in_=ident[:], pattern=[[-1, 128]], compare_op=ALU.is_equal,
                                                fill=0.0, base=0, channel_multiplier=1), writes=[B_ident])
        P.op("pool", I("memset", ones_bf[:], 1.0), writes=[B_ones])
        P.op("pool", I("memset", ones_f[:], 1.0), writes=[B_onesf])

        P.dma("sp", gout_sb[:], g_out, writes=[B_gout])

        pbank = [pst(es, "pbank%d" % i, [128, 512], F32) for i in range(8)]
        B_bank = [Buf("bank%d" % i, excl=True) for i in range(8)]

        cast_sem = P.new_sem("cast")
        cast_state = {"tok": None, "done": False}

        cast_list = []
        if 4 in phases:
            for cb in range(4):
                cast_list.append((wout_s[cb], w_out[:, cb * 512:(cb + 1) * 512].rearrange("(h p) c -> p h c", p=128)))
            for f2 in range(22):
                for gu in range(2):
                    c0 = gu * DFF + f2 * 256
                    cast_list.append((wgu_s[f2, :, gu], w_gu[:, c0:c0 + 256].rearrange("(kc p) c -> p kc c", p=128)))
            for cb in range(4):
                for pc in range(3):
                    n = 16 if pc < 2 else NFC - 32
                    r0 = pc * 16 * 128
                    cast_list.append((wd_s[cb, pc, :, 0:n, :],
                                      w_dn[r0:r0 + n * 128, cb * 512:(cb + 1) * 512].rearrange("(fc p) c -> p fc c", p=128)))

        def cast_one():
            if cast_list:
                dst, src = cast_list.pop(0)
                cast_state["tok"] = P.dma("pool", dst, src, sem=cast_sem)

        def emit_casts():
            while cast_list:
                cast_one()

        if 1 in phases:
            with ExitStack() as ps1:
                w_sb = sbt(ps1, "w_sb", [128, 16, PROJ], BF16)
                B_w = Buf("w_sb")
                for kq in range(4):
                    P.dma("pool", w_sb[:, kq * 4:(kq + 1) * 4, :],
                          w_in[kq * 512:(kq + 1) * 512, :].rearrange("(kc p) c -> p kc c", p=128), writes=[B_w])
                gmix_sb = sbt(ps1, "gmix_sb", [128, 16], F32)
                gqk_sb = sbt(ps1, "gqk_sb", [128, 4, 128], F32)
                B_gmix, B_gqk = Buf("gmix"), Buf("gqk")
                P.dma("sp", gmix_sb[:], g_mix, writes=[B_gmix])
                P.dma("sp", gqk_sb[:], g_qk, writes=[B_gqk])

                xb = [sbt(ps1, "xb%d" % i, [128, D], F32) for i in range(2)]
                B_x = [Buf("xb%d" % i) for i in range(2)]
                tabs = [sbt(ps1, "tabs%d" % i, [128, 320], F32) for i in range(2)]
                B_tabs = [Buf("tabs%d" % i) for i in range(2)]
                dtab = [sbt(ps1, "dtab%d" % i, [128, 4 * 128], F32) for i in range(2)]
                B_dtab = [Buf("dtab%d" % i) for i in range(2)]
                stat = [sbt(ps1, "stat%d" % i, [128, 8], F32) for i in range(2)]
                B_stat = [Buf("stat%d" % i) for i in range(2)]
                hb = [sbt(ps1, "hb%d" % i, [128, D], BF16) for i in range(2)]
                B_h = [Buf("hb%d" % i) for i in range(2)]
                hT = [sbt(ps1, "hT%d" % i, [128, 16, 128], BF16) for i in range(2)]
                B_hT = [Buf("hT%d" % i) for i in range(2)]
                NSC = 2
                scr1 = [sbt(ps1, "scr1_%d" % i, [128, 512], F32) for i in range(NSC)]
                scr2 = [sbt(ps1, "scr2_%d" % i, [128, 512], F32) for i in range(NSC)]
                scr3 = [sbt(ps1, "scr3_%d" % i, [128, 512], F32) for i in range(NSC)]
                B_scr1 = [Buf("scr1_%d" % i) for i in range(NSC)]
                B_scr2 = [Buf("scr2_%d" % i) for i in range(NSC)]
                B_scr3 = [Buf("scr3_%d" % i) for i in range(NSC)]
                st4 = [sbt(ps1, "st4_%d" % i, [128, 8], F32) for i in range(NSC)]
                B_st4 = [Buf("st4_%d" % i) for i in range(NSC)]
                NSTG = 3
                stg = [sbt(ps1, "stg%d" % i, [128, 512], BF16) for i in range(NSTG)]
                B_stg = [Buf("stg%d" % i) for i in range(NSTG)]
                NTS = 3
                tst = [sbt(ps1, "tst%d" % i, [128, 4, 128], BF16) for i in range(NTS)]
                B_tst = [Buf("tst%d" % i) for i in range(NTS)]
                NVS = 2
                vst = [sbt(ps1, "vst%d" % i, [128, 512], BF16) for i in range(NVS)]
                B_vst = [Buf("vst%d" % i) for i in range(NVS)]

                tp_bank = [pbank[0][:].bitcast(BF16), pbank[1][:].bitcast(BF16)]

                def load_tile(i):
                    s = i % 2
                    r0 = i * 128
                    P.dma("sp", xb[s][:], x[r0:r0 + 128, :], writes=[B_x[s]])
                    P.dma("sp", tabs[s][:, 0:128], ropa_c[r0:r0 + 128, :], writes=[B_tabs[s]])
                    P.dma("sp", tabs[s][:, 128:256], ropa_s[r0:r0 + 128, :], writes=[B_tabs[s]])
                    P.dma("sp", tabs[s][:, 256:288], ropb_c[r0:r0 + 128, :], writes=[B_tabs[s]])
                    P.dma("sp", tabs[s][:, 288:320], ropb_s[r0:r0 + 128, :], writes=[B_tabs[s]])

                def norm_tile(i):
                    s = i % 2
                    P.op("dve", I("memset", stat[s][:], 0.0), writes=[B_stat[s]])
                    P.op("act", I("activation", out=hb[s][:], in_=xb[s][:], func=AF.Square, accum_out=stat[s][:, 0:1]),
                         reads=[B_x[s]], writes=[B_h[s], B_stat[s]])
                    P.op("dve", I("tensor_scalar", out=stat[s][:, 1:2], in0=stat[s][:, 0:1], scalar1=1.0 / D, scalar2=EPS,
                                                          op0=ALU.mult, op1=ALU.add), writes=[B_stat[s]])
                    P.op("act", I("activation", out=stat[s][:, 2:3], in_=stat[s][:, 1:2], func=AF.Sqrt), writes=[B_stat[s]])
                    P.op("dve", I("reciprocal", out=stat[s][:, 3:4], in_=stat[s][:, 2:3]), writes=[B_stat[s]])
                    P.op("dve", I("tensor_scalar", out=hb[s][:], in0=xb[s][:], scalar1=stat[s][:, 3:4], scalar2=None,
                                                          op0=ALU.mult), reads=[B_x[s], B_stat[s]], writes=[B_h[s]])
                    t_, d_ = tabs[s], dtab[s]
                    P.op("pool", I("tensor_tensor", out=d_[:, 0:128], in0=t_[:, 0:128], in1=gqk_sb[:, 0, :], op=ALU.mult),
                         reads=[B_tabs[s], B_gqk], writes=[B_dtab[s]])
                    P.op("pool", I("tensor_tensor", out=d_[:, 128:256], in0=t_[:, 128:256], in1=gqk_sb[:, 1, :], op=ALU.mult),
                         reads=[B_tabs[s], B_gqk], writes=[B_dtab[s]])
                    P.op("pool", I("tensor_tensor", out=d_[:, 256:384], in0=t_[:, 0:128], in1=gqk_sb[:, 2, :], op=ALU.mult),
                         reads=[B_tabs[s], B_gqk], writes=[B_dtab[s]])
                    P.op("pool", I("tensor_tensor", out=d_[:, 384:512], in0=t_[:, 128:256], in1=gqk_sb[:, 3, :], op=ALU.mult),
                         reads=[B_tabs[s], B_gqk], writes=[B_dtab[s]])

                def transp_h(i):
                    s = i % 2
                    for half in range(2):
                        fns = []
                        for kk in range(8):
                            kc = half * 8 + kk
                            fns.append(I("transpose",
                                tp_bank[half][:, kk * 128:(kk + 1) * 128], hb[s][:, kc * 128:(kc + 1) * 128], ident[:]))
                        P.group("pe", fns, reads=[B_h[s], B_ident], writes=[B_bank[half]])
                        src = tp_bank[half][:, :].rearrange("p (k t) -> p k t", k=8)
                        gm = gmix_sb[:, half * 8:(half + 1) * 8].unsqueeze(2).to_broadcast([128, 8, 128])
                        P.op("dve", I("tensor_tensor",
                            out=hT[s][:, half * 8:(half + 1) * 8, :], in0=src, in1=gm, op=ALU.mult),
                            reads=[B_bank[half], B_gmix], writes=[B_hT[s]])

                cnt = {"scr": 0, "stg": 0, "tst": 0, "vst": 0, "pj": 0, "tq": 0}
                pending = []

                def rope_norm_block(pb_ap, nh, t1off, rstd_mode, dst_slot):
                    k = cnt["scr"] % NSC
                    cnt["scr"] += 1
                    s_ = cur["s"]
                    bank = cur["bank"]
                    d_ = dtab[s_]
                    W = nh * 128
                    P.op("dve", I("memset", st4[k][:], 0.0), writes=[B_st4[k]])
                    for h in range(nh):
                        P.op("act", I("activation", out=scr1[k][:, h * 128:(h + 1) * 128], in_=pb_ap[:, h * 128:(h + 1) * 128], func=AF.Square,
                                      accum_out=st4[k][:, h:h + 1]), reads=[bank], writes=[B_scr1[k], B_st4[k]])
                    if rstd_mode == "q":
                        P.op("dve", I("tensor_scalar", out=st4[k][:, 0:nh], in0=st4[k][:, 0:nh], scalar1=1.0, scalar2=128 * EPS,
                                      op0=ALU.mult, op1=ALU.add), writes=[B_st4[k]])
                    else:
                        P.op("dve", I("tensor_scalar", out=st4[k][:, 0:nh], in0=st4[k][:, 0:nh], scalar1=1.0 / 128, scalar2=EPS,
                                      op0=ALU.mult, op1=ALU.add), writes=[B_st4[k]])
                    P.op("act", I("activation", out=st4[k][:, 0:nh], in_=st4[k][:, 0:nh], func=AF.Sqrt), writes=[B_st4[k]])
                    P.op("dve", I("reciprocal", out=st4[k][:, 4:4 + nh], in_=st4[k][:, 0:nh]), writes=[B_st4[k]])
                    for h in range(nh):
                        P.op("act", I("activation", out=scr2[k][:, h * 128:(h + 1) * 128], in_=pb_ap[:, h * 128:(h + 1) * 128], func=AF.Copy,
                                      scale=st4[k][:, 4 + h:5 + h]), reads=[bank, B_st4[k]], writes=[B_scr2[k]])
                    xs = scr2[k][:, 0:W]
                    T1 = d_[:, t1off:t1off + 128].unsqueeze(1).to_broadcast([128, nh, 128])
                    P.op("dve", I("tensor_tensor", out=scr1[k][:, 0:W].rearrange("p (h d) -> p h d", h=nh),
                                  in0=xs.rearrange("p (h d) -> p h d", h=nh), in1=T1, op=ALU.mult),
                         reads=[B_scr2[k], B_dtab[s_]], writes=[B_scr1[k]])
                    x5 = xs.rearrange("p (h a f j) -> p h a f j", h=nh, a=2, f=2)
                    o5 = scr3[k][:, 0:W].rearrange("p (h a f j) -> p h a f j", h=nh, a=2, f=2)
                    T2 = d_[:, t1off + 128:t1off + 256].rearrange("p (a f j) -> p a f j", a=2, f=2)
                    for f in range(2):
                        tb = T2[:, :, f, :].unsqueeze(1).to_broadcast([128, nh, 2, 32])
                        P.op("dve", I("tensor_tensor", out=o5[:, :, :, f, :], in0=x5[:, :, :, 1 - f, :], in1=tb, op=ALU.mult),
                             reads=[B_scr2[k], B_dtab[s_]], writes=[B_scr3[k]])
                    P.op("dve", I("tensor_tensor", out=stg[dst_slot][:, 0:W], in0=scr1[k][:, 0:W], in1=scr3[k][:, 0:W], op=ALU.add),
                         reads=[B_scr1[k], B_scr3[k]], writes=[B_stg[dst_slot]])

                def rope_part_block(pb_ap, scale, coff, dst_slot):
                    k = cnt["scr"] % NSC
                    cnt["scr"] += 1
                    s_ = cur["s"]
                    bank = cur["bank"]
                    P.op("act", I("activation", out=stg[dst_slot][:], in_=pb_ap, func=AF.Copy, scale=float(scale)),
                         reads=[bank], writes=[B_stg[dst_slot]])
                    pb3 = pb_ap.rearrange("p (h d) -> p h d", h=4)
                    xr = scr3[k][:, 0:128].rearrange("p (h j) -> p h j", h=4)
                    P.op("act", I("activation", out=xr, in_=pb3[:, :, 0:32], func=AF.Copy, scale=float(scale)),
                         reads=[bank], writes=[B_scr3[k]])
                    ctab = tabs[s_][:, 256:288]
                    stab = tabs[s_][:, 288:320]
                    tb_ = B_tabs[s_]
                    r1 = scr1[k][:, 0:128].rearrange("p (h j) -> p h j", h=4)
                    r2 = scr2[k][:, 0:128].rearrange("p (h j) -> p h j", h=4)
                    P.op("dve", I("tensor_tensor", out=r1, in0=xr, in1=ctab.unsqueeze(1).to_broadcast([128, 4, 32]), op=ALU.mult),
                         reads=[B_scr3[k], tb_], writes=[B_scr1[k]])
                    for f in range(2):
                        P.op("dve", I("tensor_tensor", out=r2[:, :, f * 16:(f + 1) * 16], in0=xr[:, :, (1 - f) * 16:(2 - f) * 16],
                                      in1=stab[:, f * 16:(f + 1) * 16].unsqueeze(1).to_broadcast([128, 4, 16]), op=ALU.mult),
                             reads=[B_scr3[k], tb_], writes=[B_scr2[k]])
                    P.op("dve", I("tensor_tensor", out=stg[dst_slot][:].rearrange("p (h d) -> p h d", h=4)[:, :, 0:32], in0=r1, in1=r2, op=ALU.add),
                         reads=[B_scr1[k], B_scr2[k]], writes=[B_stg[dst_slot]])

                def out_transposes(slot, nh, dst_fn):
                    tb = 5 + cnt["tq"] % 2
                    cnt["tq"] += 1
                    tpv = pbank[tb][:].bitcast(BF16)
                    fns = [I("transpose", tpv[:, h * 128:(h + 1) * 128], stg[slot][:, h * 128:(h + 1) * 128], ident[:])
                           for h in range(nh)]
                    P.group("pe", fns, reads=[B_stg[slot], B_ident], writes=[B_bank[tb]])
                    ts = cnt["tst"] % NTS
                    cnt["tst"] += 1
                    P.op("act", I("activation", out=tst[ts][:, 0:nh, :], in_=tpv[:, 0:nh * 128].rearrange("p (h t) -> p h t", h=nh),
                                                       func=AF.Copy), reads=[B_bank[tb]], writes=[B_tst[ts]])
                    P.dma("sp", dst_fn(), tst[ts][:, 0:nh, :], reads=[B_tst[ts]])

                cur = {}

                def proj_tile(i):
                    s = i % 2
                    r0 = i * 128
                    cur["s"] = s
                    for cb in range(9):
                        bk = 2 + cnt["pj"] % 3
                        cnt["pj"] += 1
                        cur["bank"] = B_bank[bk]
                        pb_ap = pbank[bk][:, :]
                        fns = [I("matmul", pbank[bk][:, :], lhsT=hT[s][:, kc, :],
                                                                      rhs=w_sb[:, kc, cb * 512:(cb + 1) * 512],
                                                                      start=(kc == 0), stop=(kc == 15)) for kc in range(16)]
                        P.group("pe", fns, reads=[B_hT[s], B_w], writes=[B_bank[bk]])
                        if cb in (0, 1):
                            sl = cnt["stg"] % NSTG
                            cnt["stg"] += 1
                            rope_norm_block(pb_ap, 4, 0, "q", sl)
                            pending.append((sl, 4, (lambda cb=cb, r0=r0: qaT[cb * 4:(cb + 1) * 4, :, r0:r0 + 128].rearrange("h d t -> d h t"))))
                        elif cb == 2:
                            sl = cnt["stg"] % NSTG
                            cnt["stg"] += 1
                            rope_norm_block(pbank[bk][:, 0:256], 2, 256, "k", sl)
                            pending.append((sl, 2, (lambda r0=r0: kaT[:, :, r0:r0 + 128].rearrange("h d t -> d h t"))))
                            vs = cnt["vst"] % NVS
                            cnt["vst"] += 1
                            P.op("act", I("activation", out=vst[vs][:, 0:256], in_=pbank[bk][:, 256:512], func=AF.Copy),
                                 reads=[B_bank[bk]], writes=[B_vst[vs]])
                            P.dma("sp", va[r0:r0 + 128, :], vst[vs][:, 0:256], reads=[B_vst[vs]])
                        elif cb in (3, 4, 5, 6):
                            sl = cnt["stg"] % NSTG
                            cnt["stg"] += 1
                            isq = cb in (3, 4)
                            rope_part_block(pb_ap, SCALE if isq else 1.0, 0, sl)
                            hb0 = (cb - 3) * 4 if isq else (cb - 5) * 4
                            tgt = qbT if isq else kbT
                            pending.append((sl, 4, (lambda tgt=tgt, hb0=hb0, r0=r0: tgt[hb0:hb0 + 4, :, r0:r0 + 128].rearrange("h d t -> d h t"))))
                        else:
                            vs = cnt["vst"] % NVS
                            cnt["vst"] += 1
                            P.op("act", I("activation", out=vst[vs][:], in_=pbank[bk][:, :], func=AF.Copy),
                                 reads=[B_bank[bk]], writes=[B_vst[vs]])
                            c0 = (cb - 7) * 512
                            P.dma("sp", vb[r0:r0 + 128, c0:c0 + 512], vst[vs][:], reads=[B_vst[vs]])
                        while len(pending) > 2:
                            sl_, nh_, fn_ = pending.pop(0)
                            out_transposes(sl_, nh_, fn_)

                load_tile(0)
                norm_tile(0)
                transp_h(0)
                for i in range(NT):
                    if i + 1 < NT:
                        load_tile(i + 1)
                        norm_tile(i + 1)
                        transp_h(i + 1)
                    proj_tile(i)
                while pending:
                    sl_, nh_, fn_ = pending.pop(0)
                    out_transposes(sl_, nh_, fn_)
            P.barrier()

        def attention(stack, heads, masked):
            pfx = "m" if masked else "u"
            _sbt = sbt

            def sbt2(stack_, name, shape, dt):
                return _sbt(stack_, pfx + name, shape, dt)
            kt_sb = [sbt2(stack, "kt%d" % i, [128, S], BF16) for i in range(2)]
            v_sb = [sbt2(stack, "v%d" % i, [128, NT, 128], BF16) for i in range(2)]
            B_kv = [Buf("kv%d" % i) for i in range(2)]
            NQ = 3
            qt_sb = [sbt2(stack, "qt%d" % i, [128, 512], BF16) for i in range(NQ)]
            B_qt = [Buf("qt%d" % i) for i in range(NQ)]
            NP = 8
            pt_sb = [sbt2(stack, "pt%d" % i, [128, 512], BF16) for i in range(NP)]
            B_pt = [Buf("pt%d" % i) for i in range(NP)]
            if masked:
                pe_sb = [sbt2(stack, "pe%d" % i, [128, 512], BF16) for i in range(NP)]
                B_pe = [Buf("pe%d" % i) for i in range(NP)]
                mask_sb = sbt2(stack, "mask_sb", [128, NREL, 512], BF16)
                B_mask = Buf("mask")
                P.dma("sp", mask_sb[:], maskb.rearrange("r k q -> k r q"), writes=[B_mask])
            if not masked:
                accD = [sbt2(stack, "accD%d" % i, [128, 512], F32) for i in range(2)]
                accP = [sbt2(stack, "accP%d" % i, [128, 512], F32) for i in range(2)]
                hi_sb = [sbt2(stack, "hi%d" % i, [128, 512], BF16) for i in range(2)]
                lo_sb = [sbt2(stack, "lo%d" % i, [128, 512], BF16) for i in range(2)]
                B_accD = [Buf("accD%d" % i) for i in range(2)]
                B_accP = [Buf("accP%d" % i) for i in range(2)]
                B_hi = [Buf("hi%d" % i) for i in range(2)]
                B_lo = [Buf("lo%d" % i) for i in range(2)]
            rd_sb = [sbt2(stack, "rd%d" % i, [128, 512], F32) for i in range(2)]
            o_sb = [sbt2(stack, "o%d" % i, [128, 512], F32) for i in range(2)]
            sq_sb = [sbt2(stack, "sq%d" % i, [128, 512], F32) for i in range(2)]
            ost = [sbt2(stack, "ost%d" % i, [128, 512], BF16) for i in range(2)]
            B_rd = [Buf("rd%d" % i) for i in range(2)]
            B_o = [Buf("o%d" % i) for i in range(2)]
            B_sq = [Buf("sq%d" % i) for i in range(2)]
            B_ost = [Buf("ost%d" % i) for i in range(2)]
            steps = []
            kvslot = {}
            nkv = 0
            for (mh, qT_ap, kT_ap, v_ap, kvkey) in heads:
                for qb in range(NQB):
                    if masked:
                        kcs = [kc for kc in range(qb * 4 - 8, qb * 4 + 12) if 0 <= kc < NT]
                    else:
                        kcs = list(range(NT))
                    for j, kc in enumerate(kcs):
                        steps.append(dict(mh=mh, qT=qT_ap, kT=kT_ap, v=v_ap, kvkey=kvkey, qb=qb, kc=kc,
                                          first=(j == 0), last=(j == len(kcs) - 1)))
            state = {"kv": None, "kvn": 0, "qn": 0, "blk": 0}
            cur_kv = {}
            cur_q = {}

            def issue_loads(st_):
                if st_["kvkey"] not in cur_kv:
                    s = state["kvn"] % 2
                    state["kvn"] += 1
                    cur_kv.clear()
                    cur_kv[st_["kvkey"]] = s
                    P.dma("sp", kt_sb[s][:], st_["kT"], writes=[B_kv[s]])
                    P.dma("sp", v_sb[s][:], st_["v"].rearrange("(c p) d -> p c d", p=128), writes=[B_kv[s]])
                key = (st_["mh"], st_["qb"])
                if key not in cur_q:
                    s = state["qn"] % NQ
                    state["qn"] += 1
                    cur_q.clear()
                    cur_q[key] = s
                    P.dma("sp", qt_sb[s][:], st_["qT"][:, st_["qb"] * 512:(st_["qb"] + 1) * 512], writes=[B_qt[s]])
                st_["kvs"] = cur_kv[st_["kvkey"]]
                st_["qs"] = cur_q[key]

            def emit_scores(n):
                st_ = steps[n]
                issue_loads(st_)
                bk = (0, 1, 2, 7)[n % 4]
                ps_ = n % NP
                kvs, qs, kc = st_["kvs"], st_["qs"], st_["kc"]
                P.op("pe", I("matmul", pbank[bk][:, :], lhsT=kt_sb[kvs][:, kc * 128:(kc + 1) * 128], rhs=qt_sb[qs][:],
                                              start=True, stop=True),
                     reads=[B_kv[kvs], B_qt[qs]], writes=[B_bank[bk]])
                if masked:
                    rel = kc - st_["qb"] * 4 + 8
                    P.op("act", I("activation", out=pe_sb[ps_][:], in_=pbank[bk][:, :], func=AF.Exp),
                         reads=[B_bank[bk]], writes=[B_pe[ps_]])
                    P.op("dve",
                         I("tensor_tensor", out=pt_sb[ps_][:], in0=pe_sb[ps_][:], in1=mask_sb[:, rel, :], op=ALU.mult),
                         reads=[B_pe[ps_], B_mask], writes=[B_pt[ps_]])
                else:
                    P.op("act", I("activation", out=pt_sb[ps_][:], in_=pbank[bk][:, :], func=AF.Exp),
                         reads=[B_bank[bk]], writes=[B_pt[ps_]])

            deferred = []

            def finish(b2, bo, bd, mh, qb):
                if not masked:
                    fd = [I("matmul", pbank[bd][:, :], lhsT=ones_bf[:], rhs=hi_sb[b2][:], start=False, stop=False),
                          I("matmul", pbank[bd][:, :], lhsT=ones_bf[:], rhs=lo_sb[b2][:], start=False, stop=True)]
                    P.group("pe", fd, reads=[B_hi[b2], B_lo[b2], B_ones], writes=[B_bank[bd]])
                P.op("act", I("activation", out=rd_sb[b2][:], in_=pbank[bd][:, :], func=AF.Ln), reads=[B_bank[bd]], writes=[B_rd[b2]])
                P.op("act", I("activation", out=rd_sb[b2][:], in_=rd_sb[b2][:], func=AF.Exp, scale=-1.0), writes=[B_rd[b2]])
                P.op("dve", I("tensor_tensor", out=o_sb[b2][:], in0=pbank[bo][:, :], in1=rd_sb[b2][:], op=ALU.mult),
                     reads=[B_bank[bo], B_rd[b2]], writes=[B_o[b2]])
                fe = "pool" if masked else "dve"
                P.op(fe, I("tensor_tensor", out=sq_sb[b2][:], in0=o_sb[b2][:], in1=o_sb[b2][:], op=ALU.mult),
                     reads=[B_o[b2]], writes=[B_sq[b2]])
                P.op(fe, I("tensor_tensor", out=ost[b2][:], in0=o_sb[b2][:], in1=gout_sb[:, mh:mh + 1].to_broadcast([128, 512]), op=ALU.mult),
                     reads=[B_o[b2], B_gout], writes=[B_ost[b2]])
                P.dma("sp", mixT[mh, :, qb * 512:(qb + 1) * 512], ost[b2][:], reads=[B_ost[b2]])
                fns2 = [I("matmul", pbank[bd][:, t:t + 1], lhsT=sq_sb[b2][:, t * 128:(t + 1) * 128], rhs=ones_f[:],
                          start=True, stop=True) for t in range(4)]
                P.group("pe", fns2, reads=[B_sq[b2], B_onesf], writes=[B_bank[bd]])
                P.op("dve", I("tensor_copy", out=ssq[:, qb * 4:(qb + 1) * 4, mh], in_=pbank[bd][:, 0:4]),
                     reads=[B_bank[bd]], writes=[B_ssq])

            def emit_pv(n):
                st_ = steps[n]
                ps_ = n % NP
                if st_["first"]:
                    st_["blk"] = state["blk"]
                    st_["j"] = 0
                    state["blk"] += 1
                else:
                    st_["blk"] = steps[n - 1]["blk"]
                    st_["j"] = steps[n - 1]["j"] + 1
                b2 = st_["blk"] % 2
                bo, bd = 3 + b2, 5 + b2
                kvs, kc, j = st_["kvs"], st_["kc"], st_["j"]
                if masked:
                    fns = [I("matmul", pbank[bo][:, :], lhsT=v_sb[kvs][:, kc, :], rhs=pt_sb[ps_][:], start=st_["first"], stop=st_["last"]),
                           I("matmul", pbank[bd][:, :], lhsT=ones_bf[:], rhs=pt_sb[ps_][:], start=st_["first"], stop=st_["last"])]
                    P.group("pe", fns, reads=[B_kv[kvs], B_pt[ps_], B_ones], writes=[B_bank[bo], B_bank[bd]])
                else:
                    on_pe = (j % 2 == 1)
                    fns = [I("matmul", pbank[bo][:, :], lhsT=v_sb[kvs][:, kc, :], rhs=pt_sb[ps_][:], start=st_["first"], stop=st_["last"])]
                    wr = [B_bank[bo]]
                    if on_pe:
                        fns.append(I("matmul", pbank[bd][:, :], lhsT=ones_bf[:], rhs=pt_sb[ps_][:], start=(j == 1), stop=False))
                        wr.append(B_bank[bd])
                    P.group("pe", fns, reads=[B_kv[kvs], B_pt[ps_], B_ones], writes=wr)
                    if not on_pe:
                        if j == 0:
                            P.op("dve", I("tensor_copy", out=accD[b2][:], in_=pt_sb[ps_][:]), reads=[B_pt[ps_]], writes=[B_accD[b2]])
                        else:
                            P.op("dve", I("tensor_tensor", out=accD[b2][:], in0=accD[b2][:], in1=pt_sb[ps_][:], op=ALU.add),
                                 reads=[B_pt[ps_]], writes=[B_accD[b2]])
                    if st_["last"]:
                        P.op("dve", I("tensor_copy", out=hi_sb[b2][:], in_=accD[b2][:]), reads=[B_accD[b2]], writes=[B_hi[b2]])
                        P.op("dve", I("tensor_tensor", out=lo_sb[b2][:], in0=accD[b2][:], in1=hi_sb[b2][:], op=ALU.subtract),
                             reads=[B_accD[b2], B_hi[b2]], writes=[B_lo[b2]])
                if st_["last"]:
                    deferred.append([3, (b2, bo, bd, st_["mh"], st_["qb"])])

            LOOK = 3
            for n in range(min(LOOK, len(steps))):
                emit_scores(n)
            for n in range(len(steps)):
                if n + LOOK < len(steps):
                    emit_scores(n + LOOK)
                emit_pv(n)
                for d_ in deferred:
                    d_[0] -= 1
                while deferred and deferred[0][0] <= 0:
                    finish(*deferred.pop(0)[1])
            while deferred:
                finish(*deferred.pop(0)[1])

        emit_casts()
        if 2 in phases:
            with ExitStack() as ps2:
                heads = []
                for h in range(NHA):
                    g = h // 4
                    heads.append((h, qaT[h], kaT[g], va[:, g * 128:(g + 1) * 128], ("a", g)))
                attention(ps2, heads, masked=False)
            P.barrier()
        if 3 in phases:
            with ExitStack() as ps3:
                heads = []
                for h in range(NHB):
                    heads.append((8 + h, qbT[h], kbT[h], vb[:, h * 128:(h + 1) * 128], ("b", h)))
                attention(ps3, heads, masked=True)
            P.barrier()

        emit_casts()
        if 4 in phases:
            with ExitStack() as ps4:
                NRING = 4
                ring = [sbt(ps4, "ring%d" % i, [128, 8192], BF16) for i in range(NRING)]
                B_ring = [Buf("ring%d" % i) for i in range(NRING)]
                mix_sb = sbt(ps4, "mix_sb", [128, 16, 512], BF16)
                B_mix = Buf("mix")
                gffn_sb = sbt(ps4, "gffn_sb", [128, 16], F32)
                gfin_sb = sbt(ps4, "gfin_sb", [128, D], F32)
                B_gffn, B_gfin = Buf("gffn"), Buf("gfin")
                P.dma("sp", gffn_sb[:], g_ffn, writes=[B_gffn])
                P.dma("sp", gfin_sb[:], g_fin, writes=[B_gfin])
                xres = [sbt(ps4, "xres%d" % i, [128, D], F32) for i in range(4)]
                B_xres = [Buf("xres%d" % i) for i in range(4)]
                h2 = [sbt(ps4, "h2_%d" % i, [128, D], BF16) for i in range(2)]
                B_h2 = [Buf("h2_%d" % i) for i in range(2)]
                h2T = sbt(ps4, "h2T", [128, 16, 512], BF16)
                B_h2T = [Buf("h2T_%d" % t) for t in range(4)]
                actT = sbt(ps4, "actT", [128, NFC, 512], BF16)
                B_act = [Buf("act%d" % f) for f in range(NFC)]
                st2 = [sbt(ps4, "st2_%d" % i, [128, 16], F32) for i in range(2)]
                B_st2 = [Buf("st2_%d" % i) for i in range(2)]
                st3 = [sbt(ps4, "st3_%d" % i, [128, 16], F32) for i in range(2)]
                B_st3 = [Buf("st3_%d" % i) for i in range(2)]
                sg = [sbt(ps4, "sg%d" % i, [128, 512], F32) for i in range(2)]
                B_sg = [Buf("sg%d" % i) for i in range(2)]

                items = []
                for tb in range(NQB):
                    for cb in range(4):
                        items.append(("wout", cb, 0))
                    for f2 in range(22):
                        items.append(("wgu", f2, 0))
                    for cb in range(4):
                        for pc in range(3):
                            items.append(("wd", cb, pc))
                rstate = {"loaded": 0, "used": 0}

                def ring_prefetch(upto):
                    while rstate["loaded"] < min(upto, len(items)):
                        n = rstate["loaded"]
                        kind, a, b = items[n]
                        s = n % NRING
                        if kind == "wout":
                            src = wout_s[a].rearrange("p h c -> p (h c)")
                        elif kind == "wgu":
                            src = wgu_s[a].rearrange("p g k c -> p (g k c)")
                        else:
                            nfp = 16 if b < 2 else NFC - 32
                            src = wd_s[a, b, :, 0:nfp, :].rearrange("p f c -> p (f c)")
                        P.dma("pool", ring[s][:, 0:src.shape[1]], src, writes=[B_ring[s]], extra=[cast_state["tok"]])
                        rstate["loaded"] += 1

                def ring_next(kind):
                    n = rstate["used"]
                    assert items[n][0] == kind, (items[n], kind)
                    ring_prefetch(n + NRING)
                    rstate["used"] += 1
                    s = n % NRING
                    return ring[s], B_ring[s]

                tp_bank4 = [pbank[4][:].bitcast(BF16), pbank[5][:].bitcast(BF16)]
                c4 = {"op": 0, "gu": 0}
                ring_prefetch(NRING - 1)
                for tb in range(NQB):
                    t0 = tb * 512
                    s2 = tb % 2
                    P.dma("sp", mix_sb[:], mixT[:, :, t0:t0 + 512].rearrange("h d t -> d h t"), writes=[B_mix])
                    for t in range(4):
                        r0 = t0 + t * 128
                        P.dma("sp", xres[t][:], x[r0:r0 + 128, :], writes=[B_xres[t]])
                    for t in range(4):
                        tt = tb * 4 + t
                        for ab in range(2):
                            P.op("dve", I("tensor_reduce",
                                out=st2[s2][:, t * 2 + ab:t * 2 + ab + 1], in_=ssq[:, tt, ab * 8:(ab + 1) * 8], axis=AX.X, op=ALU.add),
                                reads=[B_ssq], writes=[B_st2[s2]])
                    P.op("dve", I("tensor_scalar", out=st2[s2][:, 0:8], in0=st2[s2][:, 0:8], scalar1=1.0 / 1024, scalar2=EPS,
                                                          op0=ALU.mult, op1=ALU.add), writes=[B_st2[s2]])
                    P.op("act", I("activation", out=st2[s2][:, 0:8], in_=st2[s2][:, 0:8], func=AF.Sqrt), writes=[B_st2[s2]])
                    P.op("dve", I("reciprocal", out=st2[s2][:, 8:16], in_=st2[s2][:, 0:8]), writes=[B_st2[s2]])
                    for cb in range(4):
                        wt, bw = ring_next("wout")
                        wv = wt[:].rearrange("p (h c) -> p h c", h=16)
                        cs = slice(cb * 512, (cb + 1) * 512)
                        for t in range(4):
                            pa = (c4["op"] % 2) * 2
                            c4["op"] += 1
                            pbk = pa + 1
                            fa = [I("matmul", pbank[pa][:, :], lhsT=mix_sb[:, h, t * 128:(t + 1) * 128],
                                                                           rhs=wv[:, h, :], start=(h == 0), stop=(h == 7)) for h in range(8)]
                            fb = [I("matmul", pbank[pbk][:, :], lhsT=mix_sb[:, h, t * 128:(t + 1) * 128],
                                                                             rhs=wv[:, h, :], start=(h == 8), stop=(h == 15)) for h in range(8, 16)]
                            P.group("pe", fa + fb, reads=[B_mix, bw], writes=[B_bank[pa], B_bank[pbk]])
                            P.op("dve", I("scalar_tensor_tensor",
                                out=xres[t][:, cs], in0=pbank[pa][:, :], scalar=st2[s2][:, 8 + t * 2:9 + t * 2], in1=xres[t][:, cs],
                                op0=ALU.mult, op1=ALU.add), reads=[B_bank[pa], B_st2[s2]], writes=[B_xres[t]])
                            P.op("dve", I("scalar_tensor_tensor",
                                out=xres[t][:, cs], in0=pbank[pbk][:, :], scalar=st2[s2][:, 9 + t * 2:10 + t * 2], in1=xres[t][:, cs],
                                op0=ALU.mult, op1=ALU.add), reads=[B_bank[pbk], B_st2[s2]], writes=[B_xres[t]])
                    ss = st3[s2]
                    P.op("dve", I("memset", ss[:], 0.0), writes=[B_st3[s2]])
                    for t in range(4):
                        hs = t % 2
                        P.op("act", I("activation", out=h2[hs][:], in_=xres[t][:], func=AF.Square,
                                                                           accum_out=ss[:, t:t + 1]),
                             reads=[B_xres[t]], writes=[B_h2[hs], B_st3[s2]])
                        P.op("dve", I("tensor_scalar", out=ss[:, 4 + t:5 + t], in0=ss[:, t:t + 1], scalar1=1.0 / D, scalar2=EPS,
                                                                        op0=ALU.mult, op1=ALU.add), writes=[B_st3[s2]])
                        P.op("act", I("activation", out=ss[:, 4 + t:5 + t], in_=ss[:, 4 + t:5 + t], func=AF.Sqrt),
                             writes=[B_st3[s2]])
                        P.op("dve", I("reciprocal", out=ss[:, 4 + t:5 + t], in_=ss[:, 4 + t:5 + t]), writes=[B_st3[s2]])
                        P.op("dve", I("tensor_scalar", out=h2[hs][:], in0=xres[t][:], scalar1=ss[:, 4 + t:5 + t],
                                                                               scalar2=None, op0=ALU.mult),
                             reads=[B_xres[t], B_st3[s2]], writes=[B_h2[hs]])
                        for half in range(2):
                            fns = [I("transpose",
                                tp_bank4[half][:, kk * 128:(kk + 1) * 128], h2[hs][:, (half * 8 + kk) * 128:(half * 8 + kk + 1) * 128], ident[:])
                                for kk in range(8)]
                            P.group("pe", fns, reads=[B_h2[hs], B_ident], writes=[B_bank[4 + half]])
                            src = tp_bank4[half][:, :].rearrange("p (k t) -> p k t", k=8)
                            gm = gffn_sb[:, half * 8:(half + 1) * 8].unsqueeze(2).to_broadcast([128, 8, 128])
                            P.op("dve", I("tensor_tensor",
                                out=h2T[:, half * 8:(half + 1) * 8, t * 128:(t + 1) * 128], in0=src, in1=gm, op=ALU.mult),
                                reads=[B_bank[4 + half], B_gffn], writes=[B_h2T[t]])
                    for f2 in range(22):
                        wt, bw = ring_next("wgu")
                        wv = wt[:].rearrange("p (g k c) -> p g k c", g=2, k=16)
                        for j in range(2):
                            fc = f2 * 2 + j
                            pg = (c4["gu"] % 2) * 2
                            c4["gu"] += 1
                            pu = pg + 1
                            fg = [I("matmul", pbank[pg][:, :], lhsT=wv[:, 0, kc, j * 128:(j + 1) * 128],
                                                                             rhs=h2T[:, kc, :], start=(kc == 0), stop=(kc == 15)) for kc in range(16)]
                            fu = [I("matmul", pbank[pu][:, :], lhsT=wv[:, 1, kc, j * 128:(j + 1) * 128],
                                                                             rhs=h2T[:, kc, :], start=(kc == 0), stop=(kc == 15)) for kc in range(16)]
                            P.group("pe", fg, reads=B_h2T + [bw], writes=[B_bank[pg]])
                            P.group("pe", fu, reads=B_h2T + [bw], writes=[B_bank[pu]])
                            gs = fc % 2
                            P.op("act", I("activation", out=sg[gs][:], in_=pbank[pg][:, :], func=AF.Silu),
                                 reads=[B_bank[pg]], writes=[B_sg[gs]])
                            P.op("dve", I("tensor_tensor", out=actT[:, fc, :], in0=pbank[pu][:, :], in1=sg[gs][:],
                                                                                     op=ALU.mult),
                                 reads=[B_bank[pu], B_sg[gs]], writes=[B_act[fc]])
                    for cb in range(4):
                        cs = slice(cb * 512, (cb + 1) * 512)
                        for pc in range(3):
                            wt, bw = ring_next("wd")
                            wv = wt[:].rearrange("p (f c) -> p f c", f=16)
                            nf = 16 if pc < 2 else NFC - 32
                            for t in range(4):
                                fns = [I("matmul",
                                    pbank[4 + t][:, :], lhsT=actT[:, pc * 16 + f, t * 128:(t + 1) * 128], rhs=wv[:, f, :],
                                    start=(pc == 0 and f == 0), stop=(pc == 2 and f == nf - 1)) for f in range(nf)]
                                P.group("pe", fns, reads=B_act[pc * 16:pc * 16 + nf] + [bw], writes=[B_bank[4 + t]])
                        for t in range(4):
                            P.op("dve", I("tensor_tensor", out=xres[t][:, cs], in0=pbank[4 + t][:, :], in1=xres[t][:, cs],
                                                                            op=ALU.add),
                                 reads=[B_bank[4 + t]], writes=[B_xres[t]])
                    for t in range(4):
                        r0 = t0 + t * 128
                        hs = t % 2
                        P.op("act", I("activation", out=h2[hs][:], in_=xres[t][:], func=AF.Square,
                                                                           accum_out=ss[:, 8 + t:9 + t]),
                             reads=[B_xres[t]], writes=[B_h2[hs], B_st3[s2]])
                        P.op("dve", I("tensor_scalar", out=ss[:, 12 + t:13 + t], in0=ss[:, 8 + t:9 + t], scalar1=1.0 / D,
                                                                        scalar2=EPS, op0=ALU.mult, op1=ALU.add), writes=[B_st3[s2]])
                        P.op("act", I("activation", out=ss[:, 12 + t:13 + t], in_=ss[:, 12 + t:13 + t], func=AF.Sqrt),
                             writes=[B_st3[s2]])
                        P.op("dve", I("reciprocal", out=ss[:, 12 + t:13 + t], in_=ss[:, 12 + t:13 + t]), writes=[B_st3[s2]])
                        P.op("dve", I("scalar_tensor_tensor", out=xres[t][:], in0=xres[t][:], scalar=ss[:, 12 + t:13 + t],
                                                                               in1=gfin_sb[:], op0=ALU.mult, op1=ALU.mult),
                             reads=[B_gfin, B_st3[s2]], writes=[B_xres[t]])
                        P.dma("sp", out[r0:r0 + 128, :], xres[t][:], reads=[B_xres[t]])
        P.fence_stores(engs=("sp",))
        P.run()
    return nc


def _host_inputs(inputs, S):
    f32 = np.float32
    ca, sa, cb, sb_, mask = _const_tables(S)

    def swap(g):
        return np.ascontiguousarray(g.reshape(2, 2, 32)[:, ::-1, :]).reshape(128)

    gq = np.asarray(inputs["g_q_a"], f32)[0]
    gk = np.asarray(inputs["g_k_a"], f32)[0]
    g_qk = np.ascontiguousarray(np.broadcast_to(np.stack([gq, swap(gq), gk, swap(gk)])[None], (128, 4, 128)))
    g_out = np.concatenate([np.asarray(inputs["g_out_a"], f32)[0], np.asarray(inputs["g_out_b"], f32)[0]])
    shared = {
        "w_in": np.ascontiguousarray(np.asarray(inputs["w_in"], f32)[0]),
        "w_out": np.ascontiguousarray(np.asarray(inputs["w_out"], f32)[0]),
        "w_gate_up": np.ascontiguousarray(np.asarray(inputs["w_gate_up"], f32)[0]),
        "w_down": np.ascontiguousarray(np.asarray(inputs["w_down"], f32)[0]),
        "g_mix": np.ascontiguousarray(np.asarray(inputs["g_mix"], f32)[0].reshape(16, 128).T),
        "g_ffn": np.ascontiguousarray(np.asarray(inputs["g_ffn"], f32)[0].reshape(16, 128).T),
        "g_final": np.ascontiguousarray(np.broadcast_to(np.asarray(inputs["g_final"], f32)[None, :], (128, D))),
        "g_qk": g_qk,
        "g_out": np.ascontiguousarray(g_out.reshape(16, 128).T),
        "ropa_c": ca, "ropa_s": sa, "ropb_c": cb, "ropb_s": sb_, "maskb": mask,
    }
    return shared


_NC_CACHE = {}


def kernel(**inputs):
    x = np.asarray(inputs["x"], np.float32)
    B, S, _ = x.shape
    shared = _host_inputs(inputs, S)
    if S not in _NC_CACHE:
        _NC_CACHE[S] = build(S)
    nc = _NC_CACHE[S]
    in_maps = []
    for b in range(B):
        m = dict(shared)
        m["x"] = np.ascontiguousarray(x[b])
        in_maps.append(m)
    res = run_bass_kernel_spmd(nc, in_maps, core_ids=list(range(B)))
    return np.stack([np.asarray(r["out"], np.float32) for r in res.results], axis=0)
```

```python
import numpy as np
from contextlib import ExitStack
import ml_dtypes
import concourse.bass as bass
import concourse.mybir as mybir
from concourse.bass_utils import run_bass_kernel_spmd

F32 = mybir.dt.float32
BF16 = mybir.dt.bfloat16
ALU = mybir.AluOpType
AF = mybir.ActivationFunctionType
AX = mybir.AxisListType

D = 2048
HD = 128
NHA = 8
NKV = 2
NHB = 8
DFF = 5632
PROJ = 4608
EPS = 1e-6
GRID_W = 64
NFC = DFF // 128
SCALE = HD ** -0.5
NREL = 20


def I(name, *a, **k):
    return lambda e: getattr(e, name)(*a, **k)


class Buf:
    __slots__ = ("name", "w", "r", "ld", "st", "excl")

    def __init__(self, name, excl=False):
        self.name = name
        self.excl = excl
        self.w = None
        self.r = {}
        self.ld = None
        self.st = None


class Prog:
    ENG = ("sp", "act", "dve", "pool", "pe")

    def __init__(self, nc, es):
        self.nc = nc
        self.es = es
        self.q = {e: [] for e in self.ENG}
        self.sem = {e: es.enter_context(nc.semaphore("prog_" + e)) for e in self.ENG}
        self.cnt = {e: 0 for e in self.ENG}
        self.waited = {e: {} for e in self.ENG}
        self.dcnt = {}
        self.nsem = 0
        self.stores = {}
        self.alldma = {}

    def new_sem(self, name):
        s = self.es.enter_context(self.nc.semaphore(name + "_%d" % self.nsem))
        self.nsem += 1
        self.dcnt[id(s)] = 0
        return s

    def wait(self, eng, toks):
        w = self.waited[eng]
        for t in toks:
            if t is None:
                continue
            sem, val = t
            if w.get(id(sem), 0) >= val:
                continue
            w[id(sem)] = val
            self.q[eng].append(lambda e, sem=sem, val=val: e.wait_ge(sem, val))

    def _deps(self, reads, writes, extra):
        toks = list(extra)
        for b in reads:
            toks.append(b.w)
            if b.excl:
                toks.extend(b.r.values())
        for b in writes:
            toks.append(b.w)
            toks.extend(b.r.values())
        return toks

    def _commit(self, tok, reads, writes):
        for b in reads:
            b.r[id(tok[0])] = tok
        for b in writes:
            b.w = tok
            b.r = {}

    def op(self, eng, fn, reads=(), writes=(), extra=()):
        return self.group(eng, [fn], reads, writes, extra)

    def group(self, eng, fns, reads=(), writes=(), extra=()):
        self.wait(eng, self._deps(reads, writes, extra))
        self.cnt[eng] += 1
        sem = self.sem[eng]
        tok = (sem, self.cnt[eng])
        for fn in fns[:-1]:
            self.q[eng].append(lambda e, fn=fn: fn(e))
        self.q[eng].append(lambda e, fn=fns[-1], sem=sem: fn(e).then_inc(sem, 1))
        self._commit(tok, reads, writes)
        return tok

    def dma(self, eng, out, in_, reads=(), writes=(), extra=(), sem=None):
        deps = self._deps(reads, writes, extra)
        if writes and writes[0].ld is not None:
            deps = [t for t in deps if t is None or t[0] is not writes[0].ld]
        self.wait(eng, deps)
        if sem is None:
            if writes:
                b = writes[0]
                if b.ld is None:
                    b.ld = self.new_sem("ld_" + b.name)
                sem = b.ld
            else:
                b = reads[0]
                if b.st is None:
                    b.st = self.new_sem("st_" + b.name)
                sem = b.st
        self.dcnt[id(sem)] += 16
        tok = (sem, self.dcnt[id(sem)])
        self.q[eng].append(lambda e, out=out, in_=in_, sem=sem: e.dma_start(out=out, in_=in_).then_inc(sem, 16))
        self._commit(tok, reads, writes)
        if reads and not writes:
            self.stores[id(sem)] = tok
        self.alldma[id(sem)] = tok
        return tok

    def barrier(self):
        toks = [(self.sem[e], self.cnt[e]) for e in self.ENG if self.cnt[e] > 0] + list(self.alldma.values())
        for e in self.ENG:
            self.wait(e, toks)

    def fence_stores(self, engs=("sp", "pool", "act")):
        toks = list(self.stores.values())
        for e in engs:
            self.wait(e, toks)

    def run(self):
        nc = self.nc
        with nc.Block() as block:
            @block.sync
            def _(e):
                for f in self.q["sp"]:
                    f(e)

            @block.scalar
            def _(e):
                for f in self.q["act"]:
                    f(e)

            @block.vector
            def _(e):
                for f in self.q["dve"]:
                    f(e)

            @block.gpsimd
            def _(e):
                for f in self.q["pool"]:
                    f(e)

            @block.tensor
            def _(e):
                for f in self.q["pe"]:
                    f(e)


def _const_tables(S):
    t = np.arange(S)
    half = HD // 2
    inv_a = (10000.0 ** (-(np.arange(0, half, 2, dtype=np.float32) / half))).astype(np.float32)
    row = (t // GRID_W).astype(np.float32)
    col = (t % GRID_W).astype(np.float32)
    ang_r = row[:, None] * inv_a[None, :]
    ang_c = col[:, None] * inv_a[None, :]
    ca = np.zeros((S, 2, 2, 32), np.float32)
    sa = np.zeros((S, 2, 2, 32), np.float32)
    for a, ang in enumerate((ang_r, ang_c)):
        c = np.cos(ang.astype(np.float32)).astype(np.float32)
        s = np.sin(ang.astype(np.float32)).astype(np.float32)
        ca[:, a, 0] = c
        ca[:, a, 1] = c
        sa[:, a, 0] = -s
        sa[:, a, 1] = s
    pr = HD // 4
    inv_b = (500000.0 ** (-(np.arange(0, pr, 2, dtype=np.float32) / pr))).astype(np.float32)
    ang_b = t.astype(np.float32)[:, None] * inv_b[None, :]
    cb = np.cos(ang_b).astype(np.float32)
    sb_ = np.sin(ang_b).astype(np.float32)
    cbt = np.concatenate([cb, cb], axis=1)
    sbt = np.concatenate([-sb_, sb_], axis=1)
    k = np.arange(128)[:, None]
    q = np.arange(512)[None, :]
    mask = np.zeros((NREL, 128, 512), np.float32)
    for r in range(NREL):
        d = 128 * (r - 8) + k - q
        ad = np.abs(d)
        mask[r] = (ad <= 64).astype(np.float32) + ((d % 4 == 0) & (ad <= 256)) + ((d % 16 == 0) & (ad <= 1024))
    return (ca.reshape(S, 128), sa.reshape(S, 128), cbt.astype(np.float32), sbt.astype(np.float32),
            mask.astype(ml_dtypes.bfloat16))


def build(S=4096, debug=False, phases=(1, 2, 3, 4)):
    NT = S // 128
    NQB = S // 512
    nc = bass.Bass("TRN2", target_bir_lowering=False)

    def din(name, shape, dt=F32):
        return nc.dram_tensor(name, list(shape), dt, kind="ExternalInput").ap()

    def dscr(name, shape, dt):
        if debug:
            return nc.dram_tensor(name, list(shape), dt, kind="ExternalOutput").ap()
        return nc.dram_tensor(name, list(shape), dt).ap()

    x = din("x", [S, D])
    w_in = din("w_in", [D, PROJ])
    w_out = din("w_out", [D, D])
    w_gu = din("w_gate_up", [D, 2 * DFF])
    w_dn = din("w_down", [DFF, D])
    g_mix = din("g_mix", [128, 16])
    g_ffn = din("g_ffn", [128, 16])
    g_fin = din("g_final", [128, D])
    g_qk = din("g_qk", [128, 4, 128])
    g_out = din("g_out", [128, 16])
    ropa_c = din("ropa_c", [S, 128])
    ropa_s = din("ropa_s", [S, 128])
    ropb_c = din("ropb_c", [S, 32])
    ropb_s = din("ropb_s", [S, 32])
    maskb = din("maskb", [NREL, 128, 512], BF16)
    out = nc.dram_tensor("out", [S, D], F32, kind="ExternalOutput").ap()

    qaT = dscr("qaT", [NHA, 128, S], BF16)
    kaT = dscr("kaT", [NKV, 128, S], BF16)
    va = dscr("va", [S, NKV * 128], BF16)
    qbT = dscr("qbT", [NHB, 128, S], BF16)
    kbT = dscr("kbT", [NHB, 128, S], BF16)
    vb = dscr("vb", [S, NHB * 128], BF16)
    mixT = dscr("mixT", [16, 128, S], BF16)
    x1d = dscr("x1d", [S, D], F32)
    wout_s = dscr("wout_s", [4, 128, 16, 512], BF16)
    wgu_s = dscr("wgu_s", [22, 128, 2, 16, 256], BF16)
    wd_s = dscr("wd_s", [4, 3, 128, 16, 512], BF16)

    with ExitStack() as es:
        P = Prog(nc, es)

        def sbt(stack, name, shape, dt):
            return stack.enter_context(nc.sbuf_tensor(name, list(shape), dt))

        def pst(stack, name, shape, dt):
            return stack.enter_context(nc.psum_tensor(name, list(shape), dt))

        ident = sbt(es, "ident", [128, 128], BF16)
        ones_bf = sbt(es, "ones_bf", [128, 128], BF16)
        ones_f = sbt(es, "ones_f", [128, 1], F32)
        gout_sb = sbt(es, "gout_sb", [128, 16], F32)
        ssq = sbt(es, "ssq", [128, NT, 16], F32)
        B_ident, B_ones, B_onesf, B_gout, B_ssq = Buf("ident"), Buf("ones"), Buf("onesf"), Buf("gout"), Buf("ssq")
        P.op("pool", I("memset", ident[:], 1.0), writes=[B_ident])
        P.op("pool", I("affine_select", out=ident[:], in_=ident[:], pattern=[[-1, 128]], compare_op=ALU.is_equal,
                                                fill=0.0, base=0, channel_multiplier=1), writes=[B_ident])
        P.op("pool", I("memset", ones_bf[:], 1.0), writes=[B_ones])
        P.op("pool", I("memset", ones_f[:], 1.0), writes=[B_onesf])

        P.dma("sp", gout_sb[:], g_out, writes=[B_gout])

        pbank = [pst(es, "pbank%d" % i, [128, 512], F32) for i in range(8)]
        B_bank = [Buf("bank%d" % i, excl=True) for i in range(8)]

        cast_sem = P.new_sem("cast")
        cast_state = {"tok": None, "done": False}

        cast_list = []
        if 4 in phases:
            for cb in range(4):
                cast_list.append((wout_s[cb], w_out[:, cb * 512:(cb + 1) * 512].rearrange("(h p) c -> p h c", p=128)))
            for f2 in range(22):
                for gu in range(2):
                    c0 = gu * DFF + f2 * 256
                    cast_list.append((wgu_s[f2, :, gu], w_gu[:, c0:c0 + 256].rearrange("(kc p) c -> p kc c", p=128)))
            for cb in range(4):
                for pc in range(3):
                    n = 16 if pc < 2 else NFC - 32
                    r0 = pc * 16 * 128
                    cast_list.append((wd_s[cb, pc, :, 0:n, :],
                                      w_dn[r0:r0 + n * 128, cb * 512:(cb + 1) * 512].rearrange("(fc p) c -> p fc c", p=128)))

        def cast_one():
            if cast_list:
                dst, src = cast_list.pop(0)
                cast_state["tok"] = P.dma("pool", dst, src, sem=cast_sem)

        def emit_casts():
            while cast_list:
                cast_one()

        if 1 in phases:
            with ExitStack() as ps1:
                w_sb = sbt(ps1, "w_sb", [128, 16, PROJ], BF16)
                B_w = Buf("w_sb")
                for kq in range(4):
                    P.dma("pool", w_sb[:, kq * 4:(kq + 1) * 4, :],
                          w_in[kq * 512:(kq + 1) * 512, :].rearrange("(kc p) c -> p kc c", p=128), writes=[B_w])
                gmix_sb = sbt(ps1, "gmix_sb", [128, 16], F32)
                gqk_sb = sbt(ps1, "gqk_sb", [128, 4, 128], F32)
                B_gmix, B_gqk = Buf("gmix"), Buf("gqk")
                P.dma("sp", gmix_sb[:], g_mix, writes=[B_gmix])
                P.dma("sp", gqk_sb[:], g_qk, writes=[B_gqk])

                xb = [sbt(ps1, "xb%d" % i, [128, D], F32) for i in range(2)]
                B_x = [Buf("xb%d" % i) for i in range(2)]
                tabs = [sbt(ps1, "tabs%d" % i, [128, 320], F32) for i in range(2)]
                B_tabs = [Buf("tabs%d" % i) for i in range(2)]
                dtab = [sbt(ps1, "dtab%d" % i, [128, 4 * 128], F32) for i in range(2)]
                B_dtab = [Buf("dtab%d" % i) for i in range(2)]
                stat = [sbt(ps1, "stat%d" % i, [128, 8], F32) for i in range(2)]
                B_stat = [Buf("stat%d" % i) for i in range(2)]
                hb = [sbt(ps1, "hb0", [128, D], BF16)] * 2
                B_h = [Buf("hb0")] * 2
                neghalf = sbt(ps1, "neghalf", [128, 4], F32)
                B_neghalf = Buf("neghalf")
                P.op("pool", I("memset", neghalf[:], -0.5), writes=[B_neghalf])
                hT = [sbt(ps1, "hT%d" % i, [128, 16, 128], BF16) for i in range(2)]
                B_hT = [Buf("hT%d" % i) for i in range(2)]
                NSC = 2
                scr1 = [sbt(ps1, "scr1_%d" % i, [128, 512], F32) for i in range(NSC)]
                scr2 = [sbt(ps1, "scr2_%d" % i, [128, 512], F32) for i in range(NSC)]
                scr3 = [sbt(ps1, "scr3_%d" % i, [128, 512], F32) for i in range(NSC)]
                B_scr1 = [Buf("scr1_%d" % i) for i in range(NSC)]
                B_scr2 = [Buf("scr2_%d" % i) for i in range(NSC)]
                B_scr3 = [Buf("scr3_%d" % i) for i in range(NSC)]
                st4 = [sbt(ps1, "st4_%d" % i, [128, 8], F32) for i in range(NSC)]
                B_st4 = [Buf("st4_%d" % i) for i in range(NSC)]
                NSTG = 6
                stg = [sbt(ps1, "stg%d" % i, [128, 512], BF16) for i in range(NSTG)]
                B_stg = [Buf("stg%d" % i) for i in range(NSTG)]
                NTS = 3
                tst = [sbt(ps1, "tst%d" % i, [128, 4, 128], BF16) for i in range(NTS)]
                B_tst = [Buf("tst%d" % i) for i in range(NTS)]
                NVS = 2
                vst = [sbt(ps1, "vst%d" % i, [128, 512], BF16) for i in range(NVS)]
                B_vst = [Buf("vst%d" % i) for i in range(NVS)]

                tp_bank = [pbank[0][:].bitcast(BF16), pbank[1][:].bitcast(BF16)]

                def load_tile(i):
                    s = i % 2
                    r0 = i * 128
                    P.dma("sp", xb[s][:], x[r0:r0 + 128, :], writes=[B_x[s]])
                    P.dma("sp", tabs[s][:, 0:128], ropa_c[r0:r0 + 128, :], writes=[B_tabs[s]])
                    P.dma("sp", tabs[s][:, 128:256], ropa_s[r0:r0 + 128, :], writes=[B_tabs[s]])
                    P.dma("sp", tabs[s][:, 256:288], ropb_c[r0:r0 + 128, :], writes=[B_tabs[s]])
                    P.dma("sp", tabs[s][:, 288:320], ropb_s[r0:r0 + 128, :], writes=[B_tabs[s]])

                def norm_tile(i):
                    s = i % 2
                    P.op("dve", I("memset", stat[s][:], 0.0), writes=[B_stat[s]])
                    P.op("act", I("activation", out=hb[s][:], in_=xb[s][:], func=AF.Square, accum_out=stat[s][:, 0:1]),
                         reads=[B_x[s]], writes=[B_h[s], B_stat[s]])
                    P.op("dve", I("tensor_scalar", out=stat[s][:, 1:2], in0=stat[s][:, 0:1], scalar1=1.0 / D, scalar2=EPS,
                                                          op0=ALU.mult, op1=ALU.add), writes=[B_stat[s]])
                    P.op("act", I("activation", out=stat[s][:, 2:3], in_=stat[s][:, 1:2], func=AF.Sqrt), writes=[B_stat[s]])
                    P.op("dve", I("reciprocal", out=stat[s][:, 3:4], in_=stat[s][:, 2:3]), writes=[B_stat[s]])
                    P.op("dve", I("tensor_scalar", out=hb[s][:], in0=xb[s][:], scalar1=stat[s][:, 3:4], scalar2=None,
                                                          op0=ALU.mult), reads=[B_x[s], B_stat[s]], writes=[B_h[s]])
                    t_, d_ = tabs[s], dtab[s]
                    P.op("pool", I("tensor_tensor", out=d_[:, 0:128], in0=t_[:, 0:128], in1=gqk_sb[:, 0, :], op=ALU.mult),
                         reads=[B_tabs[s], B_gqk], writes=[B_dtab[s]])
                    P.op("pool", I("tensor_tensor", out=d_[:, 128:256], in0=t_[:, 128:256], in1=gqk_sb[:, 1, :], op=ALU.mult),
                         reads=[B_tabs[s], B_gqk], writes=[B_dtab[s]])
                    P.op("pool", I("tensor_tensor", out=d_[:, 256:384], in0=t_[:, 0:128], in1=gqk_sb[:, 2, :], op=ALU.mult),
                         reads=[B_tabs[s], B_gqk], writes=[B_dtab[s]])
                    P.op("pool", I("tensor_tensor", out=d_[:, 384:512], in0=t_[:, 128:256], in1=gqk_sb[:, 3, :], op=ALU.mult),
                         reads=[B_tabs[s], B_gqk], writes=[B_dtab[s]])

                def transp_h(i):
                    s = i % 2
                    for half in range(2):
                        fns = []
                        for kk in range(8):
                            kc = half * 8 + kk
                            fns.append(I("transpose",
                                tp_bank[half][:, kk * 128:(kk + 1) * 128], hb[s][:, kc * 128:(kc + 1) * 128], ident[:]))
                        P.group("pe", fns, reads=[B_h[s], B_ident], writes=[B_bank[half]])
                        src = tp_bank[half][:, :].rearrange("p (k t) -> p k t", k=8)
                        gm = gmix_sb[:, half * 8:(half + 1) * 8].unsqueeze(2).to_broadcast([128, 8, 128])
                        P.op("dve", I("tensor_tensor",
                            out=hT[s][:, half * 8:(half + 1) * 8, :], in0=src, in1=gm, op=ALU.mult),
                            reads=[B_bank[half], B_gmix], writes=[B_hT[s]])

                cnt = {"scr": 0, "stg": 0, "tst": 0, "vst": 0, "pj": 0, "tq": 0}
                pending = []

                def rope_norm_block(pb_ap, nh, t1off, rstd_mode, dst_slot):
                    k = cnt["scr"] % NSC
                    cnt["scr"] += 1
                    s_ = cur["s"]
                    bank = cur["bank"]
                    d_ = dtab[s_]
                    W = nh * 128
                    P.op("act", I("activation", out=scr2[k][:, 0:W], in_=pb_ap, func=AF.Copy), reads=[bank], writes=[B_scr2[k]])
                    P.op("pool", I("memset", st4[k][:], 0.0), writes=[B_st4[k]])
                    for h in range(nh):
                        P.op("act", I("activation", out=scr1[k][:, h * 128:(h + 1) * 128], in_=scr2[k][:, h * 128:(h + 1) * 128], func=AF.Square,
                                      accum_out=st4[k][:, h:h + 1]), reads=[B_scr2[k]], writes=[B_scr1[k], B_st4[k]])
                    if rstd_mode == "q":
                        P.op("pool", I("tensor_scalar", out=st4[k][:, 0:nh], in0=st4[k][:, 0:nh], scalar1=1.0, scalar2=128 * EPS,
                                       op0=ALU.mult, op1=ALU.add), writes=[B_st4[k]])
                    else:
                        P.op("pool", I("tensor_scalar", out=st4[k][:, 0:nh], in0=st4[k][:, 0:nh], scalar1=1.0 / 128, scalar2=EPS,
                                       op0=ALU.mult, op1=ALU.add), writes=[B_st4[k]])
                    P.op("pool", I("tensor_tensor", out=st4[k][:, 4:4 + nh], in0=st4[k][:, 0:nh], in1=neghalf[:, 0:nh], op=ALU.pow),
                         reads=[B_neghalf], writes=[B_st4[k]])
                    xs = scr2[k][:, 0:W]
                    T1 = d_[:, t1off:t1off + 128].unsqueeze(1).to_broadcast([128, nh, 128])
                    P.op("dve", I("tensor_tensor", out=scr1[k][:, 0:W].rearrange("p (h d) -> p h d", h=nh),
                                  in0=xs.rearrange("p (h d) -> p h d", h=nh), in1=T1, op=ALU.mult),
                         reads=[B_scr2[k], B_dtab[s_]], writes=[B_scr1[k]])
                    x5 = xs.rearrange("p (h a f j) -> p h a f j", h=nh, a=2, f=2)
                    o5 = scr3[k][:, 0:W].rearrange("p (h a f j) -> p h a f j", h=nh, a=2, f=2)
                    T2 = d_[:, t1off + 128:t1off + 256].rearrange("p (a f j) -> p a f j", a=2, f=2)
                    for f in range(2):
                        tb = T2[:, :, f, :].unsqueeze(1).to_broadcast([128, nh, 2, 32])
                        P.op("dve", I("tensor_tensor", out=o5[:, :, :, f, :], in0=x5[:, :, :, 1 - f, :], in1=tb, op=ALU.mult),
                             reads=[B_scr2[k], B_dtab[s_]], writes=[B_scr3[k]])
                    P.op("dve", I("tensor_tensor", out=scr1[k][:, 0:W], in0=scr1[k][:, 0:W], in1=scr3[k][:, 0:W], op=ALU.add),
                         reads=[B_scr3[k]], writes=[B_scr1[k]])
                    rb = st4[k][:, 4:4 + nh].unsqueeze(2).to_broadcast([128, nh, 128])
                    P.op("dve", I("tensor_tensor", out=stg[dst_slot][:, 0:W].rearrange("p (h d) -> p h d", h=nh),
                                  in0=scr1[k][:, 0:W].rearrange("p (h d) -> p h d", h=nh), in1=rb, op=ALU.mult),
                         reads=[B_scr1[k], B_st4[k]], writes=[B_stg[dst_slot]])

                def rope_part_block(pb_ap, scale, coff, dst_slot):
                    k = cnt["scr"] % NSC
                    cnt["scr"] += 1
                    s_ = cur["s"]
                    bank = cur["bank"]
                    P.op("act", I("activation", out=stg[dst_slot][:], in_=pb_ap, func=AF.Copy, scale=float(scale)),
                         reads=[bank], writes=[B_stg[dst_slot]])
                    pb3 = pb_ap.rearrange("p (h d) -> p h d", h=4)
                    xr = scr3[k][:, 0:128].rearrange("p (h j) -> p h j", h=4)
                    P.op("act", I("activation", out=xr, in_=pb3[:, :, 0:32], func=AF.Copy, scale=float(scale)),
                         reads=[bank], writes=[B_scr3[k]])
                    ctab = tabs[s_][:, 256:288]
                    stab = tabs[s_][:, 288:320]
                    tb_ = B_tabs[s_]
                    r1 = scr1[k][:, 0:128].rearrange("p (h j) -> p h j", h=4)
                    r2 = scr2[k][:, 0:128].rearrange("p (h j) -> p h j", h=4)
                    P.op("dve", I("tensor_tensor", out=r1, in0=xr, in1=ctab.unsqueeze(1).to_broadcast([128, 4, 32]), op=ALU.mult),
                         reads=[B_scr3[k], tb_], writes=[B_scr1[k]])
                    for f in range(2):
                        P.op("dve", I("tensor_tensor", out=r2[:, :, f * 16:(f + 1) * 16], in0=xr[:, :, (1 - f) * 16:(2 - f) * 16],
                                      in1=stab[:, f * 16:(f + 1) * 16].unsqueeze(1).to_broadcast([128, 4, 16]), op=ALU.mult),
                             reads=[B_scr3[k], tb_], writes=[B_scr2[k]])
                    P.op("dve", I("tensor_tensor", out=stg[dst_slot][:].rearrange("p (h d) -> p h d", h=4)[:, :, 0:32], in0=r1, in1=r2, op=ALU.add),
                         reads=[B_scr1[k], B_scr2[k]], writes=[B_stg[dst_slot]])

                def out_transposes(slot, nh, dst_fn):
                    tb = 5 + cnt["tq"] % 2
                    cnt["tq"] += 1
                    tpv = pbank[tb][:].bitcast(BF16)
                    fns = [I("transpose", tpv[:, h * 128:(h + 1) * 128], stg[slot][:, h * 128:(h + 1) * 128], ident[:])
                           for h in range(nh)]
                    P.group("pe", fns, reads=[B_stg[slot], B_ident], writes=[B_bank[tb]])
                    ts = cnt["tst"] % NTS
                    cnt["tst"] += 1
                    P.op("act", I("activation", out=tst[ts][:, 0:nh, :], in_=tpv[:, 0:nh * 128].rearrange("p (h t) -> p h t", h=nh),
                                                       func=AF.Copy), reads=[B_bank[tb]], writes=[B_tst[ts]])
                    P.dma("sp", dst_fn(), tst[ts][:, 0:nh, :], reads=[B_tst[ts]])

                cur = {}

                def proj_tile(i):
                    s = i % 2
                    r0 = i * 128
                    cur["s"] = s
                    for cb in range(9):
                        bk = 2 + cnt["pj"] % 3
                        cnt["pj"] += 1
                        cur["bank"] = B_bank[bk]
                        pb_ap = pbank[bk][:, :]
                        fns = [I("matmul", pbank[bk][:, :], lhsT=hT[s][:, kc, :],
                                                                      rhs=w_sb[:, kc, cb * 512:(cb + 1) * 512],
                                                                      start=(kc == 0), stop=(kc == 15)) for kc in range(16)]
                        P.group("pe", fns, reads=[B_hT[s], B_w], writes=[B_bank[bk]])
                        if cb in (0, 1):
                            sl = cnt["stg"] % NSTG
                            cnt["stg"] += 1
                            rope_norm_block(pb_ap, 4, 0, "q", sl)
                            pending.append((sl, 4, (lambda cb=cb, r0=r0: qaT[cb * 4:(cb + 1) * 4, :, r0:r0 + 128].rearrange("h d t -> d h t"))))
                        elif cb == 2:
                            sl = cnt["stg"] % NSTG
                            cnt["stg"] += 1
                            rope_norm_block(pbank[bk][:, 0:256], 2, 256, "k", sl)
                            pending.append((sl, 2, (lambda r0=r0: kaT[:, :, r0:r0 + 128].rearrange("h d t -> d h t"))))
                            vs = cnt["vst"] % NVS
                            cnt["vst"] += 1
                            P.op("act", I("activation", out=vst[vs][:, 0:256], in_=pbank[bk][:, 256:512], func=AF.Copy),
                                 reads=[B_bank[bk]], writes=[B_vst[vs]])
                            P.dma("sp", va[r0:r0 + 128, :], vst[vs][:, 0:256], reads=[B_vst[vs]])
                        elif cb in (3, 4, 5, 6):
                            sl = cnt["stg"] % NSTG
                            cnt["stg"] += 1
                            isq = cb in (3, 4)
                            rope_part_block(pb_ap, SCALE if isq else 1.0, 0, sl)
                            hb0 = (cb - 3) * 4 if isq else (cb - 5) * 4
                            tgt = qbT if isq else kbT
                            pending.append((sl, 4, (lambda tgt=tgt, hb0=hb0, r0=r0: tgt[hb0:hb0 + 4, :, r0:r0 + 128].rearrange("h d t -> d h t"))))
                        else:
                            vs = cnt["vst"] % NVS
                            cnt["vst"] += 1
                            P.op("act", I("activation", out=vst[vs][:], in_=pbank[bk][:, :], func=AF.Copy),
                                 reads=[B_bank[bk]], writes=[B_vst[vs]])
                            c0 = (cb - 7) * 512
                            P.dma("sp", vb[r0:r0 + 128, c0:c0 + 512], vst[vs][:], reads=[B_vst[vs]])
                        while len(pending) > 4:
                            sl_, nh_, fn_ = pending.pop(0)
                            out_transposes(sl_, nh_, fn_)

                load_tile(0)
                norm_tile(0)
                transp_h(0)
                for i in range(NT):
                    if i + 1 < NT:
                        load_tile(i + 1)
                        norm_tile(i + 1)
                        transp_h(i + 1)
                    proj_tile(i)
                while pending:
                    sl_, nh_, fn_ = pending.pop(0)
                    out_transposes(sl_, nh_, fn_)
            P.barrier()

        def attention(stack, heads, masked):
            pfx = "m" if masked else "u"
            _sbt = sbt

            def sbt2(stack_, name, shape, dt):
                return _sbt(stack_, pfx + name, shape, dt)
            kt_sb = [sbt2(stack, "kt%d" % i, [128, S], BF16) for i in range(2)]
            v_sb = [sbt2(stack, "v%d" % i, [128, NT, 128], BF16) for i in range(2)]
            B_kv = [Buf("kv%d" % i) for i in range(2)]
            NQ = 3
            qt_sb = [sbt2(stack, "qt%d" % i, [128, 512], BF16) for i in range(NQ)]
            B_qt = [Buf("qt%d" % i) for i in range(NQ)]
            NP = 8
            pt_sb = [sbt2(stack, "pt%d" % i, [128, 512], BF16) for i in range(NP)]
            B_pt = [Buf("pt%d" % i) for i in range(NP)]
            if masked:
                pe_sb = [sbt2(stack, "pe%d" % i, [128, 512], BF16) for i in range(NP)]
                B_pe = [Buf("pe%d" % i) for i in range(NP)]
                mask_sb = sbt2(stack, "mask_sb", [128, NREL, 512], BF16)
                B_mask = Buf("mask")
                P.dma("sp", mask_sb[:], maskb.rearrange("r k q -> k r q"), writes=[B_mask])
            if not masked:
                accD = [sbt2(stack, "accD%d" % i, [128, 512], F32) for i in range(2)]
                accP = [sbt2(stack, "accP%d" % i, [128, 512], F32) for i in range(2)]
                hi_sb = [sbt2(stack, "hi%d" % i, [128, 512], BF16) for i in range(2)]
                lo_sb = [sbt2(stack, "lo%d" % i, [128, 512], BF16) for i in range(2)]
                B_accD = [Buf("accD%d" % i) for i in range(2)]
                B_accP = [Buf("accP%d" % i) for i in range(2)]
                B_hi = [Buf("hi%d" % i) for i in range(2)]
                B_lo = [Buf("lo%d" % i) for i in range(2)]
            rd_sb = [sbt2(stack, "rd%d" % i, [128, 512], F32) for i in range(2)]
            o_sb = [sbt2(stack, "o%d" % i, [128, 512], F32) for i in range(2)]
            sq_sb = [sbt2(stack, "sq%d" % i, [128, 512], F32) for i in range(2)]
            ost = [sbt2(stack, "ost%d" % i, [128, 512], BF16) for i in range(2)]
            B_rd = [Buf("rd%d" % i) for i in range(2)]
            B_o = [Buf("o%d" % i) for i in range(2)]
            B_sq = [Buf("sq%d" % i) for i in range(2)]
            B_ost = [Buf("ost%d" % i) for i in range(2)]
            steps = []
            kvslot = {}
            nkv = 0
            for (mh, qT_ap, kT_ap, v_ap, kvkey) in heads:
                for qb in range(NQB):
                    if masked:
                        kcs = [kc for kc in range(qb * 4 - 8, qb * 4 + 12) if 0 <= kc < NT]
                    else:
                        kcs = list(range(NT))
                    for j, kc in enumerate(kcs):
                        steps.append(dict(mh=mh, qT=qT_ap, kT=kT_ap, v=v_ap, kvkey=kvkey, qb=qb, kc=kc,
                                          first=(j == 0), last=(j == len(kcs) - 1)))
            state = {"kv": None, "kvn": 0, "qn": 0, "blk": 0}
            cur_kv = {}
            cur_q = {}

            def issue_loads(st_):
                if st_["kvkey"] not in cur_kv:
                    s = state["kvn"] % 2
                    state["kvn"] += 1
                    cur_kv.clear()
                    cur_kv[st_["kvkey"]] = s
                    P.dma("sp", kt_sb[s][:], st_["kT"], writes=[B_kv[s]])
                    P.dma("sp", v_sb[s][:], st_["v"].rearrange("(c p) d -> p c d", p=128), writes=[B_kv[s]])
                key = (st_["mh"], st_["qb"])
                if key not in cur_q:
                    s = state["qn"] % NQ
                    state["qn"] += 1
                    cur_q.clear()
                    cur_q[key] = s
                    P.dma("sp", qt_sb[s][:], st_["qT"][:, st_["qb"] * 512:(st_["qb"] + 1) * 512], writes=[B_qt[s]])
                st_["kvs"] = cur_kv[st_["kvkey"]]
                st_["qs"] = cur_q[key]

            def emit_scores(n):
                st_ = steps[n]
                issue_loads(st_)
                bk = (0, 1, 2, 7)[n % 4]
                ps_ = n % NP
                kvs, qs, kc = st_["kvs"], st_["qs"], st_["kc"]
                P.op("pe", I("matmul", pbank[bk][:, :], lhsT=kt_sb[kvs][:, kc * 128:(kc + 1) * 128], rhs=qt_sb[qs][:],
                                              start=True, stop=True),
                     reads=[B_kv[kvs], B_qt[qs]], writes=[B_bank[bk]])
                if masked:
                    rel = kc - st_["qb"] * 4 + 8
                    P.op("act", I("activation", out=pe_sb[ps_][:], in_=pbank[bk][:, :], func=AF.Exp),
                         reads=[B_bank[bk]], writes=[B_pe[ps_]])
                    P.op("dve",
                         I("tensor_tensor", out=pt_sb[ps_][:], in0=pe_sb[ps_][:], in1=mask_sb[:, rel, :], op=ALU.mult),
                         reads=[B_pe[ps_], B_mask], writes=[B_pt[ps_]])
                else:
                    P.op("act", I("activation", out=pt_sb[ps_][:], in_=pbank[bk][:, :], func=AF.Exp),
                         reads=[B_bank[bk]], writes=[B_pt[ps_]])

            deferred = []

            def finish(b2, bo, bd, mh, qb):
                if not masked:
                    fd = [I("matmul", pbank[bd][:, :], lhsT=ones_bf[:], rhs=hi_sb[b2][:], start=False, stop=False),
                          I("matmul", pbank[bd][:, :], lhsT=ones_bf[:], rhs=lo_sb[b2][:], start=False, stop=True)]
                    P.group("pe", fd, reads=[B_hi[b2], B_lo[b2], B_ones], writes=[B_bank[bd]])
                P.op("act", I("activation", out=rd_sb[b2][:], in_=pbank[bd][:, :], func=AF.Ln), reads=[B_bank[bd]], writes=[B_rd[b2]])
                P.op("act", I("activation", out=rd_sb[b2][:], in_=rd_sb[b2][:], func=AF.Exp, scale=-1.0), writes=[B_rd[b2]])
                P.op("dve", I("tensor_tensor", out=o_sb[b2][:], in0=pbank[bo][:, :], in1=rd_sb[b2][:], op=ALU.mult),
                     reads=[B_bank[bo], B_rd[b2]], writes=[B_o[b2]])
                fe = "pool" if masked else "dve"
                P.op(fe, I("tensor_tensor", out=sq_sb[b2][:], in0=o_sb[b2][:], in1=o_sb[b2][:], op=ALU.mult),
                     reads=[B_o[b2]], writes=[B_sq[b2]])
                P.op(fe, I("tensor_tensor", out=ost[b2][:], in0=o_sb[b2][:], in1=gout_sb[:, mh:mh + 1].to_broadcast([128, 512]), op=ALU.mult),
                     reads=[B_o[b2], B_gout], writes=[B_ost[b2]])
                P.dma("sp", mixT[mh, :, qb * 512:(qb + 1) * 512], ost[b2][:], reads=[B_ost[b2]])
                fns2 = [I("matmul", pbank[bd][:, t:t + 1], lhsT=sq_sb[b2][:, t * 128:(t + 1) * 128], rhs=ones_f[:],
                          start=True, stop=True) for t in range(4)]
                P.group("pe", fns2, reads=[B_sq[b2], B_onesf], writes=[B_bank[bd]])
                P.op("dve", I("tensor_copy", out=ssq[:, qb * 4:(qb + 1) * 4, mh], in_=pbank[bd][:, 0:4]),
                     reads=[B_bank[bd]], writes=[B_ssq])

            def emit_pv(n):
                st_ = steps[n]
                ps_ = n % NP
                if st_["first"]:
                    st_["blk"] = state["blk"]
                    st_["j"] = 0
                    state["blk"] += 1
                else:
                    st_["blk"] = steps[n - 1]["blk"]
                    st_["j"] = steps[n - 1]["j"] + 1
                b2 = st_["blk"] % 2
                bo, bd = 3 + b2, 5 + b2
                kvs, kc, j = st_["kvs"], st_["kc"], st_["j"]
                if masked:
                    fns = [I("matmul", pbank[bo][:, :], lhsT=v_sb[kvs][:, kc, :], rhs=pt_sb[ps_][:], start=st_["first"], stop=st_["last"]),
                           I("matmul", pbank[bd][:, :], lhsT=ones_bf[:], rhs=pt_sb[ps_][:], start=st_["first"], stop=st_["last"])]
                    P.group("pe", fns, reads=[B_kv[kvs], B_pt[ps_], B_ones], writes=[B_bank[bo], B_bank[bd]])
                else:
                    on_pe = (j % 2 == 1)
                    fns = [I("matmul", pbank[bo][:, :], lhsT=v_sb[kvs][:, kc, :], rhs=pt_sb[ps_][:], start=st_["first"], stop=st_["last"])]
                    wr = [B_bank[bo]]
                    if on_pe:
                        fns.append(I("matmul", pbank[bd][:, :], lhsT=ones_bf[:], rhs=pt_sb[ps_][:], start=(j == 1), stop=False))
                        wr.append(B_bank[bd])
                    P.group("pe", fns, reads=[B_kv[kvs], B_pt[ps_], B_ones], writes=wr)
                    if not on_pe:
                        if j == 0:
                            P.op("dve", I("tensor_copy", out=accD[b2][:], in_=pt_sb[ps_][:]), reads=[B_pt[ps_]], writes=[B_accD[b2]])
                        else:
                            P.op("dve", I("tensor_tensor", out=accD[b2][:], in0=accD[b2][:], in1=pt_sb[ps_][:], op=ALU.add),
                                 reads=[B_pt[ps_]], writes=[B_accD[b2]])
                    if st_["last"]:
                        P.op("dve", I("tensor_copy", out=hi_sb[b2][:], in_=accD[b2][:]), reads=[B_accD[b2]], writes=[B_hi[b2]])
                        P.op("dve", I("tensor_tensor", out=lo_sb[b2][:], in0=accD[b2][:], in1=hi_sb[b2][:], op=ALU.subtract),
                             reads=[B_accD[b2], B_hi[b2]], writes=[B_lo[b2]])
                if st_["last"]:
                    deferred.append([3, (b2, bo, bd, st_["mh"], st_["qb"])])

            LOOK = 3
            for n in range(min(LOOK, len(steps))):
                emit_scores(n)
            for n in range(len(steps)):
                if n + LOOK < len(steps):
                    emit_scores(n + LOOK)
                emit_pv(n)
                for d_ in deferred:
                    d_[0] -= 1
                while deferred and deferred[0][0] <= 0:
                    finish(*deferred.pop(0)[1])
            while deferred:
                finish(*deferred.pop(0)[1])

        emit_casts()
        if 2 in phases:
            with ExitStack() as ps2:
                heads = []
                for h in range(NHA):
                    g = h // 4
                    heads.append((h, qaT[h], kaT[g], va[:, g * 128:(g + 1) * 128], ("a", g)))
                attention(ps2, heads, masked=False)
            P.barrier()
        if 3 in phases:
            with ExitStack() as ps3:
                heads = []
                for h in range(NHB):
                    heads.append((8 + h, qbT[h], kbT[h], vb[:, h * 128:(h + 1) * 128], ("b", h)))
                attention(ps3, heads, masked=True)
            P.barrier()

        emit_casts()
        if 4 in phases:
            with ExitStack() as ps4:
                NRING = 4
                ring = [sbt(ps4, "ring%d" % i, [128, 8192], BF16) for i in range(NRING)]
                B_ring = [Buf("ring%d" % i) for i in range(NRING)]
                mix_sb = sbt(ps4, "mix_sb", [128, 16, 512], BF16)
                B_mix = Buf("mix")
                gffn_sb = sbt(ps4, "gffn_sb", [128, 16], F32)
                gfin_sb = sbt(ps4, "gfin_sb", [128, D], F32)
                B_gffn, B_gfin = Buf("gffn"), Buf("gfin")
                P.dma("sp", gffn_sb[:], g_ffn, writes=[B_gffn])
                P.dma("sp", gfin_sb[:], g_fin, writes=[B_gfin])
                xres = [sbt(ps4, "xres%d" % i, [128, D], F32) for i in range(4)]
                B_xres = [Buf("xres%d" % i) for i in range(4)]
                h2 = [sbt(ps4, "h2_%d" % i, [128, D], BF16) for i in range(2)]
                B_h2 = [Buf("h2_%d" % i) for i in range(2)]
                h2T = sbt(ps4, "h2T", [128, 16, 512], BF16)
                B_h2T = [Buf("h2T_%d" % t) for t in range(4)]
                actT = sbt(ps4, "actT", [128, NFC, 512], BF16)
                B_act = [Buf("act%d" % f) for f in range(NFC)]
                st2 = [sbt(ps4, "st2_%d" % i, [128, 16], F32) for i in range(2)]
                B_st2 = [Buf("st2_%d" % i) for i in range(2)]
                st3 = [sbt(ps4, "st3_%d" % i, [128, 16], F32) for i in range(2)]
                B_st3 = [Buf("st3_%d" % i) for i in range(2)]
                sg = [sbt(ps4, "sg%d" % i, [128, 512], F32) for i in range(2)]
                B_sg = [Buf("sg%d" % i) for i in range(2)]

                items = []
                for tb in range(NQB):
                    for cb in range(4):
                        items.append(("wout", cb, 0))
                    for f2 in range(22):
                        items.append(("wgu", f2, 0))
                    for cb in range(4):
                        for pc in range(3):
                            items.append(("wd", cb, pc))
                rstate = {"loaded": 0, "used": 0}

                def ring_prefetch(upto):
                    while rstate["loaded"] < min(upto, len(items)):
                        n = rstate["loaded"]
                        kind, a, b = items[n]
                        s = n % NRING
                        if kind == "wout":
                            src = wout_s[a].rearrange("p h c -> p (h c)")
                        elif kind == "wgu":
                            src = wgu_s[a].rearrange("p g k c -> p (g k c)")
                        else:
                            nfp = 16 if b < 2 else NFC - 32
                            src = wd_s[a, b, :, 0:nfp, :].rearrange("p f c -> p (f c)")
                        P.dma("pool", ring[s][:, 0:src.shape[1]], src, writes=[B_ring[s]], extra=[cast_state["tok"]])
                        rstate["loaded"] += 1

                def ring_next(kind):
                    n = rstate["used"]
                    assert items[n][0] == kind, (items[n], kind)
                    ring_prefetch(n + NRING)
                    rstate["used"] += 1
                    s = n % NRING
                    return ring[s], B_ring[s]

                tp_bank4 = [pbank[4][:].bitcast(BF16), pbank[5][:].bitcast(BF16)]
                c4 = {"op": 0, "gu": 0}
                ring_prefetch(NRING - 1)
                for tb in range(NQB):
                    t0 = tb * 512
                    s2 = tb % 2
                    P.dma("sp", mix_sb[:], mixT[:, :, t0:t0 + 512].rearrange("h d t -> d h t"), writes=[B_mix])
                    for t in range(4):
                        r0 = t0 + t * 128
                        P.dma("sp", xres[t][:], x[r0:r0 + 128, :], writes=[B_xres[t]])
                    for t in range(4):
                        tt = tb * 4 + t
                        for ab in range(2):
                            P.op("dve", I("tensor_reduce",
                                out=st2[s2][:, t * 2 + ab:t * 2 + ab + 1], in_=ssq[:, tt, ab * 8:(ab + 1) * 8], axis=AX.X, op=ALU.add),
                                reads=[B_ssq], writes=[B_st2[s2]])
                    P.op("dve", I("tensor_scalar", out=st2[s2][:, 0:8], in0=st2[s2][:, 0:8], scalar1=1.0 / 1024, scalar2=EPS,
                                                          op0=ALU.mult, op1=ALU.add), writes=[B_st2[s2]])
                    P.op("act", I("activation", out=st2[s2][:, 0:8], in_=st2[s2][:, 0:8], func=AF.Sqrt), writes=[B_st2[s2]])
                    P.op("dve", I("reciprocal", out=st2[s2][:, 8:16], in_=st2[s2][:, 0:8]), writes=[B_st2[s2]])
                    for cb in range(4):
                        wt, bw = ring_next("wout")
                        wv = wt[:].rearrange("p (h c) -> p h c", h=16)
                        cs = slice(cb * 512, (cb + 1) * 512)
                        for t in range(4):
                            pa = (c4["op"] % 2) * 2
                            c4["op"] += 1
                            pbk = pa + 1
                            fa = [I("matmul", pbank[pa][:, :], lhsT=mix_sb[:, h, t * 128:(t + 1) * 128],
                                                                           rhs=wv[:, h, :], start=(h == 0), stop=(h == 7)) for h in range(8)]
                            fb = [I("matmul", pbank[pbk][:, :], lhsT=mix_sb[:, h, t * 128:(t + 1) * 128],
                                                                             rhs=wv[:, h, :], start=(h == 8), stop=(h == 15)) for h in range(8, 16)]
                            P.group("pe", fa + fb, reads=[B_mix, bw], writes=[B_bank[pa], B_bank[pbk]])
                            P.op("dve", I("scalar_tensor_tensor",
                                out=xres[t][:, cs], in0=pbank[pa][:, :], scalar=st2[s2][:, 8 + t * 2:9 + t * 2], in1=xres[t][:, cs],
                                op0=ALU.mult, op1=ALU.add), reads=[B_bank[pa], B_st2[s2]], writes=[B_xres[t]])
                            P.op("dve", I("scalar_tensor_tensor",
                                out=xres[t][:, cs], in0=pbank[pbk][:, :], scalar=st2[s2][:, 9 + t * 2:10 + t * 2], in1=xres[t][:, cs],
                                op0=ALU.mult, op1=ALU.add), reads=[B_bank[pbk], B_st2[s2]], writes=[B_xres[t]])
                    ss = st3[s2]
                    P.op("dve", I("memset", ss[:], 0.0), writes=[B_st3[s2]])
                    for t in range(4):
                        hs = t % 2
                        P.op("act", I("activation", out=h2[hs][:], in_=xres[t][:], func=AF.Square,
                                                                           accum_out=ss[:, t:t + 1]),
                             reads=[B_xres[t]], writes=[B_h2[hs], B_st3[s2]])
                        P.op("dve", I("tensor_scalar", out=ss[:, 4 + t:5 + t], in0=ss[:, t:t + 1], scalar1=1.0 / D, scalar2=EPS,
                                                                        op0=ALU.mult, op1=ALU.add), writes=[B_st3[s2]])
                        P.op("act", I("activation", out=ss[:, 4 + t:5 + t], in_=ss[:, 4 + t:5 + t], func=AF.Sqrt),
                             writes=[B_st3[s2]])
                        P.op("dve", I("reciprocal", out=ss[:, 4 + t:5 + t], in_=ss[:, 4 + t:5 + t]), writes=[B_st3[s2]])
                        P.op("dve", I("tensor_scalar", out=h2[hs][:], in0=xres[t][:], scalar1=ss[:, 4 + t:5 + t],
                                                                               scalar2=None, op0=ALU.mult),
                             reads=[B_xres[t], B_st3[s2]], writes=[B_h2[hs]])
                        for half in range(2):
                            fns = [I("transpose",
                                tp_bank4[half][:, kk * 128:(kk + 1) * 128], h2[hs][:, (half * 8 + kk) * 128:(half * 8 + kk + 1) * 128], ident[:])
                                for kk in range(8)]
                            P.group("pe", fns, reads=[B_h2[hs], B_ident], writes=[B_bank[4 + half]])
                            src = tp_bank4[half][:, :].rearrange("p (k t) -> p k t", k=8)
                            gm = gffn_sb[:, half * 8:(half + 1) * 8].unsqueeze(2).to_broadcast([128, 8, 128])
                            P.op("dve", I("tensor_tensor",
                                out=h2T[:, half * 8:(half + 1) * 8, t * 128:(t + 1) * 128], in0=src, in1=gm, op=ALU.mult),
                                reads=[B_bank[4 + half], B_gffn], writes=[B_h2T[t]])
                    for f2 in range(22):
                        wt, bw = ring_next("wgu")
                        wv = wt[:].rearrange("p (g k c) -> p g k c", g=2, k=16)
                        for j in range(2):
                            fc = f2 * 2 + j
                            pg = (c4["gu"] % 2) * 2
                            c4["gu"] += 1
                            pu = pg + 1
                            fg = [I("matmul", pbank[pg][:, :], lhsT=wv[:, 0, kc, j * 128:(j + 1) * 128],
                                                                             rhs=h2T[:, kc, :], start=(kc == 0), stop=(kc == 15)) for kc in range(16)]
                            fu = [I("matmul", pbank[pu][:, :], lhsT=wv[:, 1, kc, j * 128:(j + 1) * 128],
                                                                             rhs=h2T[:, kc, :], start=(kc == 0), stop=(kc == 15)) for kc in range(16)]
                            P.group("pe", fg, reads=B_h2T + [bw], writes=[B_bank[pg]])
                            P.group("pe", fu, reads=B_h2T + [bw], writes=[B_bank[pu]])
                            gs = fc % 2
                            P.op("act", I("activation", out=sg[gs][:], in_=pbank[pg][:, :], func=AF.Silu),
                                 reads=[B_bank[pg]], writes=[B_sg[gs]])
                            P.op("dve", I("tensor_tensor", out=actT[:, fc, :], in0=pbank[pu][:, :], in1=sg[gs][:],
                                                                                     op=ALU.mult),
                                 reads=[B_bank[pu], B_sg[gs]], writes=[B_act[fc]])
                    for cb in range(4):
                        cs = slice(cb * 512, (cb + 1) * 512)
                        for pc in range(3):
                            wt, bw = ring_next("wd")
                            wv = wt[:].rearrange("p (f c) -> p f c", f=16)
                            nf = 16 if pc < 2 else NFC - 32
                            for t in range(4):
                                fns = [I("matmul",
                                    pbank[4 + t][:, :], lhsT=actT[:, pc * 16 + f, t * 128:(t + 1) * 128], rhs=wv[:, f, :],
                                    start=(pc == 0 and f == 0), stop=(pc == 2 and f == nf - 1)) for f in range(nf)]
                                P.group("pe", fns, reads=B_act[pc * 16:pc * 16 + nf] + [bw], writes=[B_bank[4 + t]])
                        for t in range(4):
                            P.op("dve", I("tensor_tensor", out=xres[t][:, cs], in0=pbank[4 + t][:, :], in1=xres[t][:, cs],
                                                                            op=ALU.add),
                                 reads=[B_bank[4 + t]], writes=[B_xres[t]])
                    for t in range(4):
                        r0 = t0 + t * 128
                        hs = t % 2
                        P.op("act", I("activation", out=h2[hs][:], in_=xres[t][:], func=AF.Square,
                                                                           accum_out=ss[:, 8 + t:9 + t]),
                             reads=[B_xres[t]], writes=[B_h2[hs], B_st3[s2]])
                        P.op("dve", I("tensor_scalar", out=ss[:, 12 + t:13 + t], in0=ss[:, 8 + t:9 + t], scalar1=1.0 / D,
                                                                        scalar2=EPS, op0=ALU.mult, op1=ALU.add), writes=[B_st3[s2]])
                        P.op("act", I("activation", out=ss[:, 12 + t:13 + t], in_=ss[:, 12 + t:13 + t], func=AF.Sqrt),
                             writes=[B_st3[s2]])
                        P.op("dve", I("reciprocal", out=ss[:, 12 + t:13 + t], in_=ss[:, 12 + t:13 + t]), writes=[B_st3[s2]])
                        P.op("dve", I("scalar_tensor_tensor", out=xres[t][:], in0=xres[t][:], scalar=ss[:, 12 + t:13 + t],
                                                                               in1=gfin_sb[:], op0=ALU.mult, op1=ALU.mult),
                             reads=[B_gfin, B_st3[s2]], writes=[B_xres[t]])
                        P.dma("sp", out[r0:r0 + 128, :], xres[t][:], reads=[B_xres[t]])
        P.fence_stores(engs=("sp",))
        P.run()
    return nc


def _host_inputs(inputs, S):
    f32 = np.float32
    ca, sa, cb, sb_, mask = _const_tables(S)

    def swap(g):
        return np.ascontiguousarray(g.reshape(2, 2, 32)[:, ::-1, :]).reshape(128)

    gq = np.asarray(inputs["g_q_a"], f32)[0]
    gk = np.asarray(inputs["g_k_a"], f32)[0]
    g_qk = np.ascontiguousarray(np.broadcast_to(np.stack([gq, swap(gq), gk, swap(gk)])[None], (128, 4, 128)))
    g_out = np.concatenate([np.asarray(inputs["g_out_a"], f32)[0], np.asarray(inputs["g_out_b"], f32)[0]])
    shared = {
        "w_in": np.ascontiguousarray(np.asarray(inputs["w_in"], f32)[0]),
        "w_out": np.ascontiguousarray(np.asarray(inputs["w_out"], f32)[0]),
        "w_gate_up": np.ascontiguousarray(np.asarray(inputs["w_gate_up"], f32)[0]),
        "w_down": np.ascontiguousarray(np.asarray(inputs["w_down"], f32)[0]),
        "g_mix": np.ascontiguousarray(np.asarray(inputs["g_mix"], f32)[0].reshape(16, 128).T),
        "g_ffn": np.ascontiguousarray(np.asarray(inputs["g_ffn"], f32)[0].reshape(16, 128).T),
        "g_final": np.ascontiguousarray(np.broadcast_to(np.asarray(inputs["g_final"], f32)[None, :], (128, D))),
        "g_qk": g_qk,
        "g_out": np.ascontiguousarray(g_out.reshape(16, 128).T),
        "ropa_c": ca, "ropa_s": sa, "ropb_c": cb, "ropb_s": sb_, "maskb": mask,
    }
    return shared


_NC_CACHE = {}


def kernel(**inputs):
    x = np.asarray(inputs["x"], np.float32)
    B, S, _ = x.shape
    shared = _host_inputs(inputs, S)
    if S not in _NC_CACHE:
        _NC_CACHE[S] = build(S)
    nc = _NC_CACHE[S]
    in_maps = []
    for b in range(B):
        m = dict(shared)
        m["x"] = np.ascontiguousarray(x[b])
        in_maps.append(m)
    res = run_bass_kernel_spmd(nc, in_maps, core_ids=list(range(B)))
    return np.stack([np.asarray(r["out"], np.float32) for r in res.results], axis=0)
```

```python
import numpy as np
from contextlib import ExitStack
import ml_dtypes
import concourse.bass as bass
import concourse.mybir as mybir
from concourse.bass_utils import run_bass_kernel_spmd

F32 = mybir.dt.float32
BF16 = mybir.dt.bfloat16
ALU = mybir.AluOpType
AF = mybir.ActivationFunctionType
AX = mybir.AxisListType

D = 2048
HD = 128
NHA = 8
NKV = 2
NHB = 8
DFF = 5632
PROJ = 4608
EPS = 1e-6
GRID_W = 64
NFC = DFF // 128
SCALE = HD ** -0.5
NREL = 20


def I(name, *a, **k):
    return lambda e: getattr(e, name)(*a, **k)


class Buf:
    __slots__ = ("name", "w", "r", "ld", "st", "excl")

    def __init__(self, name, excl=False):
        self.name = name
        self.excl = excl
        self.w = None
        self.r = {}
        self.ld = None
        self.st = None


class Prog:
    ENG = ("sp", "act", "dve", "pool", "pe")

    def __init__(self, nc, es):
        self.nc = nc
        self.es = es
        self.q = {e: [] for e in self.ENG}
        self.sem = {e: es.enter_context(nc.semaphore("prog_" + e)) for e in self.ENG}
        self.cnt = {e: 0 for e in self.ENG}
        self.waited = {e: {} for e in self.ENG}
        self.dcnt = {}
        self.nsem = 0
        self.stores = {}
        self.alldma = {}

    def new_sem(self, name):
        s = self.es.enter_context(self.nc.semaphore(name + "_%d" % self.nsem))
        self.nsem += 1
        self.dcnt[id(s)] = 0
        return s

    def wait(self, eng, toks):
        w = self.waited[eng]
        for t in toks:
            if t is None:
                continue
            sem, val = t
            if w.get(id(sem), 0) >= val:
                continue
            w[id(sem)] = val
            self.q[eng].append(lambda e, sem=sem, val=val: e.wait_ge(sem, val))

    def _deps(self, reads, writes, extra):
        toks = list(extra)
        for b in reads:
            toks.append(b.w)
            if b.excl:
                toks.extend(b.r.values())
        for b in writes:
            toks.append(b.w)
            toks.extend(b.r.values())
        return toks

    def _commit(self, tok, reads, writes):
        for b in reads:
            b.r[id(tok[0])] = tok
        for b in writes:
            b.w = tok
            b.r = {}

    def op(self, eng, fn, reads=(), writes=(), extra=()):
        return self.group(eng, [fn], reads, writes, extra)

    def group(self, eng, fns, reads=(), writes=(), extra=()):
        self.wait(eng, self._deps(reads, writes, extra))
        self.cnt[eng] += 1
        sem = self.sem[eng]
        tok = (sem, self.cnt[eng])
        for fn in fns[:-1]:
            self.q[eng].append(lambda e, fn=fn: fn(e))
        self.q[eng].append(lambda e, fn=fns[-1], sem=sem: fn(e).then_inc(sem, 1))
        self._commit(tok, reads, writes)
        return tok

    def dma(self, eng, out, in_, reads=(), writes=(), extra=(), sem=None):
        deps = self._deps(reads, writes, extra)
        if writes and writes[0].ld is not None:
            deps = [t for t in deps if t is None or t[0] is not writes[0].ld]
        self.wait(eng, deps)
        if sem is None:
            if writes:
                b = writes[0]
                if b.ld is None:
                    b.ld = self.new_sem("ld_" + b.name)
                sem = b.ld
            else:
                b = reads[0]
                if b.st is None:
                    b.st = self.new_sem("st_" + b.name)
                sem = b.st
        self.dcnt[id(sem)] += 16
        tok = (sem, self.dcnt[id(sem)])
        self.q[eng].append(lambda e, out=out, in_=in_, sem=sem: e.dma_start(out=out, in_=in_).then_inc(sem, 16))
        self._commit(tok, reads, writes)
        if reads and not writes:
            self.stores[id(sem)] = tok
        self.alldma[id(sem)] = tok
        return tok

    def barrier(self):
        toks = [(self.sem[e], self.cnt[e]) for e in self.ENG if self.cnt[e] > 0] + list(self.alldma.values())
        for e in self.ENG:
            self.wait(e, toks)

    def fence_stores(self, engs=("sp", "pool", "act")):
        toks = list(self.stores.values())
        for e in engs:
            self.wait(e, toks)

    def run(self):
        nc = self.nc
        with nc.Block() as block:
            @block.sync
            def _(e):
                for f in self.q["sp"]:
                    f(e)

            @block.scalar
            def _(e):
                for f in self.q["act"]:
                    f(e)

            @block.vector
            def _(e):
                for f in self.q["dve"]:
                    f(e)

            @block.gpsimd
            def _(e):
                for f in self.q["pool"]:
                    f(e)

            @block.tensor
            def _(e):
                for f in self.q["pe"]:
                    f(e)


def _const_tables(S):
    t = np.arange(S)
    half = HD // 2
    inv_a = (10000.0 ** (-(np.arange(0, half, 2, dtype=np.float32) / half))).astype(np.float32)
    row = (t // GRID_W).astype(np.float32)
    col = (t % GRID_W).astype(np.float32)
    ang_r = row[:, None] * inv_a[None, :]
    ang_c = col[:, None] * inv_a[None, :]
    ca = np.zeros((S, 2, 2, 32), np.float32)
    sa = np.zeros((S, 2, 2, 32), np.float32)
    for a, ang in enumerate((ang_r, ang_c)):
        c = np.cos(ang.astype(np.float32)).astype(np.float32)
        s = np.sin(ang.astype(np.float32)).astype(np.float32)
        ca[:, a, 0] = c
        ca[:, a, 1] = c
        sa[:, a, 0] = -s
        sa[:, a, 1] = s
    pr = HD // 4
    inv_b = (500000.0 ** (-(np.arange(0, pr, 2, dtype=np.float32) / pr))).astype(np.float32)
    ang_b = t.astype(np.float32)[:, None] * inv_b[None, :]
    cb = np.cos(ang_b).astype(np.float32)
    sb_ = np.sin(ang_b).astype(np.float32)
    cbt = np.concatenate([cb, cb], axis=1)
    sbt = np.concatenate([-sb_, sb_], axis=1)
    k = np.arange(128)[:, None]
    q = np.arange(512)[None, :]
    mask = np.zeros((NREL, 128, 512), np.float32)
    for r in range(NREL):
        d = 128 * (r - 8) + k - q
        ad = np.abs(d)
        mask[r] = (ad <= 64).astype(np.float32) + ((d % 4 == 0) & (ad <= 256)) + ((d % 16 == 0) & (ad <= 1024))
    return (ca.reshape(S, 128), sa.reshape(S, 128), cbt.astype(np.float32), sbt.astype(np.float32),
            mask.astype(ml_dtypes.bfloat16))


def build(S=4096, debug=False, phases=(1, 2, 3, 4)):
    NT = S // 128
    NQB = S // 512
    nc = bass.Bass("TRN2", target_bir_lowering=False)

    def din(name, shape, dt=F32):
        return nc.dram_tensor(name, list(shape), dt, kind="ExternalInput").ap()

    def dscr(name, shape, dt):
        if debug:
            return nc.dram_tensor(name, list(shape), dt, kind="ExternalOutput").ap()
        return nc.dram_tensor(name, list(shape), dt).ap()

    x = din("x", [S, D])
    w_in = din("w_in", [D, PROJ])
    w_out = din("w_out", [D, D])
    w_gu = din("w_gate_up", [D, 2 * DFF])
    w_dn = din("w_down", [DFF, D])
    g_mix = din("g_mix", [128, 16])
    g_ffn = din("g_ffn", [128, 16])
    g_fin = din("g_final", [128, D])
    g_qk = din("g_qk", [128, 4, 128])
    g_out = din("g_out", [128, 16])
    ropa_c = din("ropa_c", [S, 128])
    ropa_s = din("ropa_s", [S, 128])
    ropb_c = din("ropb_c", [S, 32])
    ropb_s = din("ropb_s", [S, 32])
    maskb = din("maskb", [NREL, 128, 512], BF16)
    out = nc.dram_tensor("out", [S, D], F32, kind="ExternalOutput").ap()

    qaT = dscr("qaT", [NHA, 128, S], BF16)
    kaT = dscr("kaT", [NKV, 128, S], BF16)
    va = dscr("va", [S, NKV * 128], BF16)
    qbT = dscr("qbT", [NHB, 128, S], BF16)
    kbT = dscr("kbT", [NHB, 128, S], BF16)
    vb = dscr("vb", [S, NHB * 128], BF16)
    mixT = dscr("mixT", [16, 128, S], BF16)
    x1d = dscr("x1d", [S, D], F32)
    wout_s = dscr("wout_s", [4, 128, 16, 512], BF16)
    wgu_s = dscr("wgu_s", [22, 128, 2, 16, 256], BF16)
    wd_s = dscr("wd_s", [4, 3, 128, 16, 512], BF16)

    with ExitStack() as es:
        P = Prog(nc, es)

        def sbt(stack, name, shape, dt):
            return stack.enter_context(nc.sbuf_tensor(name, list(shape), dt))

        def pst(stack, name, shape, dt):
            return stack.enter_context(nc.psum_tensor(name, list(shape), dt))

        ident = sbt(es, "ident", [128, 128], BF16)
        ones_bf = sbt(es, "ones_bf", [128, 128], BF16)
        ones_f = sbt(es, "ones_f", [128, 1], F32)
        gout_sb = sbt(es, "gout_sb", [128, 16], F32)
        ssq = sbt(es, "ssq", [128, NT, 16], F32)
        B_ident, B_ones, B_onesf, B_gout, B_ssq = Buf("ident"), Buf("ones"), Buf("onesf"), Buf("gout"), Buf("ssq")
        P.op("pool", I("memset", ident[:], 1.0), writes=[B_ident])
        P.op("pool", I("affine_select", out=ident[:], in_=ident[:], pattern=[[-1, 128]], compare_op=ALU.is_equal,
                                                fill=0.0, base=0, channel_multiplier=1), writes=[B_ident])
        P.op("pool", I("memset", ones_bf[:], 1.0), writes=[B_ones])
        P.op("pool", I("memset", ones_f[:], 1.0), writes=[B_onesf])

        P.dma("sp", gout_sb[:], g_out, writes=[B_gout])

        pbank = [pst(es, "pbank%d" % i, [128, 512], F32) for i in range(8)]
        B_bank = [Buf("bank%d" % i, excl=True) for i in range(8)]

        cast_sem = P.new_sem("cast")
        cast_state = {"tok": None, "done": False}

        cast_list = []
        if 4 in phases:
            for cb in range(4):
                cast_list.append((wout_s[cb], w_out[:, cb * 512:(cb + 1) * 512].rearrange("(h p) c -> p h c", p=128)))
            for f2 in range(22):
                for gu in range(2):
                    c0 = gu * DFF + f2 * 256
                    cast_list.append((wgu_s[f2, :, gu], w_gu[:, c0:c0 + 256].rearrange("(kc p) c -> p kc c", p=128)))
            for cb in range(4):
                for pc in range(3):
                    n = 16 if pc < 2 else NFC - 32
                    r0 = pc * 16 * 128
                    cast_list.append((wd_s[cb, pc, :, 0:n, :],
                                      w_dn[r0:r0 + n * 128, cb * 512:(cb + 1) * 512].rearrange("(fc p) c -> p fc c", p=128)))

        def cast_one():
            if cast_list:
                dst, src = cast_list.pop(0)
                cast_state["tok"] = P.dma("pool", dst, src, sem=cast_sem)

        def emit_casts():
            while cast_list:
                cast_one()

        if 1 in phases:
            with ExitStack() as ps1:
                w_sb = sbt(ps1, "w_sb", [128, 16, PROJ], BF16)
                B_w = Buf("w_sb")
                for kq in range(4):
                    P.dma("pool", w_sb[:, kq * 4:(kq + 1) * 4, :],
                          w_in[kq * 512:(kq + 1) * 512, :].rearrange("(kc p) c -> p kc c", p=128), writes=[B_w])
                gmix_sb = sbt(ps1, "gmix_sb", [128, 16], F32)
                gqk_sb = sbt(ps1, "gqk_sb", [128, 4, 128], F32)
                B_gmix, B_gqk = Buf("gmix"), Buf("gqk")
                P.dma("sp", gmix_sb[:], g_mix, writes=[B_gmix])
                P.dma("sp", gqk_sb[:], g_qk, writes=[B_gqk])

                xb = [sbt(ps1, "xb%d" % i, [128, D], F32) for i in range(2)]
                B_x = [Buf("xb%d" % i) for i in range(2)]
                tabs = [sbt(ps1, "tabs%d" % i, [128, 320], F32) for i in range(2)]
                B_tabs = [Buf("tabs%d" % i) for i in range(2)]
                dtab = [sbt(ps1, "dtab%d" % i, [128, 4 * 128], F32) for i in range(2)]
                B_dtab = [Buf("dtab%d" % i) for i in range(2)]
                stat = [sbt(ps1, "stat%d" % i, [128, 8], F32) for i in range(2)]
                B_stat = [Buf("stat%d" % i) for i in range(2)]
                hb = [sbt(ps1, "hb0", [128, D], BF16)] * 2
                B_h = [Buf("hb0")] * 2
                neghalf = sbt(ps1, "neghalf", [128, 4], F32)
                B_neghalf = Buf("neghalf")
                P.op("pool", I("memset", neghalf[:], -0.5), writes=[B_neghalf])
                hT = [sbt(ps1, "hT%d" % i, [128, 16, 128], BF16) for i in range(2)]
                B_hT = [Buf("hT%d" % i) for i in range(2)]
                NSC = 2
                scr1 = [sbt(ps1, "scr1_%d" % i, [128, 512], F32) for i in range(NSC)]
                scr2 = [sbt(ps1, "scr2_%d" % i, [128, 512], F32) for i in range(NSC)]
                scr3 = [sbt(ps1, "scr3_%d" % i, [128, 512], F32) for i in range(NSC)]
                B_scr1 = [Buf("scr1_%d" % i) for i in range(NSC)]
                B_scr2 = [Buf("scr2_%d" % i) for i in range(NSC)]
                B_scr3 = [Buf("scr3_%d" % i) for i in range(NSC)]
                st4 = [sbt(ps1, "st4_%d" % i, [128, 8], F32) for i in range(NSC)]
                B_st4 = [Buf("st4_%d" % i) for i in range(NSC)]
                NSTG = 6
                stg = [sbt(ps1, "stg%d" % i, [128, 512], BF16) for i in range(NSTG)]
                B_stg = [Buf("stg%d" % i) for i in range(NSTG)]
                NTS = 3
                tst = [sbt(ps1, "tst%d" % i, [128, 4, 128], BF16) for i in range(NTS)]
                B_tst = [Buf("tst%d" % i) for i in range(NTS)]
                NVS = 2
                vst = [sbt(ps1, "vst%d" % i, [128, 512], BF16) for i in range(NVS)]
                B_vst = [Buf("vst%d" % i) for i in range(NVS)]

                tp_bank = [pbank[0][:].bitcast(BF16), pbank[1][:].bitcast(BF16)]

                def load_tile(i):
                    s = i % 2
                    r0 = i * 128
                    P.dma("sp", xb[s][:], x[r0:r0 + 128, :], writes=[B_x[s]])
                    P.dma("sp", tabs[s][:, 0:128], ropa_c[r0:r0 + 128, :], writes=[B_tabs[s]])
                    P.dma("sp", tabs[s][:, 128:256], ropa_s[r0:r0 + 128, :], writes=[B_tabs[s]])
                    P.dma("sp", tabs[s][:, 256:288], ropb_c[r0:r0 + 128, :], writes=[B_tabs[s]])
                    P.dma("sp", tabs[s][:, 288:320], ropb_s[r0:r0 + 128, :], writes=[B_tabs[s]])

                def norm_tile(i):
                    s = i % 2
                    P.op("dve", I("memset", stat[s][:], 0.0), writes=[B_stat[s]])
                    P.op("act", I("activation", out=hb[s][:], in_=xb[s][:], func=AF.Square, accum_out=stat[s][:, 0:1]),
                         reads=[B_x[s]], writes=[B_h[s], B_stat[s]])
                    P.op("dve", I("tensor_scalar", out=stat[s][:, 1:2], in0=stat[s][:, 0:1], scalar1=1.0 / D, scalar2=EPS,
                                                          op0=ALU.mult, op1=ALU.add), writes=[B_stat[s]])
                    P.op("act", I("activation", out=stat[s][:, 2:3], in_=stat[s][:, 1:2], func=AF.Sqrt), writes=[B_stat[s]])
                    P.op("dve", I("reciprocal", out=stat[s][:, 3:4], in_=stat[s][:, 2:3]), writes=[B_stat[s]])
                    P.op("dve", I("tensor_scalar", out=hb[s][:], in0=xb[s][:], scalar1=stat[s][:, 3:4], scalar2=None,
                                                          op0=ALU.mult), reads=[B_x[s], B_stat[s]], writes=[B_h[s]])
                    t_, d_ = tabs[s], dtab[s]
                    P.op("pool", I("tensor_tensor", out=d_[:, 0:128], in0=t_[:, 0:128], in1=gqk_sb[:, 0, :], op=ALU.mult),
                         reads=[B_tabs[s], B_gqk], writes=[B_dtab[s]])
                    P.op("pool", I("tensor_tensor", out=d_[:, 128:256], in0=t_[:, 128:256], in1=gqk_sb[:, 1, :], op=ALU.mult),
                         reads=[B_tabs[s], B_gqk], writes=[B_dtab[s]])
                    P.op("pool", I("tensor_tensor", out=d_[:, 256:384], in0=t_[:, 0:128], in1=gqk_sb[:, 2, :], op=ALU.mult),
                         reads=[B_tabs[s], B_gqk], writes=[B_dtab[s]])
                    P.op("pool", I("tensor_tensor", out=d_[:, 384:512], in0=t_[:, 128:256], in1=gqk_sb[:, 3, :], op=ALU.mult),
                         reads=[B_tabs[s], B_gqk], writes=[B_dtab[s]])

                def transp_h(i):
                    s = i % 2
                    for half in range(2):
                        fns = []
                        for kk in range(8):
                            kc = half * 8 + kk
                            fns.append(I("transpose",
                                tp_bank[half][:, kk * 128:(kk + 1) * 128], hb[s][:, kc * 128:(kc + 1) * 128], ident[:]))
                        P.group("pe", fns, reads=[B_h[s], B_ident], writes=[B_bank[half]])
                        src = tp_bank[half][:, :].rearrange("p (k t) -> p k t", k=8)
                        gm = gmix_sb[:, half * 8:(half + 1) * 8].unsqueeze(2).to_broadcast([128, 8, 128])
                        P.op("dve", I("tensor_tensor",
                            out=hT[s][:, half * 8:(half + 1) * 8, :], in0=src, in1=gm, op=ALU.mult),
                            reads=[B_bank[half], B_gmix], writes=[B_hT[s]])

                cnt = {"scr": 0, "stg": 0, "tst": 0, "vst": 0, "pj": 0, "tq": 0}
                pending = []

                def rope_norm_block(pb_ap, nh, t1off, rstd_mode, dst_slot):
                    k = cnt["scr"] % NSC
                    cnt["scr"] += 1
                    s_ = cur["s"]
                    bank = cur["bank"]
                    d_ = dtab[s_]
                    W = nh * 128
                    P.op("act", I("activation", out=scr2[k][:, 0:W], in_=pb_ap, func=AF.Copy), reads=[bank], writes=[B_scr2[k]])
                    P.op("pool", I("memset", st4[k][:], 0.0), writes=[B_st4[k]])
                    for h in range(nh):
                        P.op("act", I("activation", out=scr1[k][:, h * 128:(h + 1) * 128], in_=scr2[k][:, h * 128:(h + 1) * 128], func=AF.Square,
                                      accum_out=st4[k][:, h:h + 1]), reads=[B_scr2[k]], writes=[B_scr1[k], B_st4[k]])
                    if rstd_mode == "q":
                        P.op("pool", I("tensor_scalar", out=st4[k][:, 0:nh], in0=st4[k][:, 0:nh], scalar1=1.0, scalar2=128 * EPS,
                                       op0=ALU.mult, op1=ALU.add), writes=[B_st4[k]])
                    else:
                        P.op("pool", I("tensor_scalar", out=st4[k][:, 0:nh], in0=st4[k][:, 0:nh], scalar1=1.0 / 128, scalar2=EPS,
                                       op0=ALU.mult, op1=ALU.add), writes=[B_st4[k]])
                    P.op("pool", I("tensor_tensor", out=st4[k][:, 4:4 + nh], in0=st4[k][:, 0:nh], in1=neghalf[:, 0:nh], op=ALU.pow),
                         reads=[B_neghalf], writes=[B_st4[k]])
                    xs = scr2[k][:, 0:W]
                    T1 = d_[:, t1off:t1off + 128].unsqueeze(1).to_broadcast([128, nh, 128])
                    P.op("dve", I("tensor_tensor", out=scr1[k][:, 0:W].rearrange("p (h d) -> p h d", h=nh),
                                  in0=xs.rearrange("p (h d) -> p h d", h=nh), in1=T1, op=ALU.mult),
                         reads=[B_scr2[k], B_dtab[s_]], writes=[B_scr1[k]])
                    x5 = xs.rearrange("p (h a f j) -> p h a f j", h=nh, a=2, f=2)
                    o5 = scr3[k][:, 0:W].rearrange("p (h a f j) -> p h a f j", h=nh, a=2, f=2)
                    T2 = d_[:, t1off + 128:t1off + 256].rearrange("p (a f j) -> p a f j", a=2, f=2)
                    for f in range(2):
                        tb = T2[:, :, f, :].unsqueeze(1).to_broadcast([128, nh, 2, 32])
                        P.op("dve", I("tensor_tensor", out=o5[:, :, :, f, :], in0=x5[:, :, :, 1 - f, :], in1=tb, op=ALU.mult),
                             reads=[B_scr2[k], B_dtab[s_]], writes=[B_scr3[k]])
                    P.op("dve", I("tensor_tensor", out=scr1[k][:, 0:W], in0=scr1[k][:, 0:W], in1=scr3[k][:, 0:W], op=ALU.add),
                         reads=[B_scr3[k]], writes=[B_scr1[k]])
                    rb = st4[k][:, 4:4 + nh].unsqueeze(2).to_broadcast([128, nh, 128])
                    P.op("dve", I("tensor_tensor", out=stg[dst_slot][:, 0:W].rearrange("p (h d) -> p h d", h=nh),
                                  in0=scr1[k][:, 0:W].rearrange("p (h d) -> p h d", h=nh), in1=rb, op=ALU.mult),
                         reads=[B_scr1[k], B_st4[k]], writes=[B_stg[dst_slot]])

                def rope_part_block(pb_ap, scale, coff, dst_slot):
                    k = cnt["scr"] % NSC
                    cnt["scr"] += 1
                    s_ = cur["s"]
                    bank = cur["bank"]
                    P.op("act", I("activation", out=stg[dst_slot][:], in_=pb_ap, func=AF.Copy, scale=float(scale)),
                         reads=[bank], writes=[B_stg[dst_slot]])
                    pb3 = pb_ap.rearrange("p (h d) -> p h d", h=4)
                    xr = scr3[k][:, 0:128].rearrange("p (h j) -> p h j", h=4)
                    P.op("act", I("activation", out=xr, in_=pb3[:, :, 0:32], func=AF.Copy, scale=float(scale)),
                         reads=[bank], writes=[B_scr3[k]])
                    ctab = tabs[s_][:, 256:288]
                    stab = tabs[s_][:, 288:320]
                    tb_ = B_tabs[s_]
                    r1 = scr1[k][:, 0:128].rearrange("p (h j) -> p h j", h=4)
                    r2 = scr2[k][:, 0:128].rearrange("p (h j) -> p h j", h=4)
                    P.op("dve", I("tensor_tensor", out=r1, in0=xr, in1=ctab.unsqueeze(1).to_broadcast([128, 4, 32]), op=ALU.mult),
                         reads=[B_scr3[k], tb_], writes=[B_scr1[k]])
                    for f in range(2):
                        P.op("dve", I("tensor_tensor", out=r2[:, :, f * 16:(f + 1) * 16], in0=xr[:, :, (1 - f) * 16:(2 - f) * 16],
                                      in1=stab[:, f * 16:(f + 1) * 16].unsqueeze(1).to_broadcast([128, 4, 16]), op=ALU.mult),
                             reads=[B_scr3[k], tb_], writes=[B_scr2[k]])
                    P.op("dve", I("tensor_tensor", out=stg[dst_slot][:].rearrange("p (h d) -> p h d", h=4)[:, :, 0:32], in0=r1, in1=r2, op=ALU.add),
                         reads=[B_scr1[k], B_scr2[k]], writes=[B_stg[dst_slot]])

                def out_transposes(slot, nh, dst_fn):
                    tb = 5 + cnt["tq"] % 2
                    cnt["tq"] += 1
                    tpv = pbank[tb][:].bitcast(BF16)
                    fns = [I("transpose", tpv[:, h * 128:(h + 1) * 128], stg[slot][:, h * 128:(h + 1) * 128], ident[:])
                           for h in range(nh)]
                    P.group("pe", fns, reads=[B_stg[slot], B_ident], writes=[B_bank[tb]])
                    ts = cnt["tst"] % NTS
                    cnt["tst"] += 1
                    P.op("act", I("activation", out=tst[ts][:, 0:nh, :], in_=tpv[:, 0:nh * 128].rearrange("p (h t) -> p h t", h=nh),
                                                       func=AF.Copy), reads=[B_bank[tb]], writes=[B_tst[ts]])
                    P.dma("sp", dst_fn(), tst[ts][:, 0:nh, :], reads=[B_tst[ts]])

                cur = {}

                def proj_tile(i):
                    s = i % 2
                    r0 = i * 128
                    cur["s"] = s
                    for cb in range(9):
                        bk = 2 + cnt["pj"] % 3
                        cnt["pj"] += 1
                        cur["bank"] = B_bank[bk]
                        pb_ap = pbank[bk][:, :]
                        fns = [I("matmul", pbank[bk][:, :], lhsT=hT[s][:, kc, :],
                                                                      rhs=w_sb[:, kc, cb * 512:(cb + 1) * 512],
                                                                      start=(kc == 0), stop=(kc == 15)) for kc in range(16)]
                        P.group("pe", fns, reads=[B_hT[s], B_w], writes=[B_bank[bk]])
                        if cb in (0, 1):
                            sl = cnt["stg"] % NSTG
                            cnt["stg"] += 1
                            rope_norm_block(pb_ap, 4, 0, "q", sl)
                            pending.append((sl, 4, (lambda cb=cb, r0=r0: qaT[cb * 4:(cb + 1) * 4, :, r0:r0 + 128].rearrange("h d t -> d h t"))))
                        elif cb == 2:
                            sl = cnt["stg"] % NSTG
                            cnt["stg"] += 1
                            rope_norm_block(pbank[bk][:, 0:256], 2, 256, "k", sl)
                            pending.append((sl, 2, (lambda r0=r0: kaT[:, :, r0:r0 + 128].rearrange("h d t -> d h t"))))
                            vs = cnt["vst"] % NVS
                            cnt["vst"] += 1
                            P.op("act", I("activation", out=vst[vs][:, 0:256], in_=pbank[bk][:, 256:512], func=AF.Copy),
                                 reads=[B_bank[bk]], writes=[B_vst[vs]])
                            P.dma("sp", va[r0:r0 + 128, :], vst[vs][:, 0:256], reads=[B_vst[vs]])
                        elif cb in (3, 4, 5, 6):
                            sl = cnt["stg"] % NSTG
                            cnt["stg"] += 1
                            isq = cb in (3, 4)
                            rope_part_block(pb_ap, SCALE if isq else 1.0, 0, sl)
                            hb0 = (cb - 3) * 4 if isq else (cb - 5) * 4
                            tgt = qbT if isq else kbT
                            pending.append((sl, 4, (lambda tgt=tgt, hb0=hb0, r0=r0: tgt[hb0:hb0 + 4, :, r0:r0 + 128].rearrange("h d t -> d h t"))))
                        else:
                            vs = cnt["vst"] % NVS
                            cnt["vst"] += 1
                            P.op("act", I("activation", out=vst[vs][:], in_=pbank[bk][:, :], func=AF.Copy),
                                 reads=[B_bank[bk]], writes=[B_vst[vs]])
                            c0 = (cb - 7) * 512
                            P.dma("sp", vb[r0:r0 + 128, c0:c0 + 512], vst[vs][:], reads=[B_vst[vs]])
                        while len(pending) > 4:
                            sl_, nh_, fn_ = pending.pop(0)
                            out_transposes(sl_, nh_, fn_)

                load_tile(0)
                norm_tile(0)
                transp_h(0)
                for i in range(NT):
                    if i + 1 < NT:
                        load_tile(i + 1)
                        norm_tile(i + 1)
                        transp_h(i + 1)
                    proj_tile(i)
                while pending:
                    sl_, nh_, fn_ = pending.pop(0)
                    out_transposes(sl_, nh_, fn_)
            P.barrier()

        def attention(stack, heads, masked):
            pfx = "m" if masked else "u"
            _sbt = sbt

            def sbt2(stack_, name, shape, dt):
                return _sbt(stack_, pfx + name, shape, dt)
            kt_sb = [sbt2(stack, "kt%d" % i, [128, S], BF16) for i in range(2)]
            v_sb = [sbt2(stack, "v%d" % i, [128, NT, 128], BF16) for i in range(2)]
            B_kv = [Buf("kv%d" % i) for i in range(2)]
            NQ = 3
            qt_sb = [sbt2(stack, "qt%d" % i, [128, 512], BF16) for i in range(NQ)]
            B_qt = [Buf("qt%d" % i) for i in range(NQ)]
            NP = 8
            pt_sb = [sbt2(stack, "pt%d" % i, [128, 512], BF16) for i in range(NP)]
            B_pt = [Buf("pt%d" % i) for i in range(NP)]
            if masked:
                pe_sb = [sbt2(stack, "pe%d" % i, [128, 512], BF16) for i in range(NP)]
                B_pe = [Buf("pe%d" % i) for i in range(NP)]
                mask_sb = sbt2(stack, "mask_sb", [128, NREL, 512], BF16)
                B_mask = Buf("mask")
                P.dma("sp", mask_sb[:], maskb.rearrange("r k q -> k r q"), writes=[B_mask])
            if not masked:
                accD = [sbt2(stack, "accD%d" % i, [128, 512], F32) for i in range(2)]
                accP = [sbt2(stack, "accP%d" % i, [128, 512], F32) for i in range(2)]
                hi_sb = [sbt2(stack, "hi%d" % i, [128, 512], BF16) for i in range(2)]
                lo_sb = [sbt2(stack, "lo%d" % i, [128, 512], BF16) for i in range(2)]
                B_accD = [Buf("accD%d" % i) for i in range(2)]
                B_accP = [Buf("accP%d" % i) for i in range(2)]
                B_hi = [Buf("hi%d" % i) for i in range(2)]
                B_lo = [Buf("lo%d" % i) for i in range(2)]
            rd_sb = [sbt2(stack, "rd%d" % i, [128, 512], F32) for i in range(2)]
            o_sb = [sbt2(stack, "o%d" % i, [128, 512], F32) for i in range(2)]
            sq_sb = [sbt2(stack, "sq%d" % i, [128, 512], F32) for i in range(2)]
            ost = [sbt2(stack, "ost%d" % i, [128, 512], BF16) for i in range(2)]
            B_rd = [Buf("rd%d" % i) for i in range(2)]
            B_o = [Buf("o%d" % i) for i in range(2)]
            B_sq = [Buf("sq%d" % i) for i in range(2)]
            B_ost = [Buf("ost%d" % i) for i in range(2)]
            steps = []
            kvslot = {}
            nkv = 0
            for (mh, qT_ap, kT_ap, v_ap, kvkey) in heads:
                for qb in range(NQB):
                    if masked:
                        kcs = [kc for kc in range(qb * 4 - 8, qb * 4 + 12) if 0 <= kc < NT]
                    else:
                        kcs = list(range(NT))
                    for j, kc in enumerate(kcs):
                        steps.append(dict(mh=mh, qT=qT_ap, kT=kT_ap, v=v_ap, kvkey=kvkey, qb=qb, kc=kc,
                                          first=(j == 0), last=(j == len(kcs) - 1)))
            state = {"kv": None, "kvn": 0, "qn": 0, "blk": 0}
            cur_kv = {}
            cur_q = {}

            kv_slot = {}
            q_slot = {}
            first_of_block = [i for i, st_ in enumerate(steps) if st_["first"]]
            block_of = {}
            for bi, i0 in enumerate(first_of_block):
                block_of[i0] = bi

            def ensure_loaded(bi):
                if bi >= len(first_of_block):
                    return
                st_ = steps[first_of_block[bi]]
                if st_["kvkey"] not in kv_slot:
                    s = state["kvn"] % 2
                    state["kvn"] += 1
                    kv_slot[st_["kvkey"]] = s
                    P.dma("sp", kt_sb[s][:], st_["kT"], writes=[B_kv[s]])
                    P.dma("sp", v_sb[s][:], st_["v"].rearrange("(c p) d -> p c d", p=128), writes=[B_kv[s]])
                key = (st_["mh"], st_["qb"])
                if key not in q_slot:
                    s = state["qn"] % NQ
                    state["qn"] += 1
                    q_slot[key] = s
                    P.dma("sp", qt_sb[s][:], st_["qT"][:, st_["qb"] * 512:(st_["qb"] + 1) * 512], writes=[B_qt[s]])

            def issue_loads(st_, n):
                if st_["first"]:
                    ensure_loaded(block_of[n])
                    ensure_loaded(block_of[n] + 1)
                st_["kvs"] = kv_slot[st_["kvkey"]]
                st_["qs"] = q_slot[(st_["mh"], st_["qb"])]

            def emit_scores(n):
                st_ = steps[n]
                issue_loads(st_, n)
                bk = (0, 1, 2, 7)[n % 4]
                ps_ = n % NP
                kvs, qs, kc = st_["kvs"], st_["qs"], st_["kc"]
                P.op("pe", I("matmul", pbank[bk][:, :], lhsT=kt_sb[kvs][:, kc * 128:(kc + 1) * 128], rhs=qt_sb[qs][:],
                                              start=True, stop=True),
                     reads=[B_kv[kvs], B_qt[qs]], writes=[B_bank[bk]])
                if masked:
                    rel = kc - st_["qb"] * 4 + 8
                    P.op("act", I("activation", out=pe_sb[ps_][:], in_=pbank[bk][:, :], func=AF.Exp),
                         reads=[B_bank[bk]], writes=[B_pe[ps_]])
                    P.op("dve",
                         I("tensor_tensor", out=pt_sb[ps_][:], in0=pe_sb[ps_][:], in1=mask_sb[:, rel, :], op=ALU.mult),
                         reads=[B_pe[ps_], B_mask], writes=[B_pt[ps_]])
                else:
                    P.op("act", I("activation", out=pt_sb[ps_][:], in_=pbank[bk][:, :], func=AF.Exp),
                         reads=[B_bank[bk]], writes=[B_pt[ps_]])

            deferred = []
            deferred2 = []

            def finish(b2, bo, bd, mh, qb):
                if not masked:
                    fd = [I("matmul", pbank[bd][:, :], lhsT=ones_bf[:], rhs=hi_sb[b2][:], start=False, stop=False),
                          I("matmul", pbank[bd][:, :], lhsT=ones_bf[:], rhs=lo_sb[b2][:], start=False, stop=True)]
                    P.group("pe", fd, reads=[B_hi[b2], B_lo[b2], B_ones], writes=[B_bank[bd]])
                P.op("act", I("activation", out=rd_sb[b2][:], in_=pbank[bd][:, :], func=AF.Ln), reads=[B_bank[bd]], writes=[B_rd[b2]])
                P.op("act", I("activation", out=rd_sb[b2][:], in_=rd_sb[b2][:], func=AF.Exp, scale=-1.0), writes=[B_rd[b2]])
                P.op("dve", I("tensor_tensor", out=o_sb[b2][:], in0=pbank[bo][:, :], in1=rd_sb[b2][:], op=ALU.mult),
                     reads=[B_bank[bo], B_rd[b2]], writes=[B_o[b2]])
                fe = "pool" if masked else "dve"
                P.op(fe, I("tensor_tensor", out=sq_sb[b2][:], in0=o_sb[b2][:], in1=o_sb[b2][:], op=ALU.mult),
                     reads=[B_o[b2]], writes=[B_sq[b2]])
                P.op(fe, I("tensor_tensor", out=ost[b2][:], in0=o_sb[b2][:], in1=gout_sb[:, mh:mh + 1].to_broadcast([128, 512]), op=ALU.mult),
                     reads=[B_o[b2], B_gout], writes=[B_ost[b2]])
                P.dma("sp", mixT[mh, :, qb * 512:(qb + 1) * 512], ost[b2][:], reads=[B_ost[b2]])
                deferred2.append([6, (b2, bd, mh, qb)])

            def finish2(b2, bd, mh, qb):
                fns2 = [I("matmul", pbank[bd][:, t:t + 1], lhsT=sq_sb[b2][:, t * 128:(t + 1) * 128], rhs=ones_f[:],
                          start=True, stop=True) for t in range(4)]
                P.group("pe", fns2, reads=[B_sq[b2], B_onesf], writes=[B_bank[bd]])
                P.op("dve", I("tensor_copy", out=ssq[:, qb * 4:(qb + 1) * 4, mh], in_=pbank[bd][:, 0:4]),
                     reads=[B_bank[bd]], writes=[B_ssq])

            def emit_pv(n):
                st_ = steps[n]
                ps_ = n % NP
                if st_["first"]:
                    st_["blk"] = state["blk"]
                    st_["j"] = 0
                    state["blk"] += 1
                else:
                    st_["blk"] = steps[n - 1]["blk"]
                    st_["j"] = steps[n - 1]["j"] + 1
                b2 = st_["blk"] % 2
                bo, bd = 3 + b2, 5 + b2
                kvs, kc, j = st_["kvs"], st_["kc"], st_["j"]
                if masked:
                    fns = [I("matmul", pbank[bo][:, :], lhsT=v_sb[kvs][:, kc, :], rhs=pt_sb[ps_][:], start=st_["first"], stop=st_["last"]),
                           I("matmul", pbank[bd][:, :], lhsT=ones_bf[:], rhs=pt_sb[ps_][:], start=st_["first"], stop=st_["last"])]
                    P.group("pe", fns, reads=[B_kv[kvs], B_pt[ps_], B_ones], writes=[B_bank[bo], B_bank[bd]])
                else:
                    on_pe = (j % 2 == 1)
                    fns = [I("matmul", pbank[bo][:, :], lhsT=v_sb[kvs][:, kc, :], rhs=pt_sb[ps_][:], start=st_["first"], stop=st_["last"])]
                    wr = [B_bank[bo]]
                    if on_pe:
                        fns.append(I("matmul", pbank[bd][:, :], lhsT=ones_bf[:], rhs=pt_sb[ps_][:], start=(j == 1), stop=False))
                        wr.append(B_bank[bd])
                    P.group("pe", fns, reads=[B_kv[kvs], B_pt[ps_], B_ones], writes=wr)
                    if not on_pe:
                        if j == 0:
                            P.op("dve", I("tensor_copy", out=accD[b2][:], in_=pt_sb[ps_][:]), reads=[B_pt[ps_]], writes=[B_accD[b2]])
                        else:
                            P.op("dve", I("tensor_tensor", out=accD[b2][:], in0=accD[b2][:], in1=pt_sb[ps_][:], op=ALU.add),
                                 reads=[B_pt[ps_]], writes=[B_accD[b2]])
                    if st_["last"]:
                        P.op("dve", I("tensor_copy", out=hi_sb[b2][:], in_=accD[b2][:]), reads=[B_accD[b2]], writes=[B_hi[b2]])
                        P.op("dve", I("tensor_tensor", out=lo_sb[b2][:], in0=accD[b2][:], in1=hi_sb[b2][:], op=ALU.subtract),
                             reads=[B_accD[b2], B_hi[b2]], writes=[B_lo[b2]])
                if st_["last"]:
                    deferred.append([2, (b2, bo, bd, st_["mh"], st_["qb"])])

            LOOK = 3
            for n in range(min(LOOK, len(steps))):
                emit_scores(n)
            for n in range(len(steps)):
                if n + LOOK < len(steps):
                    emit_scores(n + LOOK)
                emit_pv(n)
                for d_ in deferred + deferred2:
                    d_[0] -= 1
                while deferred2 and deferred2[0][0] <= 0:
                    finish2(*deferred2.pop(0)[1])
                while deferred and deferred[0][0] <= 0:
                    finish(*deferred.pop(0)[1])
            while deferred:
                finish(*deferred.pop(0)[1])
            while deferred2:
                finish2(*deferred2.pop(0)[1])

        emit_casts()
        if 2 in phases:
            with ExitStack() as ps2:
                heads = []
                for h in range(NHA):
                    g = h // 4
                    heads.append((h, qaT[h], kaT[g], va[:, g * 128:(g + 1) * 128], ("a", g)))
                attention(ps2, heads, masked=False)
            P.barrier()
        if 3 in phases:
            with ExitStack() as ps3:
                heads = []
                for h in range(NHB):
                    heads.append((8 + h, qbT[h], kbT[h], vb[:, h * 128:(h + 1) * 128], ("b", h)))
                attention(ps3, heads, masked=True)
            P.barrier()

        emit_casts()
        if 4 in phases:
            with ExitStack() as ps4:
                NRING = 4
                ring = [sbt(ps4, "ring%d" % i, [128, 8192], BF16) for i in range(NRING)]
                B_ring = [Buf("ring%d" % i) for i in range(NRING)]
                mix_sbs = [sbt(ps4, "mix_sb0", [128, 16, 512], BF16)] * 2
                B_mixs = [Buf("mix0")] * 2
                gffn_sb = sbt(ps4, "gffn_sb", [128, 16], F32)
                gfin_sb = sbt(ps4, "gfin_sb", [128, D], F32)
                B_gffn, B_gfin = Buf("gffn"), Buf("gfin")
                P.dma("sp", gffn_sb[:], g_ffn, writes=[B_gffn])
                P.dma("sp", gfin_sb[:], g_fin, writes=[B_gfin])
                xres = [sbt(ps4, "xres%d" % i, [128, D], F32) for i in range(4)]
                B_xres = [Buf("xres%d" % i) for i in range(4)]
                h2 = [sbt(ps4, "h2_%d" % i, [128, D], BF16) for i in range(4)]
                B_h2 = [Buf("h2_%d" % i) for i in range(4)]
                h2T = sbt(ps4, "h2T", [128, 16, 512], BF16)
                B_h2T = [Buf("h2T_%d" % t) for t in range(4)]
                actT = sbt(ps4, "actT", [128, NFC, 512], BF16)
                B_act = [Buf("act%d" % f) for f in range(NFC)]
                st2 = [sbt(ps4, "st2_%d" % i, [128, 16], F32) for i in range(2)]
                B_st2 = [Buf("st2_%d" % i) for i in range(2)]
                st3 = [sbt(ps4, "st3_%d" % i, [128, 16], F32) for i in range(2)]
                B_st3 = [Buf("st3_%d" % i) for i in range(2)]
                sg = [sbt(ps4, "sg%d" % i, [128, 512], F32) for i in range(2)]
                B_sg = [Buf("sg%d" % i) for i in range(2)]

                items = []
                for tb in range(NQB):
                    for cb in range(4):
                        items.append(("wout", cb, 0))
                    for f2 in range(22):
                        items.append(("wgu", f2, 0))
                    for cb in range(4):
                        for pc in range(3):
                            items.append(("wd", cb, pc))
                rstate = {"loaded": 0, "used": 0}

                def ring_prefetch(upto):
                    while rstate["loaded"] < min(upto, len(items)):
                        n = rstate["loaded"]
                        kind, a, b = items[n]
                        s = n % NRING
                        if kind == "wout":
                            src = wout_s[a].rearrange("p h c -> p (h c)")
                        elif kind == "wgu":
                            src = wgu_s[a].rearrange("p g k c -> p (g k c)")
                        else:
                            nfp = 16 if b < 2 else NFC - 32
                            src = wd_s[a, b, :, 0:nfp, :].rearrange("p f c -> p (f c)")
                        P.dma("pool", ring[s][:, 0:src.shape[1]], src, writes=[B_ring[s]], extra=[cast_state["tok"]])
                        rstate["loaded"] += 1

                def ring_next(kind):
                    n = rstate["used"]
                    assert items[n][0] == kind, (items[n], kind)
                    ring_prefetch(n + NRING)
                    rstate["used"] += 1
                    s = n % NRING
                    return ring[s], B_ring[s]

                tp_bank4 = [pbank[4][:].bitcast(BF16), pbank[5][:].bitcast(BF16)]
                c4 = {"op": 0, "gu": 0}
                ring_prefetch(NRING - 1)
                for tb in range(NQB):
                    t0 = tb * 512
                    s2 = tb % 2
                    mix_sb, B_mix = mix_sbs[tb % 2], B_mixs[tb % 2]
                    if tb == 0:
                        P.dma("sp", mix_sb[:], mixT[:, :, t0:t0 + 512].rearrange("h d t -> d h t"), writes=[B_mix])
                    for t in range(4):
                        r0 = t0 + t * 128
                        P.dma("sp", xres[t][:], x[r0:r0 + 128, :], writes=[B_xres[t]])
                    for t in range(4):
                        tt = tb * 4 + t
                        for ab in range(2):
                            P.op("dve", I("tensor_reduce",
                                out=st2[s2][:, t * 2 + ab:t * 2 + ab + 1], in_=ssq[:, tt, ab * 8:(ab + 1) * 8], axis=AX.X, op=ALU.add),
                                reads=[B_ssq], writes=[B_st2[s2]])
                    P.op("dve", I("tensor_scalar", out=st2[s2][:, 0:8], in0=st2[s2][:, 0:8], scalar1=1.0 / 1024, scalar2=EPS,
                                                          op0=ALU.mult, op1=ALU.add), writes=[B_st2[s2]])
                    P.op("act", I("activation", out=st2[s2][:, 0:8], in_=st2[s2][:, 0:8], func=AF.Sqrt), writes=[B_st2[s2]])
                    P.op("dve", I("reciprocal", out=st2[s2][:, 8:16], in_=st2[s2][:, 0:8]), writes=[B_st2[s2]])
                    for cb in range(4):
                        wt, bw = ring_next("wout")
                        wv = wt[:].rearrange("p (h c) -> p h c", h=16)
                        cs = slice(cb * 512, (cb + 1) * 512)
                        for t in range(4):
                            pa = (c4["op"] % 2) * 2
                            c4["op"] += 1
                            pbk = pa + 1
                            fa = [I("matmul", pbank[pa][:, :], lhsT=mix_sb[:, h, t * 128:(t + 1) * 128],
                                                                           rhs=wv[:, h, :], start=(h == 0), stop=(h == 7)) for h in range(8)]
                            fb = [I("matmul", pbank[pbk][:, :], lhsT=mix_sb[:, h, t * 128:(t + 1) * 128],
                                                                             rhs=wv[:, h, :], start=(h == 8), stop=(h == 15)) for h in range(8, 16)]
                            P.group("pe", fa + fb, reads=[B_mix, bw], writes=[B_bank[pa], B_bank[pbk]])
                            P.op("dve", I("scalar_tensor_tensor",
                                out=xres[t][:, cs], in0=pbank[pa][:, :], scalar=st2[s2][:, 8 + t * 2:9 + t * 2], in1=xres[t][:, cs],
                                op0=ALU.mult, op1=ALU.add), reads=[B_bank[pa], B_st2[s2]], writes=[B_xres[t]])
                            P.op("dve", I("scalar_tensor_tensor",
                                out=xres[t][:, cs], in0=pbank[pbk][:, :], scalar=st2[s2][:, 9 + t * 2:10 + t * 2], in1=xres[t][:, cs],
                                op0=ALU.mult, op1=ALU.add), reads=[B_bank[pbk], B_st2[s2]], writes=[B_xres[t]])
                            if cb == 3:
                                ss = st3[s2]
                                if t == 0:
                                    P.op("dve", I("memset", ss[:], 0.0), writes=[B_st3[s2]])
                                P.op("act", I("activation", out=h2[t][:], in_=xres[t][:], func=AF.Square, accum_out=ss[:, t:t + 1]),
                                     reads=[B_xres[t]], writes=[B_h2[t], B_st3[s2]])
                                P.op("dve", I("tensor_scalar", out=ss[:, 4 + t:5 + t], in0=ss[:, t:t + 1], scalar1=1.0 / D, scalar2=EPS,
                                              op0=ALU.mult, op1=ALU.add), writes=[B_st3[s2]])
                                P.op("act", I("activation", out=ss[:, 4 + t:5 + t], in_=ss[:, 4 + t:5 + t], func=AF.Sqrt), writes=[B_st3[s2]])
                                P.op("dve", I("reciprocal", out=ss[:, 4 + t:5 + t], in_=ss[:, 4 + t:5 + t]), writes=[B_st3[s2]])
                                P.op("dve", I("tensor_scalar", out=h2[t][:], in0=xres[t][:], scalar1=ss[:, 4 + t:5 + t], scalar2=None,
                                              op0=ALU.mult), reads=[B_xres[t], B_st3[s2]], writes=[B_h2[t]])
                    for t in range(4):
                        hs = t
                        for half in range(2):
                            fns = [I("transpose", tp_bank4[half][:, kk * 128:(kk + 1) * 128],
                                     h2[hs][:, (half * 8 + kk) * 128:(half * 8 + kk + 1) * 128], ident[:]) for kk in range(8)]
                            P.group("pe", fns, reads=[B_h2[hs], B_ident], writes=[B_bank[4 + half]])
                            src = tp_bank4[half][:, :].rearrange("p (k t) -> p k t", k=8)
                            gm = gffn_sb[:, half * 8:(half + 1) * 8].unsqueeze(2).to_broadcast([128, 8, 128])
                            P.op("dve", I("tensor_tensor", out=h2T[:, half * 8:(half + 1) * 8, t * 128:(t + 1) * 128], in0=src, in1=gm, op=ALU.mult),
                                 reads=[B_bank[4 + half], B_gffn], writes=[B_h2T[t]])
                    if tb + 1 < NQB:
                        P.dma("sp", mix_sbs[(tb + 1) % 2][:], mixT[:, :, t0 + 512:t0 + 1024].rearrange("h d t -> d h t"),
                              writes=[B_mixs[(tb + 1) % 2]])
                    for f2 in range(22):
                        wt, bw = ring_next("wgu")
                        wv = wt[:].rearrange("p (g k c) -> p g k c", g=2, k=16)
                        for j in range(2):
                            fc = f2 * 2 + j
                            pg = (c4["gu"] % 2) * 2
                            c4["gu"] += 1
                            pu = pg + 1
                            fg = [I("matmul", pbank[pg][:, :], lhsT=wv[:, 0, kc, j * 128:(j + 1) * 128],
                                                                             rhs=h2T[:, kc, :], start=(kc == 0), stop=(kc == 15)) for kc in range(16)]
                            fu = [I("matmul", pbank[pu][:, :], lhsT=wv[:, 1, kc, j * 128:(j + 1) * 128],
                                                                             rhs=h2T[:, kc, :], start=(kc == 0), stop=(kc == 15)) for kc in range(16)]
                            P.group("pe", fg, reads=B_h2T + [bw], writes=[B_bank[pg]])
                            P.group("pe", fu, reads=B_h2T + [bw], writes=[B_bank[pu]])
                            gs = fc % 2
                            P.op("act", I("activation", out=sg[gs][:], in_=pbank[pg][:, :], func=AF.Silu),
                                 reads=[B_bank[pg]], writes=[B_sg[gs]])
                            P.op("dve", I("tensor_tensor", out=actT[:, fc, :], in0=pbank[pu][:, :], in1=sg[gs][:],
                                                                                     op=ALU.mult),
                                 reads=[B_bank[pu], B_sg[gs]], writes=[B_act[fc]])
                    for cb in range(4):
                        cs = slice(cb * 512, (cb + 1) * 512)
                        for pc in range(3):
                            wt, bw = ring_next("wd")
                            wv = wt[:].rearrange("p (f c) -> p f c", f=16)
                            nf = 16 if pc < 2 else NFC - 32
                            for t in range(4):
                                fns = [I("matmul",
                                    pbank[4 + t][:, :], lhsT=actT[:, pc * 16 + f, t * 128:(t + 1) * 128], rhs=wv[:, f, :],
                                    start=(pc == 0 and f == 0), stop=(pc == 2 and f == nf - 1)) for f in range(nf)]
                                P.group("pe", fns, reads=B_act[pc * 16:pc * 16 + nf] + [bw], writes=[B_bank[4 + t]])
                        for t in range(4):
                            P.op("dve", I("tensor_tensor", out=xres[t][:, cs], in0=pbank[4 + t][:, :], in1=xres[t][:, cs],
                                                                            op=ALU.add),
                                 reads=[B_bank[4 + t]], writes=[B_xres[t]])
                    ss = st3[s2]
                    for t in range(4):
                        r0 = t0 + t * 128
                        hs = t % 2
                        P.op("act", I("activation", out=h2[hs][:], in_=xres[t][:], func=AF.Square,
                                                                           accum_out=ss[:, 8 + t:9 + t]),
                             reads=[B_xres[t]], writes=[B_h2[hs], B_st3[s2]])
                        P.op("dve", I("tensor_scalar", out=ss[:, 12 + t:13 + t], in0=ss[:, 8 + t:9 + t], scalar1=1.0 / D,
                                                                        scalar2=EPS, op0=ALU.mult, op1=ALU.add), writes=[B_st3[s2]])
                        P.op("act", I("activation", out=ss[:, 12 + t:13 + t], in_=ss[:, 12 + t:13 + t], func=AF.Sqrt),
                             writes=[B_st3[s2]])
                        P.op("dve", I("reciprocal", out=ss[:, 12 + t:13 + t], in_=ss[:, 12 + t:13 + t]), writes=[B_st3[s2]])
                        P.op("dve", I("scalar_tensor_tensor", out=xres[t][:], in0=xres[t][:], scalar=ss[:, 12 + t:13 + t],
                                                                               in1=gfin_sb[:], op0=ALU.mult, op1=ALU.mult),
                             reads=[B_gfin, B_st3[s2]], writes=[B_xres[t]])
                        P.dma("sp", out[r0:r0 + 128, :], xres[t][:], reads=[B_xres[t]])
        P.fence_stores(engs=("sp",))
        P.run()
    return nc


def _host_inputs(inputs, S):
    f32 = np.float32
    ca, sa, cb, sb_, mask = _const_tables(S)

    def swap(g):
        return np.ascontiguousarray(g.reshape(2, 2, 32)[:, ::-1, :]).reshape(128)

    gq = np.asarray(inputs["g_q_a"], f32)[0]
    gk = np.asarray(inputs["g_k_a"], f32)[0]
    g_qk = np.ascontiguousarray(np.broadcast_to(np.stack([gq, swap(gq), gk, swap(gk)])[None], (128, 4, 128)))
    g_out = np.concatenate([np.asarray(inputs["g_out_a"], f32)[0], np.asarray(inputs["g_out_b"], f32)[0]])
    shared = {
        "w_in": np.ascontiguousarray(np.asarray(inputs["w_in"], f32)[0]),
        "w_out": np.ascontiguousarray(np.asarray(inputs["w_out"], f32)[0]),
        "w_gate_up": np.ascontiguousarray(np.asarray(inputs["w_gate_up"], f32)[0]),
        "w_down": np.ascontiguousarray(np.asarray(inputs["w_down"], f32)[0]),
        "g_mix": np.ascontiguousarray(np.asarray(inputs["g_mix"], f32)[0].reshape(16, 128).T),
        "g_ffn": np.ascontiguousarray(np.asarray(inputs["g_ffn"], f32)[0].reshape(16, 128).T),
        "g_final": np.ascontiguousarray(np.broadcast_to(np.asarray(inputs["g_final"], f32)[None, :], (128, D))),
        "g_qk": g_qk,
        "g_out": np.ascontiguousarray(g_out.reshape(16, 128).T),
        "ropa_c": ca, "ropa_s": sa, "ropb_c": cb, "ropb_s": sb_, "maskb": mask,
    }
    return shared


_NC_CACHE = {}


def kernel(**inputs):
    x = np.asarray(inputs["x"], np.float32)
    B, S, _ = x.shape
    shared = _host_inputs(inputs, S)
    if S not in _NC_CACHE:
        _NC_CACHE[S] = build(S)
    nc = _NC_CACHE[S]
    in_maps = []
    for b in range(B):
        m = dict(shared)
        m["x"] = np.ascontiguousarray(x[b])
        in_maps.append(m)
    res = run_bass_kernel_spmd(nc, in_maps, core_ids=list(range(B)))
    return np.stack([np.asarray(r["out"], np.float32) for r in res.results], axis=0)
```

```python
import numpy as np
from contextlib import ExitStack
import ml_dtypes
import concourse.bass as bass
import concourse.mybir as mybir
from concourse.bass_utils import run_bass_kernel_spmd

F32 = mybir.dt.float32
BF16 = mybir.dt.bfloat16
ALU = mybir.AluOpType
AF = mybir.ActivationFunctionType
AX = mybir.AxisListType

D = 2048
HD = 128
NHA = 8
NKV = 2
NHB = 8
DFF = 5632
PROJ = 4608
EPS = 1e-6
GRID_W = 64
NFC = DFF // 128
SCALE = HD ** -0.5
NREL = 20


def I(name, *a, **k):
    return lambda e: getattr(e, name)(*a, **k)


class Buf:
    __slots__ = ("name", "w", "r", "ld", "st", "excl")

    def __init__(self, name, excl=False):
        self.name = name
        self.excl = excl
        self.w = None
        self.r = {}
        self.ld = None
        self.st = None


class Prog:
    ENG = ("sp", "act", "dve", "pool", "pe")

    def __init__(self, nc, es):
        self.nc = nc
        self.es = es
        self.q = {e: [] for e in self.ENG}
        self.sem = {e: es.enter_context(nc.semaphore("prog_" + e)) for e in self.ENG}
        self.cnt = {e: 0 for e in self.ENG}
        self.waited = {e: {} for e in self.ENG}
        self.dcnt = {}
        self.nsem = 0
        self.stores = {}
        self.alldma = {}

    def new_sem(self, name):
        s = self.es.enter_context(self.nc.semaphore(name + "_%d" % self.nsem))
        self.nsem += 1
        self.dcnt[id(s)] = 0
        return s

    def wait(self, eng, toks):
        w = self.waited[eng]
        for t in toks:
            if t is None:
                continue
            sem, val = t
            if w.get(id(sem), 0) >= val:
                continue
            w[id(sem)] = val
            self.q[eng].append(lambda e, sem=sem, val=val: e.wait_ge(sem, val))

    def _deps(self, reads, writes, extra):
        toks = list(extra)
        for b in reads:
            toks.append(b.w)
            if b.excl:
                toks.extend(b.r.values())
        for b in writes:
            toks.append(b.w)
            toks.extend(b.r.values())
        return toks

    def _commit(self, tok, reads, writes):
        for b in reads:
            b.r[id(tok[0])] = tok
        for b in writes:
            b.w = tok
            b.r = {}

    def op(self, eng, fn, reads=(), writes=(), extra=()):
        return self.group(eng, [fn], reads, writes, extra)

    def group(self, eng, fns, reads=(), writes=(), extra=()):
        self.wait(eng, self._deps(reads, writes, extra))
        self.cnt[eng] += 1
        sem = self.sem[eng]
        tok = (sem, self.cnt[eng])
        for fn in fns[:-1]:
            self.q[eng].append(lambda e, fn=fn: fn(e))
        self.q[eng].append(lambda e, fn=fns[-1], sem=sem: fn(e).then_inc(sem, 1))
        self._commit(tok, reads, writes)
        return tok

    def dma(self, eng, out, in_, reads=(), writes=(), extra=(), sem=None):
        deps = self._deps(reads, writes, extra)
        if writes and writes[0].ld is not None:
            deps = [t for t in deps if t is None or t[0] is not writes[0].ld]
        self.wait(eng, deps)
        if sem is None:
            if writes:
                b = writes[0]
                if b.ld is None:
                    b.ld = self.new_sem("ld_" + b.name)
                sem = b.ld
            else:
                b = reads[0]
                if b.st is None:
                    b.st = self.new_sem("st_" + b.name)
                sem = b.st
        self.dcnt[id(sem)] += 16
        tok = (sem, self.dcnt[id(sem)])
        self.q[eng].append(lambda e, out=out, in_=in_, sem=sem: e.dma_start(out=out, in_=in_).then_inc(sem, 16))
        self._commit(tok, reads, writes)
        if reads and not writes:
            self.stores[id(sem)] = tok
        self.alldma[id(sem)] = tok
        return tok

    def barrier(self):
        toks = [(self.sem[e], self.cnt[e]) for e in self.ENG if self.cnt[e] > 0] + list(self.alldma.values())
        for e in self.ENG:
            self.wait(e, toks)

    def fence_stores(self, engs=("sp", "pool", "act")):
        toks = list(self.stores.values())
        for e in engs:
            self.wait(e, toks)

    def run(self):
        nc = self.nc
        with nc.Block() as block:
            @block.sync
            def _(e):
                for f in self.q["sp"]:
                    f(e)

            @block.scalar
            def _(e):
                for f in self.q["act"]:
                    f(e)

            @block.vector
            def _(e):
                for f in self.q["dve"]:
                    f(e)

            @block.gpsimd
            def _(e):
                for f in self.q["pool"]:
                    f(e)

            @block.tensor
            def _(e):
                for f in self.q["pe"]:
                    f(e)


def _const_tables(S):
    t = np.arange(S)
    half = HD // 2
    inv_a = (10000.0 ** (-(np.arange(0, half, 2, dtype=np.float32) / half))).astype(np.float32)
    row = (t // GRID_W).astype(np.float32)
    col = (t % GRID_W).astype(np.float32)
    ang_r = row[:, None] * inv_a[None, :]
    ang_c = col[:, None] * inv_a[None, :]
    ca = np.zeros((S, 2, 2, 32), np.float32)
    sa = np.zeros((S, 2, 2, 32), np.float32)
    for a, ang in enumerate((ang_r, ang_c)):
        c = np.cos(ang.astype(np.float32)).astype(np.float32)
        s = np.sin(ang.astype(np.float32)).astype(np.float32)
        ca[:, a, 0] = c
        ca[:, a, 1] = c
        sa[:, a, 0] = -s
        sa[:, a, 1] = s
    pr = HD // 4
    inv_b = (500000.0 ** (-(np.arange(0, pr, 2, dtype=np.float32) / pr))).astype(np.float32)
    ang_b = t.astype(np.float32)[:, None] * inv_b[None, :]
    cb = np.cos(ang_b).astype(np.float32)
    sb_ = np.sin(ang_b).astype(np.float32)
    cbt = np.concatenate([cb, cb], axis=1)
    sbt = np.concatenate([-sb_, sb_], axis=1)
    k = np.arange(128)[:, None]
    q = np.arange(512)[None, :]
    mask = np.zeros((NREL, 128, 512), np.float32)
    for r in range(NREL):
        d = 128 * (r - 8) + k - q
        ad = np.abs(d)
        mask[r] = (ad <= 64).astype(np.float32) + ((d % 4 == 0) & (ad <= 256)) + ((d % 16 == 0) & (ad <= 1024))
    return (ca.reshape(S, 128), sa.reshape(S, 128), cbt.astype(np.float32), sbt.astype(np.float32),
            mask.astype(ml_dtypes.bfloat16))


def build(S=4096, debug=False, phases=(1, 2, 3, 4)):
    NT = S // 128
    NQB = S // 512
    nc = bass.Bass("TRN2", target_bir_lowering=False)

    def din(name, shape, dt=F32):
        return nc.dram_tensor(name, list(shape), dt, kind="ExternalInput").ap()

    def dscr(name, shape, dt):
        if debug:
            return nc.dram_tensor(name, list(shape), dt, kind="ExternalOutput").ap()
        return nc.dram_tensor(name, list(shape), dt).ap()

    x = din("x", [S, D])
    w_in = din("w_in", [D, PROJ])
    w_out = din("w_out", [D, D])
    w_gu = din("w_gate_up", [D, 2 * DFF])
    w_dn = din("w_down", [DFF, D])
    g_mix = din("g_mix", [128, 16])
    g_ffn = din("g_ffn", [128, 16])
    g_fin = din("g_final", [128, D])
    g_qk = din("g_qk", [128, 4, 128])
    g_out = din("g_out", [128, 16])
    ropa_c = din("ropa_c", [S, 128])
    ropa_s = din("ropa_s", [S, 128])
    ropb_c = din("ropb_c", [S, 32])
    ropb_s = din("ropb_s", [S, 32])
    maskb = din("maskb", [NREL, 128, 512], BF16)
    out = nc.dram_tensor("out", [S, D], F32, kind="ExternalOutput").ap()

    qaT = dscr("qaT", [NHA, 128, S], BF16)
    kaT = dscr("kaT", [NKV, 128, S], BF16)
    va = dscr("va", [S, NKV * 128], BF16)
    qbT = dscr("qbT", [NHB, 128, S], BF16)
    kbT = dscr("kbT", [NHB, 128, S], BF16)
    vb = dscr("vb", [S, NHB * 128], BF16)
    mixT = dscr("mixT", [16, 128, S], BF16)
    x1d = dscr("x1d", [S, D], F32)
    wout_s = dscr("wout_s", [4, 128, 16, 512], BF16)
    wgu_s = dscr("wgu_s", [22, 128, 2, 16, 256], BF16)
    wd_s = dscr("wd_s", [4, 3, 128, 16, 512], BF16)

    with ExitStack() as es:
        P = Prog(nc, es)

        def sbt(stack, name, shape, dt):
            return stack.enter_context(nc.sbuf_tensor(name, list(shape), dt))

        def pst(stack, name, shape, dt):
            return stack.enter_context(nc.psum_tensor(name, list(shape), dt))

        ident = sbt(es, "ident", [128, 128], BF16)
        ones_bf = sbt(es, "ones_bf", [128, 128], BF16)
        ones_f = sbt(es, "ones_f", [128, 1], F32)
        gout_sb = sbt(es, "gout_sb", [128, 16], F32)
        ssq = sbt(es, "ssq", [128, NT, 16], F32)
        B_ident, B_ones, B_onesf, B_gout, B_ssq = Buf("ident"), Buf("ones"), Buf("onesf"), Buf("gout"), Buf("ssq")
        P.op("pool", I("memset", ident[:], 1.0), writes=[B_ident])
        P.op("pool", I("affine_select", out=ident[:], in_=ident[:], pattern=[[-1, 128]], compare_op=ALU.is_equal,
                                                fill=0.0, base=0, channel_multiplier=1), writes=[B_ident])
        P.op("pool", I("memset", ones_bf[:], 1.0), writes=[B_ones])
        P.op("pool", I("memset", ones_f[:], 1.0), writes=[B_onesf])

        P.dma("sp", gout_sb[:], g_out, writes=[B_gout])

        pbank = [pst(es, "pbank%d" % i, [128, 512], F32) for i in range(8)]
        B_bank = [Buf("bank%d" % i, excl=True) for i in range(8)]

        cast_sem = P.new_sem("cast")
        cast_state = {"tok": None, "done": False}

        cast_list = []
        if 4 in phases:
            for cb in range(4):
                cast_list.append((wout_s[cb], w_out[:, cb * 512:(cb + 1) * 512].rearrange("(h p) c -> p h c", p=128)))
            for f2 in range(22):
                for gu in range(2):
                    c0 = gu * DFF + f2 * 256
                    cast_list.append((wgu_s[f2, :, gu], w_gu[:, c0:c0 + 256].rearrange("(kc p) c -> p kc c", p=128)))
            for cb in range(4):
                for pc in range(3):
                    n = 16 if pc < 2 else NFC - 32
                    r0 = pc * 16 * 128
                    cast_list.append((wd_s[cb, pc, :, 0:n, :],
                                      w_dn[r0:r0 + n * 128, cb * 512:(cb + 1) * 512].rearrange("(fc p) c -> p fc c", p=128)))

        def cast_one():
            if cast_list:
                dst, src = cast_list.pop(0)
                cast_state["tok"] = P.dma("pool", dst, src, sem=cast_sem)

        def emit_casts():
            while cast_list:
                cast_one()

        if 1 in phases:
            with ExitStack() as ps1:
                w_sb = sbt(ps1, "w_sb", [128, 16, PROJ], BF16)
                B_w = Buf("w_sb")
                for kq in range(4):
                    P.dma("pool", w_sb[:, kq * 4:(kq + 1) * 4, :],
                          w_in[kq * 512:(kq + 1) * 512, :].rearrange("(kc p) c -> p kc c", p=128), writes=[B_w])
                gmix_sb = sbt(ps1, "gmix_sb", [128, 16], F32)
                gqk_sb = sbt(ps1, "gqk_sb", [128, 4, 128], F32)
                B_gmix, B_gqk = Buf("gmix"), Buf("gqk")
                P.dma("sp", gmix_sb[:], g_mix, writes=[B_gmix])
                P.dma("sp", gqk_sb[:], g_qk, writes=[B_gqk])

                xb = [sbt(ps1, "xb%d" % i, [128, D], F32) for i in range(2)]
                B_x = [Buf("xb%d" % i) for i in range(2)]
                tabs = [sbt(ps1, "tabs%d" % i, [128, 320], F32) for i in range(2)]
                B_tabs = [Buf("tabs%d" % i) for i in range(2)]
                dtab = [sbt(ps1, "dtab%d" % i, [128, 4 * 128], F32) for i in range(2)]
                B_dtab = [Buf("dtab%d" % i) for i in range(2)]
                stat = [sbt(ps1, "stat%d" % i, [128, 8], F32) for i in range(2)]
                B_stat = [Buf("stat%d" % i) for i in range(2)]
                hb = [sbt(ps1, "hb0", [128, D], BF16)] * 2
                B_h = [Buf("hb0")] * 2
                neghalf = sbt(ps1, "neghalf", [128, 4], F32)
                B_neghalf = Buf("neghalf")
                P.op("pool", I("memset", neghalf[:], -0.5), writes=[B_neghalf])
                hT = [sbt(ps1, "hT%d" % i, [128, 16, 128], BF16) for i in range(2)]
                B_hT = [Buf("hT%d" % i) for i in range(2)]
                NSC = 2
                scr1 = [sbt(ps1, "scr1_%d" % i, [128, 512], F32) for i in range(NSC)]
                scr2 = [sbt(ps1, "scr2_%d" % i, [128, 512], F32) for i in range(NSC)]
                scr3 = [sbt(ps1, "scr3_%d" % i, [128, 512], F32) for i in range(NSC)]
                B_scr1 = [Buf("scr1_%d" % i) for i in range(NSC)]
                B_scr2 = [Buf("scr2_%d" % i) for i in range(NSC)]
                B_scr3 = [Buf("scr3_%d" % i) for i in range(NSC)]
                st4 = [sbt(ps1, "st4_%d" % i, [128, 8], F32) for i in range(NSC)]
                B_st4 = [Buf("st4_%d" % i) for i in range(NSC)]
                NSTG = 6
                stg = [sbt(ps1, "stg%d" % i, [128, 512], BF16) for i in range(NSTG)]
                B_stg = [Buf("stg%d" % i) for i in range(NSTG)]
                NTS = 3
                tst = [sbt(ps1, "tst%d" % i, [128, 4, 128], BF16) for i in range(NTS)]
                B_tst = [Buf("tst%d" % i) for i in range(NTS)]
                NVS = 2
                vst = [sbt(ps1, "vst%d" % i, [128, 512], BF16) for i in range(NVS)]
                B_vst = [Buf("vst%d" % i) for i in range(NVS)]

                tp_bank = [pbank[0][:].bitcast(BF16), pbank[1][:].bitcast(BF16)]

                def load_tile(i):
                    s = i % 2
                    r0 = i * 128
                    P.dma("sp", xb[s][:], x[r0:r0 + 128, :], writes=[B_x[s]])
                    P.dma("sp", tabs[s][:, 0:128], ropa_c[r0:r0 + 128, :], writes=[B_tabs[s]])
                    P.dma("sp", tabs[s][:, 128:256], ropa_s[r0:r0 + 128, :], writes=[B_tabs[s]])
                    P.dma("sp", tabs[s][:, 256:288], ropb_c[r0:r0 + 128, :], writes=[B_tabs[s]])
                    P.dma("sp", tabs[s][:, 288:320], ropb_s[r0:r0 + 128, :], writes=[B_tabs[s]])

                def norm_tile(i):
                    s = i % 2
                    P.op("dve", I("memset", stat[s][:], 0.0), writes=[B_stat[s]])
                    P.op("act", I("activation", out=hb[s][:], in_=xb[s][:], func=AF.Square, accum_out=stat[s][:, 0:1]),
                         reads=[B_x[s]], writes=[B_h[s], B_stat[s]])
                    P.op("dve", I("tensor_scalar", out=stat[s][:, 1:2], in0=stat[s][:, 0:1], scalar1=1.0 / D, scalar2=EPS,
                                                          op0=ALU.mult, op1=ALU.add), writes=[B_stat[s]])
                    P.op("act", I("activation", out=stat[s][:, 2:3], in_=stat[s][:, 1:2], func=AF.Sqrt), writes=[B_stat[s]])
                    P.op("dve", I("reciprocal", out=stat[s][:, 3:4], in_=stat[s][:, 2:3]), writes=[B_stat[s]])
                    P.op("dve", I("tensor_scalar", out=hb[s][:], in0=xb[s][:], scalar1=stat[s][:, 3:4], scalar2=None,
                                                          op0=ALU.mult), reads=[B_x[s], B_stat[s]], writes=[B_h[s]])
                    t_, d_ = tabs[s], dtab[s]
                    P.op("pool", I("tensor_tensor", out=d_[:, 0:128], in0=t_[:, 0:128], in1=gqk_sb[:, 0, :], op=ALU.mult),
                         reads=[B_tabs[s], B_gqk], writes=[B_dtab[s]])
                    P.op("pool", I("tensor_tensor", out=d_[:, 128:256], in0=t_[:, 128:256], in1=gqk_sb[:, 1, :], op=ALU.mult),
                         reads=[B_tabs[s], B_gqk], writes=[B_dtab[s]])
                    P.op("pool", I("tensor_tensor", out=d_[:, 256:384], in0=t_[:, 0:128], in1=gqk_sb[:, 2, :], op=ALU.mult),
                         reads=[B_tabs[s], B_gqk], writes=[B_dtab[s]])
                    P.op("pool", I("tensor_tensor", out=d_[:, 384:512], in0=t_[:, 128:256], in1=gqk_sb[:, 3, :], op=ALU.mult),
                         reads=[B_tabs[s], B_gqk], writes=[B_dtab[s]])

                def transp_h(i):
                    s = i % 2
                    for half in range(2):
                        fns = []
                        for kk in range(8):
                            kc = half * 8 + kk
                            fns.append(I("transpose",
                                tp_bank[half][:, kk * 128:(kk + 1) * 128], hb[s][:, kc * 128:(kc + 1) * 128], ident[:]))
                        P.group("pe", fns, reads=[B_h[s], B_ident], writes=[B_bank[half]])
                        src = tp_bank[half][:, :].rearrange("p (k t) -> p k t", k=8)
                        gm = gmix_sb[:, half * 8:(half + 1) * 8].unsqueeze(2).to_broadcast([128, 8, 128])
                        P.op("dve", I("tensor_tensor",
                            out=hT[s][:, half * 8:(half + 1) * 8, :], in0=src, in1=gm, op=ALU.mult),
                            reads=[B_bank[half], B_gmix], writes=[B_hT[s]])

                cnt = {"scr": 0, "stg": 0, "tst": 0, "vst": 0, "pj": 0, "tq": 0}
                pending = []

                def rope_norm_block(pb_ap, nh, t1off, rstd_mode, dst_slot):
                    k = cnt["scr"] % NSC
                    cnt["scr"] += 1
                    s_ = cur["s"]
                    bank = cur["bank"]
                    d_ = dtab[s_]
                    W = nh * 128
                    P.op("act", I("activation", out=scr2[k][:, 0:W], in_=pb_ap, func=AF.Copy), reads=[bank], writes=[B_scr2[k]])
                    P.op("pool", I("memset", st4[k][:], 0.0), writes=[B_st4[k]])
                    for h in range(nh):
                        P.op("act", I("activation", out=scr1[k][:, h * 128:(h + 1) * 128], in_=scr2[k][:, h * 128:(h + 1) * 128], func=AF.Square,
                                      accum_out=st4[k][:, h:h + 1]), reads=[B_scr2[k]], writes=[B_scr1[k], B_st4[k]])
                    if rstd_mode == "q":
                        P.op("pool", I("tensor_scalar", out=st4[k][:, 0:nh], in0=st4[k][:, 0:nh], scalar1=1.0, scalar2=128 * EPS,
                                       op0=ALU.mult, op1=ALU.add), writes=[B_st4[k]])
                    else:
                        P.op("pool", I("tensor_scalar", out=st4[k][:, 0:nh], in0=st4[k][:, 0:nh], scalar1=1.0 / 128, scalar2=EPS,
                                       op0=ALU.mult, op1=ALU.add), writes=[B_st4[k]])
                    P.op("pool", I("tensor_tensor", out=st4[k][:, 4:4 + nh], in0=st4[k][:, 0:nh], in1=neghalf[:, 0:nh], op=ALU.pow),
                         reads=[B_neghalf], writes=[B_st4[k]])
                    xs = scr2[k][:, 0:W]
                    T1 = d_[:, t1off:t1off + 128].unsqueeze(1).to_broadcast([128, nh, 128])
                    P.op("dve", I("tensor_tensor", out=scr1[k][:, 0:W].rearrange("p (h d) -> p h d", h=nh),
                                  in0=xs.rearrange("p (h d) -> p h d", h=nh), in1=T1, op=ALU.mult),
                         reads=[B_scr2[k], B_dtab[s_]], writes=[B_scr1[k]])
                    x5 = xs.rearrange("p (h a f j) -> p h a f j", h=nh, a=2, f=2)
                    o5 = scr3[k][:, 0:W].rearrange("p (h a f j) -> p h a f j", h=nh, a=2, f=2)
                    T2 = d_[:, t1off + 128:t1off + 256].rearrange("p (a f j) -> p a f j", a=2, f=2)
                    for f in range(2):
                        tb = T2[:, :, f, :].unsqueeze(1).to_broadcast([128, nh, 2, 32])
                        P.op("dve", I("tensor_tensor", out=o5[:, :, :, f, :], in0=x5[:, :, :, 1 - f, :], in1=tb, op=ALU.mult),
                             reads=[B_scr2[k], B_dtab[s_]], writes=[B_scr3[k]])
                    P.op("dve", I("tensor_tensor", out=scr1[k][:, 0:W], in0=scr1[k][:, 0:W], in1=scr3[k][:, 0:W], op=ALU.add),
                         reads=[B_scr3[k]], writes=[B_scr1[k]])
                    rb = st4[k][:, 4:4 + nh].unsqueeze(2).to_broadcast([128, nh, 128])
                    P.op("dve", I("tensor_tensor", out=stg[dst_slot][:, 0:W].rearrange("p (h d) -> p h d", h=nh),
                                  in0=scr1[k][:, 0:W].rearrange("p (h d) -> p h d", h=nh), in1=rb, op=ALU.mult),
                         reads=[B_scr1[k], B_st4[k]], writes=[B_stg[dst_slot]])

                def rope_part_block(pb_ap, scale, coff, dst_slot):
                    k = cnt["scr"] % NSC
                    cnt["scr"] += 1
                    s_ = cur["s"]
                    bank = cur["bank"]
                    P.op("act", I("activation", out=stg[dst_slot][:], in_=pb_ap, func=AF.Copy, scale=float(scale)),
                         reads=[bank], writes=[B_stg[dst_slot]])
                    pb3 = pb_ap.rearrange("p (h d) -> p h d", h=4)
                    xr = scr3[k][:, 0:128].rearrange("p (h j) -> p h j", h=4)
                    P.op("act", I("activation", out=xr, in_=pb3[:, :, 0:32], func=AF.Copy, scale=float(scale)),
                         reads=[bank], writes=[B_scr3[k]])
                    ctab = tabs[s_][:, 256:288]
                    stab = tabs[s_][:, 288:320]
                    tb_ = B_tabs[s_]
                    r1 = scr1[k][:, 0:128].rearrange("p (h j) -> p h j", h=4)
                    r2 = scr2[k][:, 0:128].rearrange("p (h j) -> p h j", h=4)
                    P.op("dve", I("tensor_tensor", out=r1, in0=xr, in1=ctab.unsqueeze(1).to_broadcast([128, 4, 32]), op=ALU.mult),
                         reads=[B_scr3[k], tb_], writes=[B_scr1[k]])
                    for f in range(2):
                        P.op("dve", I("tensor_tensor", out=r2[:, :, f * 16:(f + 1) * 16], in0=xr[:, :, (1 - f) * 16:(2 - f) * 16],
                                      in1=stab[:, f * 16:(f + 1) * 16].unsqueeze(1).to_broadcast([128, 4, 16]), op=ALU.mult),
                             reads=[B_scr3[k], tb_], writes=[B_scr2[k]])
                    P.op("dve", I("tensor_tensor", out=stg[dst_slot][:].rearrange("p (h d) -> p h d", h=4)[:, :, 0:32], in0=r1, in1=r2, op=ALU.add),
                         reads=[B_scr1[k], B_scr2[k]], writes=[B_stg[dst_slot]])

                def out_transposes(slot, nh, dst_fn):
                    tb = 5 + cnt["tq"] % 2
                    cnt["tq"] += 1
                    tpv = pbank[tb][:].bitcast(BF16)
                    fns = [I("transpose", tpv[:, h * 128:(h + 1) * 128], stg[slot][:, h * 128:(h + 1) * 128], ident[:])
                           for h in range(nh)]
                    P.group("pe", fns, reads=[B_stg[slot], B_ident], writes=[B_bank[tb]])
                    ts = cnt["tst"] % NTS
                    cnt["tst"] += 1
                    P.op("act", I("activation", out=tst[ts][:, 0:nh, :], in_=tpv[:, 0:nh * 128].rearrange("p (h t) -> p h t", h=nh),
                                                       func=AF.Copy), reads=[B_bank[tb]], writes=[B_tst[ts]])
                    P.dma("sp", dst_fn(), tst[ts][:, 0:nh, :], reads=[B_tst[ts]])

                cur = {}

                def proj_tile(i, mid1=None, mid2=None):
                    s = i % 2
                    r0 = i * 128
                    cur["s"] = s
                    for cb in range(9):
                        bk = 2 + cnt["pj"] % 3
                        cnt["pj"] += 1
                        cur["bank"] = B_bank[bk]
                        pb_ap = pbank[bk][:, :]
                        fns = [I("matmul", pbank[bk][:, :], lhsT=hT[s][:, kc, :],
                                                                      rhs=w_sb[:, kc, cb * 512:(cb + 1) * 512],
                                                                      start=(kc == 0), stop=(kc == 15)) for kc in range(16)]
                        P.group("pe", fns, reads=[B_hT[s], B_w], writes=[B_bank[bk]])
                        if cb in (0, 1):
                            sl = cnt["stg"] % NSTG
                            cnt["stg"] += 1
                            rope_norm_block(pb_ap, 4, 0, "q", sl)
                            pending.append((sl, 4, (lambda cb=cb, r0=r0: qaT[cb * 4:(cb + 1) * 4, :, r0:r0 + 128].rearrange("h d t -> d h t"))))
                        elif cb == 2:
                            sl = cnt["stg"] % NSTG
                            cnt["stg"] += 1
                            rope_norm_block(pbank[bk][:, 0:256], 2, 256, "k", sl)
                            pending.append((sl, 2, (lambda r0=r0: kaT[:, :, r0:r0 + 128].rearrange("h d t -> d h t"))))
                            vs = cnt["vst"] % NVS
                            cnt["vst"] += 1
                            P.op("act", I("activation", out=vst[vs][:, 0:256], in_=pbank[bk][:, 256:512], func=AF.Copy),
                                 reads=[B_bank[bk]], writes=[B_vst[vs]])
                            P.dma("sp", va[r0:r0 + 128, :], vst[vs][:, 0:256], reads=[B_vst[vs]])
                        elif cb in (3, 4, 5, 6):
                            sl = cnt["stg"] % NSTG
                            cnt["stg"] += 1
                            isq = cb in (3, 4)
                            rope_part_block(pb_ap, SCALE if isq else 1.0, 0, sl)
                            hb0 = (cb - 3) * 4 if isq else (cb - 5) * 4
                            tgt = qbT if isq else kbT
                            pending.append((sl, 4, (lambda tgt=tgt, hb0=hb0, r0=r0: tgt[hb0:hb0 + 4, :, r0:r0 + 128].rearrange("h d t -> d h t"))))
                        else:
                            vs = cnt["vst"] % NVS
                            cnt["vst"] += 1
                            P.op("act", I("activation", out=vst[vs][:], in_=pbank[bk][:, :], func=AF.Copy),
                                 reads=[B_bank[bk]], writes=[B_vst[vs]])
                            c0 = (cb - 7) * 512
                            P.dma("sp", vb[r0:r0 + 128, c0:c0 + 512], vst[vs][:], reads=[B_vst[vs]])
                        if cb == 1 and mid1 is not None:
                            mid1()
                            cur["s"] = s
                        if cb == 5 and mid2 is not None:
                            mid2()
                        while len(pending) > 4:
                            sl_, nh_, fn_ = pending.pop(0)
                            out_transposes(sl_, nh_, fn_)

                load_tile(0)
                norm_tile(0)
                transp_h(0)
                if NT > 1:
                    load_tile(1)
                for i in range(NT):
                    def mid1(i=i):
                        if i + 1 < NT:
                            norm_tile(i + 1)

                    def mid2(i=i):
                        if i + 1 < NT:
                            transp_h(i + 1)
                    proj_tile(i, mid1, mid2)
                    if i + 2 < NT:
                        load_tile(i + 2)
                while pending:
                    sl_, nh_, fn_ = pending.pop(0)
                    out_transposes(sl_, nh_, fn_)
            P.barrier()

        def attention(stack, heads, masked):
            pfx = "m" if masked else "u"
            _sbt = sbt

            def sbt2(stack_, name, shape, dt):
                return _sbt(stack_, pfx + name, shape, dt)
            kt_sb = [sbt2(stack, "kt%d" % i, [128, S], BF16) for i in range(2)]
            v_sb = [sbt2(stack, "v%d" % i, [128, NT, 128], BF16) for i in range(2)]
            B_kv = [Buf("kv%d" % i) for i in range(2)]
            NQ = 3
            qt_sb = [sbt2(stack, "qt%d" % i, [128, 512], BF16) for i in range(NQ)]
            B_qt = [Buf("qt%d" % i) for i in range(NQ)]
            NP = 8
            pt_sb = [sbt2(stack, "pt%d" % i, [128, 512], BF16) for i in range(NP)]
            B_pt = [Buf("pt%d" % i) for i in range(NP)]
            if masked:
                pe_sb = [sbt2(stack, "pe%d" % i, [128, 512], BF16) for i in range(NP)]
                B_pe = [Buf("pe%d" % i) for i in range(NP)]
                mask_sb = sbt2(stack, "mask_sb", [128, NREL, 512], BF16)
                B_mask = Buf("mask")
                P.dma("sp", mask_sb[:], maskb.rearrange("r k q -> k r q"), writes=[B_mask])
            if not masked:
                accD = [sbt2(stack, "accD%d" % i, [128, 512], F32) for i in range(2)]
                accP = [sbt2(stack, "accP%d" % i, [128, 512], F32) for i in range(2)]
                hi_sb = [sbt2(stack, "hi%d" % i, [128, 512], BF16) for i in range(2)]
                lo_sb = [sbt2(stack, "lo%d" % i, [128, 512], BF16) for i in range(2)]
                B_accD = [Buf("accD%d" % i) for i in range(2)]
                B_accP = [Buf("accP%d" % i) for i in range(2)]
                B_hi = [Buf("hi%d" % i) for i in range(2)]
                B_lo = [Buf("lo%d" % i) for i in range(2)]
            rd_sb = [sbt2(stack, "rd%d" % i, [128, 512], F32) for i in range(2)]
            o_sb = [sbt2(stack, "o%d" % i, [128, 512], F32) for i in range(2)]
            sq_sb = [sbt2(stack, "sq%d" % i, [128, 512], F32) for i in range(2)]
            ost = [sbt2(stack, "ost%d" % i, [128, 512], BF16) for i in range(2)]
            B_rd = [Buf("rd%d" % i) for i in range(2)]
            B_o = [Buf("o%d" % i) for i in range(2)]
            B_sq = [Buf("sq%d" % i) for i in range(2)]
            B_ost = [Buf("ost%d" % i) for i in range(2)]
            steps = []
            kvslot = {}
            nkv = 0
            for (mh, qT_ap, kT_ap, v_ap, kvkey) in heads:
                for qb in range(NQB):
                    if masked:
                        kcs = [kc for kc in range(qb * 4 - 8, qb * 4 + 12) if 0 <= kc < NT]
                    else:
                        kcs = list(range(NT))
                    for j, kc in enumerate(kcs):
                        steps.append(dict(mh=mh, qT=qT_ap, kT=kT_ap, v=v_ap, kvkey=kvkey, qb=qb, kc=kc,
                                          first=(j == 0), last=(j == len(kcs) - 1)))
            state = {"kv": None, "kvn": 0, "qn": 0, "blk": 0}
            cur_kv = {}
            cur_q = {}

            kv_slot = {}
            q_slot = {}
            first_of_block = [i for i, st_ in enumerate(steps) if st_["first"]]
            block_of = {}
            for bi, i0 in enumerate(first_of_block):
                block_of[i0] = bi

            def ensure_loaded(bi):
                if bi >= len(first_of_block):
                    return
                st_ = steps[first_of_block[bi]]
                if st_["kvkey"] not in kv_slot:
                    s = state["kvn"] % 2
                    state["kvn"] += 1
                    kv_slot[st_["kvkey"]] = s
                    P.dma("sp", kt_sb[s][:], st_["kT"], writes=[B_kv[s]])
                    P.dma("sp", v_sb[s][:], st_["v"].rearrange("(c p) d -> p c d", p=128), writes=[B_kv[s]])
                key = (st_["mh"], st_["qb"])
                if key not in q_slot:
                    s = state["qn"] % NQ
                    state["qn"] += 1
                    q_slot[key] = s
                    P.dma("sp", qt_sb[s][:], st_["qT"][:, st_["qb"] * 512:(st_["qb"] + 1) * 512], writes=[B_qt[s]])

            def issue_loads(st_, n):
                if st_["first"]:
                    ensure_loaded(block_of[n])
                    ensure_loaded(block_of[n] + 1)
                st_["kvs"] = kv_slot[st_["kvkey"]]
                st_["qs"] = q_slot[(st_["mh"], st_["qb"])]

            def emit_scores(n):
                st_ = steps[n]
                issue_loads(st_, n)
                bk = (0, 1, 2, 7)[n % 4]
                ps_ = n % NP
                kvs, qs, kc = st_["kvs"], st_["qs"], st_["kc"]
                P.op("pe", I("matmul", pbank[bk][:, :], lhsT=kt_sb[kvs][:, kc * 128:(kc + 1) * 128], rhs=qt_sb[qs][:],
                                              start=True, stop=True),
                     reads=[B_kv[kvs], B_qt[qs]], writes=[B_bank[bk]])
                if masked:
                    rel = kc - st_["qb"] * 4 + 8
                    P.op("act", I("activation", out=pe_sb[ps_][:], in_=pbank[bk][:, :], func=AF.Exp),
                         reads=[B_bank[bk]], writes=[B_pe[ps_]])
                    P.op("dve",
                         I("tensor_tensor", out=pt_sb[ps_][:], in0=pe_sb[ps_][:], in1=mask_sb[:, rel, :], op=ALU.mult),
                         reads=[B_pe[ps_], B_mask], writes=[B_pt[ps_]])
                else:
                    P.op("act", I("activation", out=pt_sb[ps_][:], in_=pbank[bk][:, :], func=AF.Exp),
                         reads=[B_bank[bk]], writes=[B_pt[ps_]])

            deferred = []
            deferred2 = []

            def finish(b2, bo, bd, mh, qb):
                if not masked:
                    fd = [I("matmul", pbank[bd][:, :], lhsT=ones_bf[:], rhs=hi_sb[b2][:], start=False, stop=False),
                          I("matmul", pbank[bd][:, :], lhsT=ones_bf[:], rhs=lo_sb[b2][:], start=False, stop=True)]
                    P.group("pe", fd, reads=[B_hi[b2], B_lo[b2], B_ones], writes=[B_bank[bd]])
                P.op("act", I("activation", out=rd_sb[b2][:], in_=pbank[bd][:, :], func=AF.Ln), reads=[B_bank[bd]], writes=[B_rd[b2]])
                P.op("act", I("activation", out=rd_sb[b2][:], in_=rd_sb[b2][:], func=AF.Exp, scale=-1.0), writes=[B_rd[b2]])
                P.op("dve", I("tensor_tensor", out=o_sb[b2][:], in0=pbank[bo][:, :], in1=rd_sb[b2][:], op=ALU.mult),
                     reads=[B_bank[bo], B_rd[b2]], writes=[B_o[b2]])
                fe = "pool" if masked else "dve"
                P.op(fe, I("tensor_tensor", out=sq_sb[b2][:], in0=o_sb[b2][:], in1=o_sb[b2][:], op=ALU.mult),
                     reads=[B_o[b2]], writes=[B_sq[b2]])
                P.op(fe, I("tensor_tensor", out=ost[b2][:], in0=o_sb[b2][:], in1=gout_sb[:, mh:mh + 1].to_broadcast([128, 512]), op=ALU.mult),
                     reads=[B_o[b2], B_gout], writes=[B_ost[b2]])
                P.dma("sp", mixT[mh, :, qb * 512:(qb + 1) * 512], ost[b2][:], reads=[B_ost[b2]])
                deferred2.append([6, (b2, bd, mh, qb)])

            def finish2(b2, bd, mh, qb):
                fns2 = [I("matmul", pbank[bd][:, t:t + 1], lhsT=sq_sb[b2][:, t * 128:(t + 1) * 128], rhs=ones_f[:],
                          start=True, stop=True) for t in range(4)]
                P.group("pe", fns2, reads=[B_sq[b2], B_onesf], writes=[B_bank[bd]])
                P.op("dve", I("tensor_copy", out=ssq[:, qb * 4:(qb + 1) * 4, mh], in_=pbank[bd][:, 0:4]),
                     reads=[B_bank[bd]], writes=[B_ssq])

            def emit_pv(n):
                st_ = steps[n]
                ps_ = n % NP
                if st_["first"]:
                    st_["blk"] = state["blk"]
                    st_["j"] = 0
                    state["blk"] += 1
                else:
                    st_["blk"] = steps[n - 1]["blk"]
                    st_["j"] = steps[n - 1]["j"] + 1
                b2 = st_["blk"] % 2
                bo, bd = 3 + b2, 5 + b2
                kvs, kc, j = st_["kvs"], st_["kc"], st_["j"]
                if masked:
                    fns = [I("matmul", pbank[bo][:, :], lhsT=v_sb[kvs][:, kc, :], rhs=pt_sb[ps_][:], start=st_["first"], stop=st_["last"]),
                           I("matmul", pbank[bd][:, :], lhsT=ones_bf[:], rhs=pt_sb[ps_][:], start=st_["first"], stop=st_["last"])]
                    P.group("pe", fns, reads=[B_kv[kvs], B_pt[ps_], B_ones], writes=[B_bank[bo], B_bank[bd]])
                else:
                    on_pe = (j % 2 == 1)
                    fns = [I("matmul", pbank[bo][:, :], lhsT=v_sb[kvs][:, kc, :], rhs=pt_sb[ps_][:], start=st_["first"], stop=st_["last"])]
                    wr = [B_bank[bo]]
                    if on_pe:
                        fns.append(I("matmul", pbank[bd][:, :], lhsT=ones_bf[:], rhs=pt_sb[ps_][:], start=(j == 1), stop=False))
                        wr.append(B_bank[bd])
                    P.group("pe", fns, reads=[B_kv[kvs], B_pt[ps_], B_ones], writes=wr)
                    if not on_pe:
                        if j == 0:
                            P.op("dve", I("tensor_copy", out=accD[b2][:], in_=pt_sb[ps_][:]), reads=[B_pt[ps_]], writes=[B_accD[b2]])
                        else:
                            P.op("dve", I("tensor_tensor", out=accD[b2][:], in0=accD[b2][:], in1=pt_sb[ps_][:], op=ALU.add),
                                 reads=[B_pt[ps_]], writes=[B_accD[b2]])
                    if st_["last"]:
                        P.op("dve", I("tensor_copy", out=hi_sb[b2][:], in_=accD[b2][:]), reads=[B_accD[b2]], writes=[B_hi[b2]])
                        P.op("dve", I("tensor_tensor", out=lo_sb[b2][:], in0=accD[b2][:], in1=hi_sb[b2][:], op=ALU.subtract),
                             reads=[B_accD[b2], B_hi[b2]], writes=[B_lo[b2]])
                if st_["last"]:
                    deferred.append([2, (b2, bo, bd, st_["mh"], st_["qb"])])

            LOOK = 3
            for n in range(min(LOOK, len(steps))):
                emit_scores(n)
            for n in range(len(steps)):
                if n + LOOK < len(steps):
                    emit_scores(n + LOOK)
                emit_pv(n)
                for d_ in deferred + deferred2:
                    d_[0] -= 1
                while deferred2 and deferred2[0][0] <= 0:
                    finish2(*deferred2.pop(0)[1])
                while deferred and deferred[0][0] <= 0:
                    finish(*deferred.pop(0)[1])
            while deferred:
                finish(*deferred.pop(0)[1])
            while deferred2:
                finish2(*deferred2.pop(0)[1])

        emit_casts()
        if 2 in phases:
            with ExitStack() as ps2:
                heads = []
                for h in range(NHA):
                    g = h // 4
                    heads.append((h, qaT[h], kaT[g], va[:, g * 128:(g + 1) * 128], ("a", g)))
                attention(ps2, heads, masked=False)
            P.barrier()
        if 3 in phases:
            with ExitStack() as ps3:
                heads = []
                for h in range(NHB):
                    heads.append((8 + h, qbT[h], kbT[h], vb[:, h * 128:(h + 1) * 128], ("b", h)))
                attention(ps3, heads, masked=True)
            P.barrier()

        emit_casts()
        if 4 in phases:
            with ExitStack() as ps4:
                NRING = 4
                ring = [sbt(ps4, "ring%d" % i, [128, 8192], BF16) for i in range(NRING)]
                B_ring = [Buf("ring%d" % i) for i in range(NRING)]
                mix_sbs = [sbt(ps4, "mix_sb0", [128, 16, 512], BF16)] * 2
                B_mixs = [Buf("mix0")] * 2
                gffn_sb = sbt(ps4, "gffn_sb", [128, 16], F32)
                gfin_sb = sbt(ps4, "gfin_sb", [128, D], F32)
                B_gffn, B_gfin = Buf("gffn"), Buf("gfin")
                P.dma("sp", gffn_sb[:], g_ffn, writes=[B_gffn])
                P.dma("sp", gfin_sb[:], g_fin, writes=[B_gfin])
                xres = [sbt(ps4, "xres%d" % i, [128, D], F32) for i in range(4)]
                B_xres = [Buf("xres%d" % i) for i in range(4)]
                h2 = [sbt(ps4, "h2_%d" % i, [128, D], BF16) for i in range(4)]
                B_h2 = [Buf("h2_%d" % i) for i in range(4)]
                h2T = sbt(ps4, "h2T", [128, 16, 512], BF16)
                B_h2T = [Buf("h2T_%d" % t) for t in range(4)]
                actT = sbt(ps4, "actT", [128, NFC, 512], BF16)
                B_act = [Buf("act%d" % f) for f in range(NFC)]
                st2 = [sbt(ps4, "st2_%d" % i, [128, 16], F32) for i in range(2)]
                B_st2 = [Buf("st2_%d" % i) for i in range(2)]
                st3 = [sbt(ps4, "st3_%d" % i, [128, 16], F32) for i in range(2)]
                B_st3 = [Buf("st3_%d" % i) for i in range(2)]
                sg = [sbt(ps4, "sg%d" % i, [128, 512], F32) for i in range(2)]
                B_sg = [Buf("sg%d" % i) for i in range(2)]

                items = []
                for tb in range(NQB):
                    for cb in range(4):
                        items.append(("wout", cb, 0))
                    for f2 in range(22):
                        items.append(("wgu", f2, 0))
                    for cb in range(4):
                        for pc in range(3):
                            items.append(("wd", cb, pc))
                rstate = {"loaded": 0, "used": 0}

                def ring_prefetch(upto):
                    while rstate["loaded"] < min(upto, len(items)):
                        n = rstate["loaded"]
                        kind, a, b = items[n]
                        s = n % NRING
                        if kind == "wout":
                            src = wout_s[a].rearrange("p h c -> p (h c)")
                        elif kind == "wgu":
                            src = wgu_s[a].rearrange("p g k c -> p (g k c)")
                        else:
                            nfp = 16 if b < 2 else NFC - 32
                            src = wd_s[a, b, :, 0:nfp, :].rearrange("p f c -> p (f c)")
                        P.dma("pool", ring[s][:, 0:src.shape[1]], src, writes=[B_ring[s]], extra=[cast_state["tok"]])
                        rstate["loaded"] += 1

                def ring_next(kind):
                    n = rstate["used"]
                    assert items[n][0] == kind, (items[n], kind)
                    ring_prefetch(n + NRING)
                    rstate["used"] += 1
                    s = n % NRING
                    return ring[s], B_ring[s]

                tp_bank4 = [pbank[4][:].bitcast(BF16), pbank[5][:].bitcast(BF16)]
                c4 = {"op": 0, "gu": 0}
                ring_prefetch(NRING - 1)
                for tb in range(NQB):
                    t0 = tb * 512
                    s2 = tb % 2
                    mix_sb, B_mix = mix_sbs[tb % 2], B_mixs[tb % 2]
                    if tb == 0:
                        P.dma("sp", mix_sb[:], mixT[:, :, t0:t0 + 512].rearrange("h d t -> d h t"), writes=[B_mix])
                    for t in range(4):
                        r0 = t0 + t * 128
                        P.dma("sp", xres[t][:], x[r0:r0 + 128, :], writes=[B_xres[t]])
                    for t in range(4):
                        tt = tb * 4 + t
                        for ab in range(2):
                            P.op("dve", I("tensor_reduce",
                                out=st2[s2][:, t * 2 + ab:t * 2 + ab + 1], in_=ssq[:, tt, ab * 8:(ab + 1) * 8], axis=AX.X, op=ALU.add),
                                reads=[B_ssq], writes=[B_st2[s2]])
                    P.op("dve", I("tensor_scalar", out=st2[s2][:, 0:8], in0=st2[s2][:, 0:8], scalar1=1.0 / 1024, scalar2=EPS,
                                                          op0=ALU.mult, op1=ALU.add), writes=[B_st2[s2]])
                    P.op("act", I("activation", out=st2[s2][:, 0:8], in_=st2[s2][:, 0:8], func=AF.Sqrt), writes=[B_st2[s2]])
                    P.op("dve", I("reciprocal", out=st2[s2][:, 8:16], in_=st2[s2][:, 0:8]), writes=[B_st2[s2]])
                    for cb in range(4):
                        wt, bw = ring_next("wout")
                        wv = wt[:].rearrange("p (h c) -> p h c", h=16)
                        cs = slice(cb * 512, (cb + 1) * 512)
                        for t in range(4):
                            pa = (c4["op"] % 2) * 2
                            c4["op"] += 1
                            pbk = pa + 1
                            fa = [I("matmul", pbank[pa][:, :], lhsT=mix_sb[:, h, t * 128:(t + 1) * 128],
                                                                           rhs=wv[:, h, :], start=(h == 0), stop=(h == 7)) for h in range(8)]
                            fb = [I("matmul", pbank[pbk][:, :], lhsT=mix_sb[:, h, t * 128:(t + 1) * 128],
                                                                             rhs=wv[:, h, :], start=(h == 8), stop=(h == 15)) for h in range(8, 16)]
                            P.group("pe", fa + fb, reads=[B_mix, bw], writes=[B_bank[pa], B_bank[pbk]])
                            P.op("dve", I("scalar_tensor_tensor",
                                out=xres[t][:, cs], in0=pbank[pa][:, :], scalar=st2[s2][:, 8 + t * 2:9 + t * 2], in1=xres[t][:, cs],
                                op0=ALU.mult, op1=ALU.add), reads=[B_bank[pa], B_st2[s2]], writes=[B_xres[t]])
                            P.op("dve", I("scalar_tensor_tensor",
                                out=xres[t][:, cs], in0=pbank[pbk][:, :], scalar=st2[s2][:, 9 + t * 2:10 + t * 2], in1=xres[t][:, cs],
                                op0=ALU.mult, op1=ALU.add), reads=[B_bank[pbk], B_st2[s2]], writes=[B_xres[t]])
                            if cb == 3:
                                ss = st3[s2]
                                if t == 0:
                                    P.op("dve", I("memset", ss[:], 0.0), writes=[B_st3[s2]])
                                P.op("act", I("activation", out=h2[t][:], in_=xres[t][:], func=AF.Square, accum_out=ss[:, t:t + 1]),
                                     reads=[B_xres[t]], writes=[B_h2[t], B_st3[s2]])
                                P.op("dve", I("tensor_scalar", out=ss[:, 4 + t:5 + t], in0=ss[:, t:t + 1], scalar1=1.0 / D, scalar2=EPS,
                                              op0=ALU.mult, op1=ALU.add), writes=[B_st3[s2]])
                                P.op("act", I("activation", out=ss[:, 4 + t:5 + t], in_=ss[:, 4 + t:5 + t], func=AF.Sqrt), writes=[B_st3[s2]])
                                P.op("dve", I("reciprocal", out=ss[:, 4 + t:5 + t], in_=ss[:, 4 + t:5 + t]), writes=[B_st3[s2]])
                                P.op("dve", I("tensor_scalar", out=h2[t][:], in0=xres[t][:], scalar1=ss[:, 4 + t:5 + t], scalar2=None,
                                              op0=ALU.mult), reads=[B_xres[t], B_st3[s2]], writes=[B_h2[t]])
                    for t in range(4):
                        hs = t
                        for half in range(2):
                            fns = [I("transpose", tp_bank4[half][:, kk * 128:(kk + 1) * 128],
                                     h2[hs][:, (half * 8 + kk) * 128:(half * 8 + kk + 1) * 128], ident[:]) for kk in range(8)]
                            P.group("pe", fns, reads=[B_h2[hs], B_ident], writes=[B_bank[4 + half]])
                            src = tp_bank4[half][:, :].rearrange("p (k t) -> p k t", k=8)
                            gm = gffn_sb[:, half * 8:(half + 1) * 8].unsqueeze(2).to_broadcast([128, 8, 128])
                            P.op("dve", I("tensor_tensor", out=h2T[:, half * 8:(half + 1) * 8, t * 128:(t + 1) * 128], in0=src, in1=gm, op=ALU.mult),
                                 reads=[B_bank[4 + half], B_gffn], writes=[B_h2T[t]])
                    if tb + 1 < NQB:
                        P.dma("sp", mix_sbs[(tb + 1) % 2][:], mixT[:, :, t0 + 512:t0 + 1024].rearrange("h d t -> d h t"),
                              writes=[B_mixs[(tb + 1) % 2]])
                    for f2 in range(22):
                        wt, bw = ring_next("wgu")
                        wv = wt[:].rearrange("p (g k c) -> p g k c", g=2, k=16)
                        for j in range(2):
                            fc = f2 * 2 + j
                            pg = (c4["gu"] % 2) * 2
                            c4["gu"] += 1
                            pu = pg + 1
                            fg = [I("matmul", pbank[pg][:, :], lhsT=wv[:, 0, kc, j * 128:(j + 1) * 128],
                                                                             rhs=h2T[:, kc, :], start=(kc == 0), stop=(kc == 15)) for kc in range(16)]
                            fu = [I("matmul", pbank[pu][:, :], lhsT=wv[:, 1, kc, j * 128:(j + 1) * 128],
                                                                             rhs=h2T[:, kc, :], start=(kc == 0), stop=(kc == 15)) for kc in range(16)]
                            P.group("pe", fg, reads=B_h2T + [bw], writes=[B_bank[pg]])
                            P.group("pe", fu, reads=B_h2T + [bw], writes=[B_bank[pu]])
                            gs = fc % 2
                            P.op("act", I("activation", out=sg[gs][:], in_=pbank[pg][:, :], func=AF.Silu),
                                 reads=[B_bank[pg]], writes=[B_sg[gs]])
                            P.op("dve", I("tensor_tensor", out=actT[:, fc, :], in0=pbank[pu][:, :], in1=sg[gs][:],
                                                                                     op=ALU.mult),
                                 reads=[B_bank[pu], B_sg[gs]], writes=[B_act[fc]])
                    for cb in range(4):
                        cs = slice(cb * 512, (cb + 1) * 512)
                        for pc in range(3):
                            wt, bw = ring_next("wd")
                            wv = wt[:].rearrange("p (f c) -> p f c", f=16)
                            nf = 16 if pc < 2 else NFC - 32
                            for t in range(4):
                                fns = [I("matmul",
                                    pbank[4 + t][:, :], lhsT=actT[:, pc * 16 + f, t * 128:(t + 1) * 128], rhs=wv[:, f, :],
                                    start=(pc == 0 and f == 0), stop=(pc == 2 and f == nf - 1)) for f in range(nf)]
                                P.group("pe", fns, reads=B_act[pc * 16:pc * 16 + nf] + [bw], writes=[B_bank[4 + t]])
                        for t in range(4):
                            P.op("dve", I("tensor_tensor", out=xres[t][:, cs], in0=pbank[4 + t][:, :], in1=xres[t][:, cs],
                                                                            op=ALU.add),
                                 reads=[B_bank[4 + t]], writes=[B_xres[t]])
                    ss = st3[s2]
                    for t in range(4):
                        r0 = t0 + t * 128
                        hs = t % 2
                        P.op("act", I("activation", out=h2[hs][:], in_=xres[t][:], func=AF.Square,
                                                                           accum_out=ss[:, 8 + t:9 + t]),
                             reads=[B_xres[t]], writes=[B_h2[hs], B_st3[s2]])
                        P.op("dve", I("tensor_scalar", out=ss[:, 12 + t:13 + t], in0=ss[:, 8 + t:9 + t], scalar1=1.0 / D,
                                                                        scalar2=EPS, op0=ALU.mult, op1=ALU.add), writes=[B_st3[s2]])
                        P.op("act", I("activation", out=ss[:, 12 + t:13 + t], in_=ss[:, 12 + t:13 + t], func=AF.Sqrt),
                             writes=[B_st3[s2]])
                        P.op("dve", I("reciprocal", out=ss[:, 12 + t:13 + t], in_=ss[:, 12 + t:13 + t]), writes=[B_st3[s2]])
                        P.op("dve", I("scalar_tensor_tensor", out=xres[t][:], in0=xres[t][:], scalar=ss[:, 12 + t:13 + t],
                                                                               in1=gfin_sb[:], op0=ALU.mult, op1=ALU.mult),
                             reads=[B_gfin, B_st3[s2]], writes=[B_xres[t]])
                        P.dma("sp", out[r0:r0 + 128, :], xres[t][:], reads=[B_xres[t]])
        P.fence_stores(engs=("sp",))
        P.run()
    return nc


def _host_inputs(inputs, S):
    f32 = np.float32
    ca, sa, cb, sb_, mask = _const_tables(S)

    def swap(g):
        return np.ascontiguousarray(g.reshape(2, 2, 32)[:, ::-1, :]).reshape(128)

    gq = np.asarray(inputs["g_q_a"], f32)[0]
    gk = np.asarray(inputs["g_k_a"], f32)[0]
    g_qk = np.ascontiguousarray(np.broadcast_to(np.stack([gq, swap(gq), gk, swap(gk)])[None], (128, 4, 128)))
    g_out = np.concatenate([np.asarray(inputs["g_out_a"], f32)[0], np.asarray(inputs["g_out_b"], f32)[0]])
    shared = {
        "w_in": np.ascontiguousarray(np.asarray(inputs["w_in"], f32)[0]),
        "w_out": np.ascontiguousarray(np.asarray(inputs["w_out"], f32)[0]),
        "w_gate_up": np.ascontiguousarray(np.asarray(inputs["w_gate_up"], f32)[0]),
        "w_down": np.ascontiguousarray(np.asarray(inputs["w_down"], f32)[0]),
        "g_mix": np.ascontiguousarray(np.asarray(inputs["g_mix"], f32)[0].reshape(16, 128).T),
        "g_ffn": np.ascontiguousarray(np.asarray(inputs["g_ffn"], f32)[0].reshape(16, 128).T),
        "g_final": np.ascontiguousarray(np.broadcast_to(np.asarray(inputs["g_final"], f32)[None, :], (128, D))),
        "g_qk": g_qk,
        "g_out": np.ascontiguousarray(g_out.reshape(16, 128).T),
        "ropa_c": ca, "ropa_s": sa, "ropb_c": cb, "ropb_s": sb_, "maskb": mask,
    }
    return shared


_NC_CACHE = {}


def kernel(**inputs):
    x = np.asarray(inputs["x"], np.float32)
    B, S, _ = x.shape
    shared = _host_inputs(inputs, S)
    if S not in _NC_CACHE:
        _NC_CACHE[S] = build(S)
    nc = _NC_CACHE[S]
    in_maps = []
    for b in range(B):
        m = dict(shared)
        m["x"] = np.ascontiguousarray(x[b])
        in_maps.append(m)
    res = run_bass_kernel_spmd(nc, in_maps, core_ids=list(range(B)))
    return np.stack([np.asarray(r["out"], np.float32) for r in res.results], axis=0)
```

```python
import numpy as np
from contextlib import ExitStack
import ml_dtypes
import concourse.bass as bass
import concourse.mybir as mybir
from concourse.bass_utils import run_bass_kernel_spmd

F32 = mybir.dt.float32
BF16 = mybir.dt.bfloat16
ALU = mybir.AluOpType
AF = mybir.ActivationFunctionType
AX = mybir.AxisListType

D = 2048
HD = 128
NHA = 8
NKV = 2
NHB = 8
DFF = 5632
PROJ = 4608
EPS = 1e-6
GRID_W = 64
NFC = DFF // 128
SCALE = HD ** -0.5
NREL = 20


def I(name, *a, **k):
    return lambda e: getattr(e, name)(*a, **k)


class Buf:
    __slots__ = ("name", "w", "r", "ld", "st", "excl")

    def __init__(self, name, excl=False):
        self.name = name
        self.excl = excl
        self.w = None
        self.r = {}
        self.ld = None
        self.st = None


class Prog:
    ENG = ("sp", "act", "dve", "pool", "pe")

    def __init__(self, nc, es):
        self.nc = nc
        self.es = es
        self.q = {e: [] for e in self.ENG}
        self.sem = {e: es.enter_context(nc.semaphore("prog_" + e)) for e in self.ENG}
        self.cnt = {e: 0 for e in self.ENG}
        self.waited = {e: {} for e in self.ENG}
        self.dcnt = {}
        self.nsem = 0
        self.stores = {}
        self.alldma = {}

    def new_sem(self, name):
        s = self.es.enter_context(self.nc.semaphore(name + "_%d" % self.nsem))
        self.nsem += 1
        self.dcnt[id(s)] = 0
        return s

    def wait(self, eng, toks):
        w = self.waited[eng]
        for t in toks:
            if t is None:
                continue
            sem, val = t
            if w.get(id(sem), 0) >= val:
                continue
            w[id(sem)] = val
            self.q[eng].append(lambda e, sem=sem, val=val: e.wait_ge(sem, val))

    def _deps(self, reads, writes, extra):
        toks = list(extra)
        for b in reads:
            toks.append(b.w)
            if b.excl:
                toks.extend(b.r.values())
        for b in writes:
            toks.append(b.w)
            toks.extend(b.r.values())
        return toks

    def _commit(self, tok, reads, writes):
        for b in reads:
            b.r[id(tok[0])] = tok
        for b in writes:
            b.w = tok
            b.r = {}

    def op(self, eng, fn, reads=(), writes=(), extra=()):
        return self.group(eng, [fn], reads, writes, extra)

    def group(self, eng, fns, reads=(), writes=(), extra=()):
        self.wait(eng, self._deps(reads, writes, extra))
        self.cnt[eng] += 1
        sem = self.sem[eng]
        tok = (sem, self.cnt[eng])
        for fn in fns[:-1]:
            self.q[eng].append(lambda e, fn=fn: fn(e))
        self.q[eng].append(lambda e, fn=fns[-1], sem=sem: fn(e).then_inc(sem, 1))
        self._commit(tok, reads, writes)
        return tok

    def dma(self, eng, out, in_, reads=(), writes=(), extra=(), sem=None):
        deps = self._deps(reads, writes, extra)
        if writes and writes[0].ld is not None:
            deps = [t for t in deps if t is None or t[0] is not writes[0].ld]
        self.wait(eng, deps)
        if sem is None:
            if writes:
                b = writes[0]
                if b.ld is None:
                    b.ld = self.new_sem("ld_" + b.name)
                sem = b.ld
            else:
                b = reads[0]
                if b.st is None:
                    b.st = self.new_sem("st_" + b.name)
                sem = b.st
        self.dcnt[id(sem)] += 16
        tok = (sem, self.dcnt[id(sem)])
        self.q[eng].append(lambda e, out=out, in_=in_, sem=sem: e.dma_start(out=out, in_=in_).then_inc(sem, 16))
        self._commit(tok, reads, writes)
        if reads and not writes:
            self.stores[id(sem)] = tok
        self.alldma[id(sem)] = tok
        return tok

    def barrier(self):
        toks = [(self.sem[e], self.cnt[e]) for e in self.ENG if self.cnt[e] > 0] + list(self.alldma.values())
        for e in self.ENG:
            self.wait(e, toks)

    def fence_stores(self, engs=("sp", "pool", "act")):
        toks = list(self.stores.values())
        for e in engs:
            self.wait(e, toks)

    def run(self):
        nc = self.nc
        with nc.Block() as block:
            @block.sync
            def _(e):
                for f in self.q["sp"]:
                    f(e)

            @block.scalar
            def _(e):
                for f in self.q["act"]:
                    f(e)

            @block.vector
            def _(e):
                for f in self.q["dve"]:
                    f(e)

            @block.gpsimd
            def _(e):
                for f in self.q["pool"]:
                    f(e)

            @block.tensor
            def _(e):
                for f in self.q["pe"]:
                    f(e)


def _const_tables(S):
    t = np.arange(S)
    half = HD // 2
    inv_a = (10000.0 ** (-(np.arange(0, half, 2, dtype=np.float32) / half))).astype(np.float32)
    row = (t // GRID_W).astype(np.float32)
    col = (t % GRID_W).astype(np.float32)
    ang_r = row[:, None] * inv_a[None, :]
    ang_c = col[:, None] * inv_a[None, :]
    ca = np.zeros((S, 2, 2, 32), np.float32)
    sa = np.zeros((S, 2, 2, 32), np.float32)
    for a, ang in enumerate((ang_r, ang_c)):
        c = np.cos(ang.astype(np.float32)).astype(np.float32)
        s = np.sin(ang.astype(np.float32)).astype(np.float32)
        ca[:, a, 0] = c
        ca[:, a, 1] = c
        sa[:, a, 0] = -s
        sa[:, a, 1] = s
    pr = HD // 4
    inv_b = (500000.0 ** (-(np.arange(0, pr, 2, dtype=np.float32) / pr))).astype(np.float32)
    ang_b = t.astype(np.float32)[:, None] * inv_b[None, :]
    cb = np.cos(ang_b).astype(np.float32)
    sb_ = np.sin(ang_b).astype(np.float32)
    cbt = np.concatenate([cb, cb], axis=1)
    sbt = np.concatenate([-sb_, sb_], axis=1)
    k = np.arange(128)[:, None]
    q = np.arange(512)[None, :]
    mask = np.zeros((NREL, 128, 512), np.float32)
    for r in range(NREL):
        d = 128 * (r - 8) + k - q
        ad = np.abs(d)
        mask[r] = (ad <= 64).astype(np.float32) + ((d % 4 == 0) & (ad <= 256)) + ((d % 16 == 0) & (ad <= 1024))
    return (ca.reshape(S, 128), sa.reshape(S, 128), cbt.astype(np.float32), sbt.astype(np.float32),
            mask.astype(ml_dtypes.bfloat16))


def build(S=4096, debug=False, phases=(1, 2, 3, 4)):
    NT = S // 128
    NQB = S // 512
    nc = bass.Bass("TRN2", target_bir_lowering=False)

    def din(name, shape, dt=F32):
        return nc.dram_tensor(name, list(shape), dt, kind="ExternalInput").ap()

    def dscr(name, shape, dt):
        if debug:
            return nc.dram_tensor(name, list(shape), dt, kind="ExternalOutput").ap()
        return nc.dram_tensor(name, list(shape), dt).ap()

    x = din("x", [S, D])
    w_in = din("w_in", [D, PROJ])
    w_out = din("w_out", [D, D])
    w_gu = din("w_gate_up", [D, 2 * DFF])
    w_dn = din("w_down", [DFF, D])
    g_mix = din("g_mix", [128, 16])
    g_ffn = din("g_ffn", [128, 16])
    g_fin = din("g_final", [128, D])
    g_qk = din("g_qk", [128, 4, 128])
    g_out = din("g_out", [128, 16])
    ropa_c = din("ropa_c", [S, 128])
    ropa_s = din("ropa_s", [S, 128])
    ropb_c = din("ropb_c", [S, 32])
    ropb_s = din("ropb_s", [S, 32])
    maskb = din("maskb", [NREL, 128, 512], BF16)
    out = nc.dram_tensor("out", [S, D], F32, kind="ExternalOutput").ap()

    qaT = dscr("qaT", [NHA, 128, S], BF16)
    kaT = dscr("kaT", [NKV, 128, S], BF16)
    va = dscr("va", [S, NKV * 128], BF16)
    qbT = dscr("qbT", [NHB, 128, S], BF16)
    kbT = dscr("kbT", [NHB, 128, S], BF16)
    vb = dscr("vb", [S, NHB * 128], BF16)
    mixT = dscr("mixT", [16, 128, S], BF16)
    wout_s = dscr("wout_s", [4, 128, 16, 512], BF16)
    wgu_s = dscr("wgu_s", [22, 128, 2, 16, 256], BF16)
    wd_s = dscr("wd_s", [4, 3, 128, 16, 512], BF16)

    with ExitStack() as es:
        P = Prog(nc, es)

        def sbt(stack, name, shape, dt):
            return stack.enter_context(nc.sbuf_tensor(name, list(shape), dt))

        def pst(stack, name, shape, dt):
            return stack.enter_context(nc.psum_tensor(name, list(shape), dt))

        ident = sbt(es, "ident", [128, 128], BF16)
        ones_bf = sbt(es, "ones_bf", [128, 128], BF16)
        ones_f = sbt(es, "ones_f", [128, 1], F32)
        gout_sb = sbt(es, "gout_sb", [128, 16], F32)
        ssq = sbt(es, "ssq", [128, NT, 16], F32)
        B_ident, B_ones, B_onesf, B_gout, B_ssq = Buf("ident"), Buf("ones"), Buf("onesf"), Buf("gout"), Buf("ssq")
        P.op("pool", I("memset", ident[:], 1.0), writes=[B_ident])
        P.op("pool", I("affine_select", out=ident[:], in_=ident[:], pattern=[[-1, 128]], compare_op=ALU.is_equal,
                                                fill=0.0, base=0, channel_multiplier=1), writes=[B_ident])
        P.op("pool", I("memset", ones_bf[:], 1.0), writes=[B_ones])
        P.op("pool", I("memset", ones_f[:], 1.0), writes=[B_onesf])

        P.dma("sp", gout_sb[:], g_out, writes=[B_gout])

        pbank = [pst(es, "pbank%d" % i, [128, 512], F32) for i in range(8)]
        B_bank = [Buf("bank%d" % i, excl=True) for i in range(8)]

        cast_sem = P.new_sem("cast")
        cast_state = {"tok": None, "done": False}

        cast_list = []
        if 4 in phases:
            for cb in range(4):
                cast_list.append((wout_s[cb], w_out[:, cb * 512:(cb + 1) * 512].rearrange("(h p) c -> p h c", p=128)))
            for f2 in range(22):
                for gu in range(2):
                    c0 = gu * DFF + f2 * 256
                    cast_list.append((wgu_s[f2, :, gu], w_gu[:, c0:c0 + 256].rearrange("(kc p) c -> p kc c", p=128)))
            for cb in range(4):
                for pc in range(3):
                    n = 16 if pc < 2 else NFC - 32
                    r0 = pc * 16 * 128
                    cast_list.append((wd_s[cb, pc, :, 0:n, :],
                                      w_dn[r0:r0 + n * 128, cb * 512:(cb + 1) * 512].rearrange("(fc p) c -> p fc c", p=128)))

        def cast_one():
            if cast_list:
                dst, src = cast_list.pop(0)
                cast_state["tok"] = P.dma("pool", dst, src, sem=cast_sem)

        def emit_casts():
            while cast_list:
                cast_one()

        if 1 in phases:
            with ExitStack() as ps1:
                w_sb = sbt(ps1, "w_sb", [128, 16, PROJ], BF16)
                B_wcb = [Buf("w_sb_cb%d" % cb) for cb in range(9)]
                for cb in range(9):
                    P.dma("pool", w_sb[:, :, cb * 512:(cb + 1) * 512],
                          w_in[:, cb * 512:(cb + 1) * 512].rearrange("(kc p) c -> p kc c", p=128), writes=[B_wcb[cb]])
                gmix_sb = sbt(ps1, "gmix_sb", [128, 16], F32)
                gqk_sb = sbt(ps1, "gqk_sb", [128, 4, 128], F32)
                B_gmix, B_gqk = Buf("gmix"), Buf("gqk")
                P.dma("sp", gmix_sb[:], g_mix, writes=[B_gmix])
                P.dma("sp", gqk_sb[:], g_qk, writes=[B_gqk])

                xb = [sbt(ps1, "xb%d" % i, [128, D], F32) for i in range(2)]
                B_x = [Buf("xb%d" % i) for i in range(2)]
                tabs = [sbt(ps1, "tabs%d" % i, [128, 320], F32) for i in range(2)]
                B_tabs = [Buf("tabs%d" % i) for i in range(2)]
                dtab = [sbt(ps1, "dtab%d" % i, [128, 4 * 128], F32) for i in range(2)]
                B_dtab = [Buf("dtab%d" % i) for i in range(2)]
                stat = [sbt(ps1, "stat%d" % i, [128, 8], F32) for i in range(2)]
                B_stat = [Buf("stat%d" % i) for i in range(2)]
                hb = [sbt(ps1, "hb0", [128, D], BF16)] * 2
                B_h = [Buf("hb0")] * 2
                neghalf = sbt(ps1, "neghalf", [128, 4], F32)
                B_neghalf = Buf("neghalf")
                P.op("pool", I("memset", neghalf[:], -0.5), writes=[B_neghalf])
                hT = [sbt(ps1, "hT%d" % i, [128, 16, 128], BF16) for i in range(2)]
                B_hT = [Buf("hT%d" % i) for i in range(2)]
                NSC = 2
                scr1 = [sbt(ps1, "scr1_%d" % i, [128, 512], F32) for i in range(NSC)]
                scr2 = [sbt(ps1, "scr2_%d" % i, [128, 512], F32) for i in range(NSC)]
                scr3 = [sbt(ps1, "scr3_%d" % i, [128, 512], F32) for i in range(NSC)]
                B_scr1 = [Buf("scr1_%d" % i) for i in range(NSC)]
                B_scr2 = [Buf("scr2_%d" % i) for i in range(NSC)]
                B_scr3 = [Buf("scr3_%d" % i) for i in range(NSC)]
                st4 = [sbt(ps1, "st4_%d" % i, [128, 8], F32) for i in range(NSC)]
                B_st4 = [Buf("st4_%d" % i) for i in range(NSC)]
                NSTG = 6
                stg = [sbt(ps1, "stg%d" % i, [128, 512], BF16) for i in range(NSTG)]
                B_stg = [Buf("stg%d" % i) for i in range(NSTG)]
                NTS = 3
                tst = [sbt(ps1, "tst%d" % i, [128, 4, 128], BF16) for i in range(NTS)]
                B_tst = [Buf("tst%d" % i) for i in range(NTS)]
                NVS = 2
                vst = [sbt(ps1, "vst%d" % i, [128, 512], BF16) for i in range(NVS)]
                B_vst = [Buf("vst%d" % i) for i in range(NVS)]

                tp_bank = [pbank[0][:].bitcast(BF16), pbank[1][:].bitcast(BF16)]

                def load_tile(i):
                    s = i % 2
                    r0 = i * 128
                    P.dma("sp", xb[s][:], x[r0:r0 + 128, :], writes=[B_x[s]])
                    P.dma("sp", tabs[s][:, 0:128], ropa_c[r0:r0 + 128, :], writes=[B_tabs[s]])
                    P.dma("sp", tabs[s][:, 128:256], ropa_s[r0:r0 + 128, :], writes=[B_tabs[s]])
                    P.dma("sp", tabs[s][:, 256:288], ropb_c[r0:r0 + 128, :], writes=[B_tabs[s]])
                    P.dma("sp", tabs[s][:, 288:320], ropb_s[r0:r0 + 128, :], writes=[B_tabs[s]])

                def norm_tile(i):
                    s = i % 2
                    P.op("dve", I("memset", stat[s][:], 0.0), writes=[B_stat[s]])
                    P.op("act", I("activation", out=hb[s][:], in_=xb[s][:], func=AF.Square, accum_out=stat[s][:, 0:1]),
                         reads=[B_x[s]], writes=[B_h[s], B_stat[s]])
                    P.op("dve", I("tensor_scalar", out=stat[s][:, 1:2], in0=stat[s][:, 0:1], scalar1=1.0 / D, scalar2=EPS,
                                                          op0=ALU.mult, op1=ALU.add), writes=[B_stat[s]])
                    P.op("act", I("activation", out=stat[s][:, 2:3], in_=stat[s][:, 1:2], func=AF.Sqrt), writes=[B_stat[s]])
                    P.op("dve", I("reciprocal", out=stat[s][:, 3:4], in_=stat[s][:, 2:3]), writes=[B_stat[s]])
                    P.op("dve", I("tensor_scalar", out=hb[s][:], in0=xb[s][:], scalar1=stat[s][:, 3:4], scalar2=None,
                                                          op0=ALU.mult), reads=[B_x[s], B_stat[s]], writes=[B_h[s]])
                    t_, d_ = tabs[s], dtab[s]
                    P.op("pool", I("tensor_tensor", out=d_[:, 0:128], in0=t_[:, 0:128], in1=gqk_sb[:, 0, :], op=ALU.mult),
                         reads=[B_tabs[s], B_gqk], writes=[B_dtab[s]])
                    P.op("pool", I("tensor_tensor", out=d_[:, 128:256], in0=t_[:, 128:256], in1=gqk_sb[:, 1, :], op=ALU.mult),
                         reads=[B_tabs[s], B_gqk], writes=[B_dtab[s]])
                    P.op("pool", I("tensor_tensor", out=d_[:, 256:384], in0=t_[:, 0:128], in1=gqk_sb[:, 2, :], op=ALU.mult),
                         reads=[B_tabs[s], B_gqk], writes=[B_dtab[s]])
                    P.op("pool", I("tensor_tensor", out=d_[:, 384:512], in0=t_[:, 128:256], in1=gqk_sb[:, 3, :], op=ALU.mult),
                         reads=[B_tabs[s], B_gqk], writes=[B_dtab[s]])

                def transp_h(i):
                    s = i % 2
                    for half in range(2):
                        fns = []
                        for kk in range(8):
                            kc = half * 8 + kk
                            fns.append(I("transpose",
                                tp_bank[half][:, kk * 128:(kk + 1) * 128], hb[s][:, kc * 128:(kc + 1) * 128], ident[:]))
                        P.group("pe", fns, reads=[B_h[s], B_ident], writes=[B_bank[half]])
                        src = tp_bank[half][:, :].rearrange("p (k t) -> p k t", k=8)
                        gm = gmix_sb[:, half * 8:(half + 1) * 8].unsqueeze(2).to_broadcast([128, 8, 128])
                        P.op("dve", I("tensor_tensor",
                            out=hT[s][:, half * 8:(half + 1) * 8, :], in0=src, in1=gm, op=ALU.mult),
                            reads=[B_bank[half], B_gmix], writes=[B_hT[s]])

                cnt = {"scr": 0, "stg": 0, "tst": 0, "vst": 0, "pj": 0, "tq": 0}
                pending = []

                def rope_norm_block(pb_ap, nh, t1off, rstd_mode, dst_slot):
                    k = cnt["scr"] % NSC
                    cnt["scr"] += 1
                    s_ = cur["s"]
                    bank = cur["bank"]
                    d_ = dtab[s_]
                    W = nh * 128
                    P.op("act", I("activation", out=scr2[k][:, 0:W], in_=pb_ap, func=AF.Copy), reads=[bank], writes=[B_scr2[k]])
                    P.op("pool", I("memset", st4[k][:], 0.0), writes=[B_st4[k]])
                    for h in range(nh):
                        P.op("act", I("activation", out=scr1[k][:, h * 128:(h + 1) * 128], in_=scr2[k][:, h * 128:(h + 1) * 128], func=AF.Square,
                                      accum_out=st4[k][:, h:h + 1]), reads=[B_scr2[k]], writes=[B_scr1[k], B_st4[k]])
                    if rstd_mode == "q":
                        P.op("pool", I("tensor_scalar", out=st4[k][:, 0:nh], in0=st4[k][:, 0:nh], scalar1=1.0, scalar2=128 * EPS,
                                       op0=ALU.mult, op1=ALU.add), writes=[B_st4[k]])
                    else:
                        P.op("pool", I("tensor_scalar", out=st4[k][:, 0:nh], in0=st4[k][:, 0:nh], scalar1=1.0 / 128, scalar2=EPS,
                                       op0=ALU.mult, op1=ALU.add), writes=[B_st4[k]])
                    P.op("pool", I("tensor_tensor", out=st4[k][:, 4:4 + nh], in0=st4[k][:, 0:nh], in1=neghalf[:, 0:nh], op=ALU.pow),
                         reads=[B_neghalf], writes=[B_st4[k]])
                    xs = scr2[k][:, 0:W]
                    T1 = d_[:, t1off:t1off + 128].unsqueeze(1).to_broadcast([128, nh, 128])
                    P.op("dve", I("tensor_tensor", out=scr1[k][:, 0:W].rearrange("p (h d) -> p h d", h=nh),
                                  in0=xs.rearrange("p (h d) -> p h d", h=nh), in1=T1, op=ALU.mult),
                         reads=[B_scr2[k], B_dtab[s_]], writes=[B_scr1[k]])
                    x5 = xs.rearrange("p (h a f j) -> p h a f j", h=nh, a=2, f=2)
                    o5 = scr3[k][:, 0:W].rearrange("p (h a f j) -> p h a f j", h=nh, a=2, f=2)
                    T2 = d_[:, t1off + 128:t1off + 256].rearrange("p (a f j) -> p a f j", a=2, f=2)
                    for f in range(2):
                        tb = T2[:, :, f, :].unsqueeze(1).to_broadcast([128, nh, 2, 32])
                        P.op("dve", I("tensor_tensor", out=o5[:, :, :, f, :], in0=x5[:, :, :, 1 - f, :], in1=tb, op=ALU.mult),
                             reads=[B_scr2[k], B_dtab[s_]], writes=[B_scr3[k]])
                    P.op("dve", I("tensor_tensor", out=scr1[k][:, 0:W], in0=scr1[k][:, 0:W], in1=scr3[k][:, 0:W], op=ALU.add),
                         reads=[B_scr3[k]], writes=[B_scr1[k]])
                    rb = st4[k][:, 4:4 + nh].unsqueeze(2).to_broadcast([128, nh, 128])
                    P.op("dve", I("tensor_tensor", out=stg[dst_slot][:, 0:W].rearrange("p (h d) -> p h d", h=nh),
                                  in0=scr1[k][:, 0:W].rearrange("p (h d) -> p h d", h=nh), in1=rb, op=ALU.mult),
                         reads=[B_scr1[k], B_st4[k]], writes=[B_stg[dst_slot]])

                def rope_part_block(pb_ap, scale, coff, dst_slot):
                    k = cnt["scr"] % NSC
                    cnt["scr"] += 1
                    s_ = cur["s"]
                    bank = cur["bank"]
                    P.op("act", I("activation", out=stg[dst_slot][:], in_=pb_ap, func=AF.Copy, scale=float(scale)),
                         reads=[bank], writes=[B_stg[dst_slot]])
                    pb3 = pb_ap.rearrange("p (h d) -> p h d", h=4)
                    xr = scr3[k][:, 0:128].rearrange("p (h j) -> p h j", h=4)
                    P.op("act", I("activation", out=xr, in_=pb3[:, :, 0:32], func=AF.Copy, scale=float(scale)),
                         reads=[bank], writes=[B_scr3[k]])
                    ctab = tabs[s_][:, 256:288]
                    stab = tabs[s_][:, 288:320]
                    tb_ = B_tabs[s_]
                    r1 = scr1[k][:, 0:128].rearrange("p (h j) -> p h j", h=4)
                    r2 = scr2[k][:, 0:128].rearrange("p (h j) -> p h j", h=4)
                    P.op("dve", I("tensor_tensor", out=r1, in0=xr, in1=ctab.unsqueeze(1).to_broadcast([128, 4, 32]), op=ALU.mult),
                         reads=[B_scr3[k], tb_], writes=[B_scr1[k]])
                    for f in range(2):
                        P.op("dve", I("tensor_tensor", out=r2[:, :, f * 16:(f + 1) * 16], in0=xr[:, :, (1 - f) * 16:(2 - f) * 16],
                                      in1=stab[:, f * 16:(f + 1) * 16].unsqueeze(1).to_broadcast([128, 4, 16]), op=ALU.mult),
                             reads=[B_scr3[k], tb_], writes=[B_scr2[k]])
                    P.op("dve", I("tensor_tensor", out=stg[dst_slot][:].rearrange("p (h d) -> p h d", h=4)[:, :, 0:32], in0=r1, in1=r2, op=ALU.add),
                         reads=[B_scr1[k], B_scr2[k]], writes=[B_stg[dst_slot]])

                def out_transposes(slot, nh, dst_fn):
                    tb = 5 + cnt["tq"] % 2
                    cnt["tq"] += 1
                    tpv = pbank[tb][:].bitcast(BF16)
                    fns = [I("transpose", tpv[:, h * 128:(h + 1) * 128], stg[slot][:, h * 128:(h + 1) * 128], ident[:])
                           for h in range(nh)]
                    P.group("pe", fns, reads=[B_stg[slot], B_ident], writes=[B_bank[tb]])
                    ts = cnt["tst"] % NTS
                    cnt["tst"] += 1
                    P.op("act", I("activation", out=tst[ts][:, 0:nh, :], in_=tpv[:, 0:nh * 128].rearrange("p (h t) -> p h t", h=nh),
                                                       func=AF.Copy), reads=[B_bank[tb]], writes=[B_tst[ts]])
                    P.dma("sp", dst_fn(), tst[ts][:, 0:nh, :], reads=[B_tst[ts]])

                cur = {}

                def proj_tile(i, mid1=None, mid2=None):
                    s = i % 2
                    r0 = i * 128
                    cur["s"] = s
                    for cb in range(9):
                        bk = 2 + cnt["pj"] % 3
                        cnt["pj"] += 1
                        cur["bank"] = B_bank[bk]
                        pb_ap = pbank[bk][:, :]
                        fns = [I("matmul", pbank[bk][:, :], lhsT=hT[s][:, kc, :],
                                                                      rhs=w_sb[:, kc, cb * 512:(cb + 1) * 512],
                                                                      start=(kc == 0), stop=(kc == 15)) for kc in range(16)]
                        P.group("pe", fns, reads=[B_hT[s], B_wcb[cb]], writes=[B_bank[bk]])
                        if cb in (0, 1):
                            sl = cnt["stg"] % NSTG
                            cnt["stg"] += 1
                            rope_norm_block(pb_ap, 4, 0, "q", sl)
                            pending.append((sl, 4, (lambda cb=cb, r0=r0: qaT[cb * 4:(cb + 1) * 4, :, r0:r0 + 128].rearrange("h d t -> d h t"))))
                        elif cb == 2:
                            sl = cnt["stg"] % NSTG
                            cnt["stg"] += 1
                            rope_norm_block(pbank[bk][:, 0:256], 2, 256, "k", sl)
                            pending.append((sl, 2, (lambda r0=r0: kaT[:, :, r0:r0 + 128].rearrange("h d t -> d h t"))))
                            vs = cnt["vst"] % NVS
                            cnt["vst"] += 1
                            P.op("act", I("activation", out=vst[vs][:, 0:256], in_=pbank[bk][:, 256:512], func=AF.Copy),
                                 reads=[B_bank[bk]], writes=[B_vst[vs]])
                            P.dma("sp", va[r0:r0 + 128, :], vst[vs][:, 0:256], reads=[B_vst[vs]])
                        elif cb in (3, 4, 5, 6):
                            sl = cnt["stg"] % NSTG
                            cnt["stg"] += 1
                            isq = cb in (3, 4)
                            rope_part_block(pb_ap, SCALE if isq else 1.0, 0, sl)
                            hb0 = (cb - 3) * 4 if isq else (cb - 5) * 4
                            tgt = qbT if isq else kbT
                            pending.append((sl, 4, (lambda tgt=tgt, hb0=hb0, r0=r0: tgt[hb0:hb0 + 4, :, r0:r0 + 128].rearrange("h d t -> d h t"))))
                        else:
                            vs = cnt["vst"] % NVS
                            cnt["vst"] += 1
                            P.op("act", I("activation", out=vst[vs][:], in_=pbank[bk][:, :], func=AF.Copy),
                                 reads=[B_bank[bk]], writes=[B_vst[vs]])
                            c0 = (cb - 7) * 512
                            P.dma("sp", vb[r0:r0 + 128, c0:c0 + 512], vst[vs][:], reads=[B_vst[vs]])
                        if cb == 1 and mid1 is not None:
                            mid1()
                            cur["s"] = s
                        if cb == 5 and mid2 is not None:
                            mid2()
                        while len(pending) > 4:
                            sl_, nh_, fn_ = pending.pop(0)
                            out_transposes(sl_, nh_, fn_)

                load_tile(0)
                norm_tile(0)
                transp_h(0)
                if NT > 1:
                    load_tile(1)
                for i in range(NT):
                    def mid1(i=i):
                        if i + 1 < NT:
                            norm_tile(i + 1)

                    def mid2(i=i):
                        if i + 1 < NT:
                            transp_h(i + 1)
                    proj_tile(i, mid1, mid2)
                    if i + 2 < NT:
                        load_tile(i + 2)
                while pending:
                    sl_, nh_, fn_ = pending.pop(0)
                    out_transposes(sl_, nh_, fn_)
            P.barrier()

        def attention(stack, heads, masked):
            pfx = "m" if masked else "u"
            _sbt = sbt

            def sbt2(stack_, name, shape, dt):
                return _sbt(stack_, pfx + name, shape, dt)
            kt_sb = [sbt2(stack, "kt%d" % i, [128, S], BF16) for i in range(2)]
            v_sb = [sbt2(stack, "v%d" % i, [128, NT, 128], BF16) for i in range(2)]
            B_kv = [Buf("kv%d" % i) for i in range(2)]
            NQ = 3
            qt_sb = [sbt2(stack, "qt%d" % i, [128, 512], BF16) for i in range(NQ)]
            B_qt = [Buf("qt%d" % i) for i in range(NQ)]
            NP = 8
            pt_sb = [sbt2(stack, "pt%d" % i, [128, 512], BF16) for i in range(NP)]
            B_pt = [Buf("pt%d" % i) for i in range(NP)]
            if masked:
                pe_sb = [sbt2(stack, "pe%d" % i, [128, 512], BF16) for i in range(NP)]
                B_pe = [Buf("pe%d" % i) for i in range(NP)]
                mask_sb = sbt2(stack, "mask_sb", [128, NREL, 512], BF16)
                B_mask = Buf("mask")
                P.dma("sp", mask_sb[:], maskb.rearrange("r k q -> k r q"), writes=[B_mask])
            if not masked:
                accD = [sbt2(stack, "accD%d" % i, [128, 512], F32) for i in range(2)]
                accP = [sbt2(stack, "accP%d" % i, [128, 512], F32) for i in range(2)]
                hi_sb = [sbt2(stack, "hi%d" % i, [128, 512], BF16) for i in range(2)]
                lo_sb = [sbt2(stack, "lo%d" % i, [128, 512], BF16) for i in range(2)]
                B_accD = [Buf("accD%d" % i) for i in range(2)]
                B_accP = [Buf("accP%d" % i) for i in range(2)]
                B_hi = [Buf("hi%d" % i) for i in range(2)]
                B_lo = [Buf("lo%d" % i) for i in range(2)]
            rd_sb = [sbt2(stack, "rd%d" % i, [128, 512], F32) for i in range(2)]
            o_sb = [sbt2(stack, "o%d" % i, [128, 512], F32) for i in range(2)]
            sq_sb = [sbt2(stack, "sq%d" % i, [128, 512], F32) for i in range(2)]
            ost = [sbt2(stack, "ost%d" % i, [128, 512], BF16) for i in range(2)]
            B_rd = [Buf("rd%d" % i) for i in range(2)]
            B_o = [Buf("o%d" % i) for i in range(2)]
            B_sq = [Buf("sq%d" % i) for i in range(2)]
            B_ost = [Buf("ost%d" % i) for i in range(2)]
            steps = []
            kvslot = {}
            nkv = 0
            for (mh, qT_ap, kT_ap, v_ap, kvkey) in heads:
                for qb in range(NQB):
                    if masked:
                        kcs = [kc for kc in range(qb * 4 - 8, qb * 4 + 12) if 0 <= kc < NT]
                    else:
                        kcs = list(range(NT))
                    for j, kc in enumerate(kcs):
                        steps.append(dict(mh=mh, qT=qT_ap, kT=kT_ap, v=v_ap, kvkey=kvkey, qb=qb, kc=kc,
                                          first=(j == 0), last=(j == len(kcs) - 1)))
            state = {"kv": None, "kvn": 0, "qn": 0, "blk": 0}
            cur_kv = {}
            cur_q = {}

            kv_slot = {}
            q_slot = {}
            first_of_block = [i for i, st_ in enumerate(steps) if st_["first"]]
            block_of = {}
            for bi, i0 in enumerate(first_of_block):
                block_of[i0] = bi

            def ensure_loaded(bi):
                if bi >= len(first_of_block):
                    return
                st_ = steps[first_of_block[bi]]
                if st_["kvkey"] not in kv_slot:
                    s = state["kvn"] % 2
                    state["kvn"] += 1
                    kv_slot[st_["kvkey"]] = s
                    P.dma("sp", kt_sb[s][:], st_["kT"], writes=[B_kv[s]])
                    P.dma("sp", v_sb[s][:], st_["v"].rearrange("(c p) d -> p c d", p=128), writes=[B_kv[s]])
                key = (st_["mh"], st_["qb"])
                if key not in q_slot:
                    s = state["qn"] % NQ
                    state["qn"] += 1
                    q_slot[key] = s
                    P.dma("sp", qt_sb[s][:], st_["qT"][:, st_["qb"] * 512:(st_["qb"] + 1) * 512], writes=[B_qt[s]])

            def issue_loads(st_, n):
                if st_["first"]:
                    ensure_loaded(block_of[n])
                    ensure_loaded(block_of[n] + 1)
                st_["kvs"] = kv_slot[st_["kvkey"]]
                st_["qs"] = q_slot[(st_["mh"], st_["qb"])]

            def emit_scores(n):
                st_ = steps[n]
                issue_loads(st_, n)
                bk = (0, 1, 2, 7)[n % 4]
                ps_ = n % NP
                kvs, qs, kc = st_["kvs"], st_["qs"], st_["kc"]
                P.op("pe", I("matmul", pbank[bk][:, :], lhsT=kt_sb[kvs][:, kc * 128:(kc + 1) * 128], rhs=qt_sb[qs][:],
                                              start=True, stop=True),
                     reads=[B_kv[kvs], B_qt[qs]], writes=[B_bank[bk]])
                if masked:
                    rel = kc - st_["qb"] * 4 + 8
                    P.op("act", I("activation", out=pe_sb[ps_][:], in_=pbank[bk][:, :], func=AF.Exp),
                         reads=[B_bank[bk]], writes=[B_pe[ps_]])
                    P.op("dve",
                         I("tensor_tensor", out=pt_sb[ps_][:], in0=pe_sb[ps_][:], in1=mask_sb[:, rel, :], op=ALU.mult),
                         reads=[B_pe[ps_], B_mask], writes=[B_pt[ps_]])
                else:
                    P.op("act", I("activation", out=pt_sb[ps_][:], in_=pbank[bk][:, :], func=AF.Exp),
                         reads=[B_bank[bk]], writes=[B_pt[ps_]])

            deferred = []
            deferred2 = []

            def finish(b2, bo, bd, mh, qb):
                if not masked:
                    fd = [I("matmul", pbank[bd][:, :], lhsT=ones_bf[:], rhs=hi_sb[b2][:], start=False, stop=False),
                          I("matmul", pbank[bd][:, :], lhsT=ones_bf[:], rhs=lo_sb[b2][:], start=False, stop=True)]
                    P.group("pe", fd, reads=[B_hi[b2], B_lo[b2], B_ones], writes=[B_bank[bd]])
                P.op("act", I("activation", out=rd_sb[b2][:], in_=pbank[bd][:, :], func=AF.Ln), reads=[B_bank[bd]], writes=[B_rd[b2]])
                P.op("act", I("activation", out=rd_sb[b2][:], in_=rd_sb[b2][:], func=AF.Exp, scale=-1.0), writes=[B_rd[b2]])
                P.op("dve", I("tensor_tensor", out=o_sb[b2][:], in0=pbank[bo][:, :], in1=rd_sb[b2][:], op=ALU.mult),
                     reads=[B_bank[bo], B_rd[b2]], writes=[B_o[b2]])
                fe = "pool" if masked else "dve"
                P.op(fe, I("tensor_tensor", out=sq_sb[b2][:], in0=o_sb[b2][:], in1=o_sb[b2][:], op=ALU.mult),
                     reads=[B_o[b2]], writes=[B_sq[b2]])
                P.op(fe, I("tensor_tensor", out=ost[b2][:], in0=o_sb[b2][:], in1=gout_sb[:, mh:mh + 1].to_broadcast([128, 512]), op=ALU.mult),
                     reads=[B_o[b2], B_gout], writes=[B_ost[b2]])
                P.dma("sp", mixT[mh, :, qb * 512:(qb + 1) * 512], ost[b2][:], reads=[B_ost[b2]])
                deferred2.append([6, (b2, bd, mh, qb)])

            def finish2(b2, bd, mh, qb):
                fns2 = [I("matmul", pbank[bd][:, t:t + 1], lhsT=sq_sb[b2][:, t * 128:(t + 1) * 128], rhs=ones_f[:],
                          start=True, stop=True) for t in range(4)]
                P.group("pe", fns2, reads=[B_sq[b2], B_onesf], writes=[B_bank[bd]])
                P.op("dve", I("tensor_copy", out=ssq[:, qb * 4:(qb + 1) * 4, mh], in_=pbank[bd][:, 0:4]),
                     reads=[B_bank[bd]], writes=[B_ssq])

            def emit_pv(n):
                st_ = steps[n]
                ps_ = n % NP
                if st_["first"]:
                    st_["blk"] = state["blk"]
                    st_["j"] = 0
                    state["blk"] += 1
                else:
                    st_["blk"] = steps[n - 1]["blk"]
                    st_["j"] = steps[n - 1]["j"] + 1
                b2 = st_["blk"] % 2
                bo, bd = 3 + b2, 5 + b2
                kvs, kc, j = st_["kvs"], st_["kc"], st_["j"]
                if masked:
                    fns = [I("matmul", pbank[bo][:, :], lhsT=v_sb[kvs][:, kc, :], rhs=pt_sb[ps_][:], start=st_["first"], stop=st_["last"]),
                           I("matmul", pbank[bd][:, :], lhsT=ones_bf[:], rhs=pt_sb[ps_][:], start=st_["first"], stop=st_["last"])]
                    P.group("pe", fns, reads=[B_kv[kvs], B_pt[ps_], B_ones], writes=[B_bank[bo], B_bank[bd]])
                else:
                    on_pe = (j % 2 == 1)
                    fns = [I("matmul", pbank[bo][:, :], lhsT=v_sb[kvs][:, kc, :], rhs=pt_sb[ps_][:], start=st_["first"], stop=st_["last"])]
                    wr = [B_bank[bo]]
                    if on_pe:
                        fns.append(I("matmul", pbank[bd][:, :], lhsT=ones_bf[:], rhs=pt_sb[ps_][:], start=(j == 1), stop=False))
                        wr.append(B_bank[bd])
                    P.group("pe", fns, reads=[B_kv[kvs], B_pt[ps_], B_ones], writes=wr)
                    if not on_pe:
                        if j == 0:
                            P.op("dve", I("tensor_copy", out=accD[b2][:], in_=pt_sb[ps_][:]), reads=[B_pt[ps_]], writes=[B_accD[b2]])
                        else:
                            P.op("dve", I("tensor_tensor", out=accD[b2][:], in0=accD[b2][:], in1=pt_sb[ps_][:], op=ALU.add),
                                 reads=[B_pt[ps_]], writes=[B_accD[b2]])
                    if st_["last"]:
                        P.op("dve", I("tensor_copy", out=hi_sb[b2][:], in_=accD[b2][:]), reads=[B_accD[b2]], writes=[B_hi[b2]])
                        P.op("dve", I("tensor_tensor", out=lo_sb[b2][:], in0=accD[b2][:], in1=hi_sb[b2][:], op=ALU.subtract),
                             reads=[B_accD[b2], B_hi[b2]], writes=[B_lo[b2]])
                if st_["last"]:
                    deferred.append([2, (b2, bo, bd, st_["mh"], st_["qb"])])

            LOOK = 3
            for n in range(min(LOOK, len(steps))):
                emit_scores(n)
            for n in range(len(steps)):
                if n + LOOK < len(steps):
                    emit_scores(n + LOOK)
                emit_pv(n)
                for d_ in deferred + deferred2:
                    d_[0] -= 1
                while deferred2 and deferred2[0][0] <= 0:
                    finish2(*deferred2.pop(0)[1])
                while deferred and deferred[0][0] <= 0:
                    finish(*deferred.pop(0)[1])
            while deferred:
                finish(*deferred.pop(0)[1])
            while deferred2:
                finish2(*deferred2.pop(0)[1])

        emit_casts()
        if 2 in phases:
            with ExitStack() as ps2:
                heads = []
                for h in range(NHA):
                    g = h // 4
                    heads.append((h, qaT[h], kaT[g], va[:, g * 128:(g + 1) * 128], ("a", g)))
                attention(ps2, heads, masked=False)
            P.barrier()
        if 3 in phases:
            with ExitStack() as ps3:
                heads = []
                for h in range(NHB):
                    heads.append((8 + h, qbT[h], kbT[h], vb[:, h * 128:(h + 1) * 128], ("b", h)))
                attention(ps3, heads, masked=True)
            P.barrier()

        emit_casts()
        if 4 in phases:
            with ExitStack() as ps4:
                NRING = 4
                ring = [sbt(ps4, "ring%d" % i, [128, 8192], BF16) for i in range(NRING)]
                B_ring = [Buf("ring%d" % i) for i in range(NRING)]
                mix_sbs = [sbt(ps4, "mix_sb0", [128, 16, 512], BF16)] * 2
                B_mixs = [Buf("mix0")] * 2
                gffn_sb = sbt(ps4, "gffn_sb", [128, 16], F32)
                gfin_sb = sbt(ps4, "gfin_sb", [128, D], F32)
                B_gffn, B_gfin = Buf("gffn"), Buf("gfin")
                P.dma("sp", gffn_sb[:], g_ffn, writes=[B_gffn])
                P.dma("sp", gfin_sb[:], g_fin, writes=[B_gfin])
                xres = [sbt(ps4, "xres%d" % i, [128, D], F32) for i in range(4)]
                B_xres = [Buf("xres%d" % i) for i in range(4)]
                h2 = [sbt(ps4, "h2_%d" % i, [128, D], BF16) for i in range(4)]
                B_h2 = [Buf("h2_%d" % i) for i in range(4)]
                h2T = sbt(ps4, "h2T", [128, 16, 512], BF16)
                B_h2T = [Buf("h2T_%d" % t) for t in range(4)]
                actT = sbt(ps4, "actT", [128, NFC, 512], BF16)
                B_act = [Buf("act%d" % f) for f in range(NFC)]
                st2 = [sbt(ps4, "st2_%d" % i, [128, 16], F32) for i in range(2)]
                B_st2 = [Buf("st2_%d" % i) for i in range(2)]
                st3 = [sbt(ps4, "st3_%d" % i, [128, 16], F32) for i in range(2)]
                B_st3 = [Buf("st3_%d" % i) for i in range(2)]
                sg = [sbt(ps4, "sg%d" % i, [128, 512], F32) for i in range(2)]
                B_sg = [Buf("sg%d" % i) for i in range(2)]

                items = []
                for tb in range(NQB):
                    for cb in range(4):
                        items.append(("wout", cb, 0))
                    for f2 in range(22):
                        items.append(("wgu", f2, 0))
                    for cb in range(4):
                        for pc in range(3):
                            items.append(("wd", cb, pc))
                rstate = {"loaded": 0, "used": 0}

                def ring_prefetch(upto):
                    while rstate["loaded"] < min(upto, len(items)):
                        n = rstate["loaded"]
                        kind, a, b = items[n]
                        s = n % NRING
                        if kind == "wout":
                            src = wout_s[a].rearrange("p h c -> p (h c)")
                        elif kind == "wgu":
                            src = wgu_s[a].rearrange("p g k c -> p (g k c)")
                        else:
                            nfp = 16 if b < 2 else NFC - 32
                            src = wd_s[a, b, :, 0:nfp, :].rearrange("p f c -> p (f c)")
                        P.dma("pool", ring[s][:, 0:src.shape[1]], src, writes=[B_ring[s]], extra=[cast_state["tok"]])
                        rstate["loaded"] += 1

                def ring_next(kind):
                    n = rstate["used"]
                    assert items[n][0] == kind, (items[n], kind)
                    ring_prefetch(n + NRING)
                    rstate["used"] += 1
                    s = n % NRING
                    return ring[s], B_ring[s]

                tp_bank4 = [pbank[4][:].bitcast(BF16), pbank[5][:].bitcast(BF16)]
                c4 = {"op": 0, "gu": 0}
                ring_prefetch(NRING - 1)
                for tb in range(NQB):
                    t0 = tb * 512
                    s2 = tb % 2
                    mix_sb, B_mix = mix_sbs[tb % 2], B_mixs[tb % 2]
                    if tb == 0:
                        P.dma("sp", mix_sb[:], mixT[:, :, t0:t0 + 512].rearrange("h d t -> d h t"), writes=[B_mix])
                    if tb == 0:
                        for t in range(4):
                            r0 = t0 + t * 128
                            P.dma("pool", xres[t][:], x[r0:r0 + 128, :], writes=[B_xres[t]])
                    for t in range(4):
                        tt = tb * 4 + t
                        for ab in range(2):
                            P.op("dve", I("tensor_reduce",
                                out=st2[s2][:, t * 2 + ab:t * 2 + ab + 1], in_=ssq[:, tt, ab * 8:(ab + 1) * 8], axis=AX.X, op=ALU.add),
                                reads=[B_ssq], writes=[B_st2[s2]])
                    P.op("dve", I("tensor_scalar", out=st2[s2][:, 0:8], in0=st2[s2][:, 0:8], scalar1=1.0 / 1024, scalar2=EPS,
                                                          op0=ALU.mult, op1=ALU.add), writes=[B_st2[s2]])
                    P.op("act", I("activation", out=st2[s2][:, 0:8], in_=st2[s2][:, 0:8], func=AF.Sqrt), writes=[B_st2[s2]])
                    P.op("dve", I("reciprocal", out=st2[s2][:, 8:16], in_=st2[s2][:, 0:8]), writes=[B_st2[s2]])
                    for cb in range(4):
                        wt, bw = ring_next("wout")
                        wv = wt[:].rearrange("p (h c) -> p h c", h=16)
                        cs = slice(cb * 512, (cb + 1) * 512)
                        for t in range(4):
                            pa = (c4["op"] % 2) * 2
                            c4["op"] += 1
                            pbk = pa + 1
                            fa = [I("matmul", pbank[pa][:, :], lhsT=mix_sb[:, h, t * 128:(t + 1) * 128],
                                                                           rhs=wv[:, h, :], start=(h == 0), stop=(h == 7)) for h in range(8)]
                            fb = [I("matmul", pbank[pbk][:, :], lhsT=mix_sb[:, h, t * 128:(t + 1) * 128],
                                                                             rhs=wv[:, h, :], start=(h == 8), stop=(h == 15)) for h in range(8, 16)]
                            P.group("pe", fa + fb, reads=[B_mix, bw], writes=[B_bank[pa], B_bank[pbk]])
                            P.op("dve", I("scalar_tensor_tensor",
                                out=xres[t][:, cs], in0=pbank[pa][:, :], scalar=st2[s2][:, 8 + t * 2:9 + t * 2], in1=xres[t][:, cs],
                                op0=ALU.mult, op1=ALU.add), reads=[B_bank[pa], B_st2[s2]], writes=[B_xres[t]])
                            P.op("dve", I("scalar_tensor_tensor",
                                out=xres[t][:, cs], in0=pbank[pbk][:, :], scalar=st2[s2][:, 9 + t * 2:10 + t * 2], in1=xres[t][:, cs],
                                op0=ALU.mult, op1=ALU.add), reads=[B_bank[pbk], B_st2[s2]], writes=[B_xres[t]])
                            if cb == 3:
                                ss = st3[s2]
                                if t == 0:
                                    P.op("dve", I("memset", ss[:], 0.0), writes=[B_st3[s2]])
                                P.op("act", I("activation", out=h2[t][:], in_=xres[t][:], func=AF.Square, accum_out=ss[:, t:t + 1]),
                                     reads=[B_xres[t]], writes=[B_h2[t], B_st3[s2]])
                                P.op("dve", I("tensor_scalar", out=ss[:, 4 + t:5 + t], in0=ss[:, t:t + 1], scalar1=1.0 / D, scalar2=EPS,
                                              op0=ALU.mult, op1=ALU.add), writes=[B_st3[s2]])
                                P.op("act", I("activation", out=ss[:, 4 + t:5 + t], in_=ss[:, 4 + t:5 + t], func=AF.Sqrt), writes=[B_st3[s2]])
                                P.op("dve", I("reciprocal", out=ss[:, 4 + t:5 + t], in_=ss[:, 4 + t:5 + t]), writes=[B_st3[s2]])
                                P.op("dve", I("tensor_scalar", out=h2[t][:], in0=xres[t][:], scalar1=ss[:, 4 + t:5 + t], scalar2=None,
                                              op0=ALU.mult), reads=[B_xres[t], B_st3[s2]], writes=[B_h2[t]])
                    for t in range(4):
                        hs = t
                        for half in range(2):
                            fns = [I("transpose", tp_bank4[half][:, kk * 128:(kk + 1) * 128],
                                     h2[hs][:, (half * 8 + kk) * 128:(half * 8 + kk + 1) * 128], ident[:]) for kk in range(8)]
                            P.group("pe", fns, reads=[B_h2[hs], B_ident], writes=[B_bank[4 + half]])
                            src = tp_bank4[half][:, :].rearrange("p (k t) -> p k t", k=8)
                            gm = gffn_sb[:, half * 8:(half + 1) * 8].unsqueeze(2).to_broadcast([128, 8, 128])
                            P.op("dve", I("tensor_tensor", out=h2T[:, half * 8:(half + 1) * 8, t * 128:(t + 1) * 128], in0=src, in1=gm, op=ALU.mult),
                                 reads=[B_bank[4 + half], B_gffn], writes=[B_h2T[t]])
                    if tb + 1 < NQB:
                        P.dma("sp", mix_sbs[(tb + 1) % 2][:], mixT[:, :, t0 + 512:t0 + 1024].rearrange("h d t -> d h t"),
                              writes=[B_mixs[(tb + 1) % 2]])
                    for f2 in range(22):
                        wt, bw = ring_next("wgu")
                        wv = wt[:].rearrange("p (g k c) -> p g k c", g=2, k=16)
                        for j in range(2):
                            fc = f2 * 2 + j
                            pg = (c4["gu"] % 2) * 2
                            c4["gu"] += 1
                            pu = pg + 1
                            fg = [I("matmul", pbank[pg][:, :], lhsT=wv[:, 0, kc, j * 128:(j + 1) * 128],
                                                                             rhs=h2T[:, kc, :], start=(kc == 0), stop=(kc == 15)) for kc in range(16)]
                            fu = [I("matmul", pbank[pu][:, :], lhsT=wv[:, 1, kc, j * 128:(j + 1) * 128],
                                                                             rhs=h2T[:, kc, :], start=(kc == 0), stop=(kc == 15)) for kc in range(16)]
                            P.group("pe", fg, reads=B_h2T + [bw], writes=[B_bank[pg]])
                            P.group("pe", fu, reads=B_h2T + [bw], writes=[B_bank[pu]])
                            gs = fc % 2
                            P.op("act", I("activation", out=sg[gs][:], in_=pbank[pg][:, :], func=AF.Silu),
                                 reads=[B_bank[pg]], writes=[B_sg[gs]])
                            P.op("dve", I("tensor_tensor", out=actT[:, fc, :], in0=pbank[pu][:, :], in1=sg[gs][:],
                                                                                     op=ALU.mult),
                                 reads=[B_bank[pu], B_sg[gs]], writes=[B_act[fc]])
                    for cb in range(4):
                        cs = slice(cb * 512, (cb + 1) * 512)
                        for pc in range(3):
                            wt, bw = ring_next("wd")
                            wv = wt[:].rearrange("p (f c) -> p f c", f=16)
                            nf = 16 if pc < 2 else NFC - 32
                            for t in range(4):
                                fns = [I("matmul",
                                    pbank[4 + t][:, :], lhsT=actT[:, pc * 16 + f, t * 128:(t + 1) * 128], rhs=wv[:, f, :],
                                    start=(pc == 0 and f == 0), stop=(pc == 2 and f == nf - 1)) for f in range(nf)]
                                P.group("pe", fns, reads=B_act[pc * 16:pc * 16 + nf] + [bw], writes=[B_bank[4 + t]])
                        for t in range(4):
                            P.op("dve", I("tensor_tensor", out=xres[t][:, cs], in0=pbank[4 + t][:, :], in1=xres[t][:, cs],
                                                                            op=ALU.add),
                                 reads=[B_bank[4 + t]], writes=[B_xres[t]])
                    ss = st3[s2]
                    for t in range(4):
                        r0 = t0 + t * 128
                        hs = t % 2
                        P.op("act", I("activation", out=h2[hs][:], in_=xres[t][:], func=AF.Square,
                                                                           accum_out=ss[:, 8 + t:9 + t]),
                             reads=[B_xres[t]], writes=[B_h2[hs], B_st3[s2]])
                        P.op("dve", I("tensor_scalar", out=ss[:, 12 + t:13 + t], in0=ss[:, 8 + t:9 + t], scalar1=1.0 / D,
                                                                        scalar2=EPS, op0=ALU.mult, op1=ALU.add), writes=[B_st3[s2]])
                        P.op("act", I("activation", out=ss[:, 12 + t:13 + t], in_=ss[:, 12 + t:13 + t], func=AF.Sqrt),
                             writes=[B_st3[s2]])
                        P.op("dve", I("reciprocal", out=ss[:, 12 + t:13 + t], in_=ss[:, 12 + t:13 + t]), writes=[B_st3[s2]])
                        P.op("dve", I("scalar_tensor_tensor", out=xres[t][:], in0=xres[t][:], scalar=ss[:, 12 + t:13 + t],
                                                                               in1=gfin_sb[:], op0=ALU.mult, op1=ALU.mult),
                             reads=[B_gfin, B_st3[s2]], writes=[B_xres[t]])
                        P.dma("sp", out[r0:r0 + 128, :], xres[t][:], reads=[B_xres[t]])
                        if tb + 1 < NQB:
                            P.dma("pool", xres[t][:], x[r0 + 512:r0 + 640, :], writes=[B_xres[t]])
        P.fence_stores(engs=("sp",))
        P.run()
    return nc


def _host_inputs(inputs, S):
    f32 = np.float32
    ca, sa, cb, sb_, mask = _const_tables(S)

    def swap(g):
        return np.ascontiguousarray(g.reshape(2, 2, 32)[:, ::-1, :]).reshape(128)

    gq = np.asarray(inputs["g_q_a"], f32)[0]
    gk = np.asarray(inputs["g_k_a"], f32)[0]
    g_qk = np.ascontiguousarray(np.broadcast_to(np.stack([gq, swap(gq), gk, swap(gk)])[None], (128, 4, 128)))
    g_out = np.concatenate([np.asarray(inputs["g_out_a"], f32)[0], np.asarray(inputs["g_out_b"], f32)[0]])
    shared = {
        "w_in": np.ascontiguousarray(np.asarray(inputs["w_in"], f32)[0]),
        "w_out": np.ascontiguousarray(np.asarray(inputs["w_out"], f32)[0]),
        "w_gate_up": np.ascontiguousarray(np.asarray(inputs["w_gate_up"], f32)[0]),
        "w_down": np.ascontiguousarray(np.asarray(inputs["w_down"], f32)[0]),
        "g_mix": np.ascontiguousarray(np.asarray(inputs["g_mix"], f32)[0].reshape(16, 128).T),
        "g_ffn": np.ascontiguousarray(np.asarray(inputs["g_ffn"], f32)[0].reshape(16, 128).T),
        "g_final": np.ascontiguousarray(np.broadcast_to(np.asarray(inputs["g_final"], f32)[None, :], (128, D))),
        "g_qk": g_qk,
        "g_out": np.ascontiguousarray(g_out.reshape(16, 128).T),
        "ropa_c": ca, "ropa_s": sa, "ropb_c": cb, "ropb_s": sb_, "maskb": mask,
    }
    return shared


_NC_CACHE = {}


def kernel(**inputs):
    x = np.asarray(inputs["x"], np.float32)
    B, S, _ = x.shape
    shared = _host_inputs(inputs, S)
    if S not in _NC_CACHE:
        _NC_CACHE[S] = build(S)
    nc = _NC_CACHE[S]
    in_maps = []
    for b in range(B):
        m = dict(shared)
        m["x"] = np.ascontiguousarray(x[b])
        in_maps.append(m)
    res = run_bass_kernel_spmd(nc, in_maps, core_ids=list(range(B)))
    return np.stack([np.asarray(r["out"], np.float32) for r in res.results], axis=0)
```
